# Optimizing a Trainium2 kernel written in Bass

```python
import jax
import jax.numpy as jnp
from jax import lax
import numpy as np

D_MODEL = 2048
BATCH = 4
SEQ = 2048
DEPTH = 2
DEC_BATCH = 8
DEC_SEQ = 4
PAST_LEN = 16384
PAGE_SIZE = 128

N_GLA_LAYERS = (DEPTH + 1) // 2
N_NSA_LAYERS = DEPTH // 2
DN_ALPHA = (2 * DEPTH) ** 0.25
DN_BETA = (8 * DEPTH) ** -0.25
LN_EPS = 1e-5
D_FF = -(-8 * D_MODEL // (3 * 256)) * 256

GLA_HEADS = 4
GLA_DK = D_MODEL // 2 // GLA_HEADS
GLA_DV = D_MODEL // GLA_HEADS
GLA_GATE_RANK = 16
GLA_TAU = 16.0
GLA_CHUNK = 32
GLA_IN_W = 2 * GLA_HEADS * GLA_DK + 2 * GLA_HEADS * GLA_DV + GLA_GATE_RANK

NSA_HEADS = 16
NSA_KV_HEADS = 4
NSA_HEAD_DIM = D_MODEL // NSA_HEADS
NSA_GROUP = NSA_HEADS // NSA_KV_HEADS
NSA_BLOCK = 64
NSA_TOPK = 16
NSA_WINDOW = 512
NSA_QBLOCK = 32
NSA_IN_W = NSA_HEADS * NSA_HEAD_DIM + 6 * NSA_KV_HEADS * NSA_HEAD_DIM + 3 * NSA_HEADS

kernel_name = "gla_nsa_hybrid_deepnorm_adaln_step"


def layer_norm(x, g, b):
    xf = x.astype(jnp.float32)
    mu = jnp.mean(xf, axis=-1, keepdims=True)
    var = jnp.mean(jnp.square(xf - mu), axis=-1, keepdims=True)
    return ((xf - mu) * lax.rsqrt(var + LN_EPS) * g + b).astype(x.dtype)


def adaln(c, w, b):
    m = jax.nn.silu(c) @ w + b
    return jnp.split(m[:, None, :], 6, axis=-1)


def swiglu(h, w_in, w_out):
    a, u = jnp.split(h @ w_in, 2, axis=-1)
    return (jax.nn.silu(a) * u) @ w_out


def masked_softmax(s, mask):
    s = jnp.where(mask, s.astype(jnp.float32), -jnp.inf)
    m = jnp.max(s, axis=-1, keepdims=True)
    m = jnp.where(jnp.isfinite(m), m, 0.0)
    e = jnp.where(mask, jnp.exp(s - m), 0.0)
    return e / jnp.maximum(jnp.sum(e, axis=-1, keepdims=True), jnp.finfo(jnp.float32).tiny)


def gla_recurrence(q, k, v, log_a, s0):
    B, T = q.shape[:2]
    n_chunks = -(-T // GLA_CHUNK)
    pad = n_chunks * GLA_CHUNK - T

    def to_chunks(a):
        a = jnp.pad(a.astype(jnp.float32), ((0, 0), (0, pad), (0, 0), (0, 0)))
        return a.reshape(B, n_chunks, GLA_CHUNK, *a.shape[2:]).transpose(1, 0, 3, 2, 4)

    qc, kc, vc, lc = to_chunks(q), to_chunks(k), to_chunks(v), to_chunks(log_a)
    causal = jnp.tril(jnp.ones((GLA_CHUNK, GLA_CHUNK), dtype=bool))
    mid = GLA_CHUNK // 2

    def step(S, xs):
        qb, kb, vb, lb = xs
        b = jnp.cumsum(lb, axis=2)
        b_last = b[:, :, -1:, :]
        b_mid = b[:, :, mid - 1:mid, :]
        o_inter = jnp.einsum('bhcd,bhde->bhce', qb * jnp.exp(b), S)
        a = jnp.einsum('bhid,bhjd->bhij', qb * jnp.exp(b - b_mid), kb * jnp.exp(b_mid - b))
        o_intra = jnp.einsum('bhij,bhje->bhie', jnp.where(causal, a, 0.0), vb)
        S_new = (jnp.exp(b_last[:, :, 0, :])[..., None] * S
                 + jnp.einsum('bhcd,bhce->bhde', kb * jnp.exp(b_last - b), vb))
        return S_new, o_inter + o_intra

    s_final, o = lax.scan(step, s0.astype(jnp.float32), (qc, kc, vc, lc))
    o = o.transpose(1, 0, 3, 2, 4).reshape(B, n_chunks * GLA_CHUNK, GLA_HEADS, GLA_DV)[:, :T]
    return o, s_final


def gla_mixer(h, s0, w_in, w_alpha, b_alpha, norm_g, w_out):
    B, T, _ = h.shape
    qd, vd = GLA_HEADS * GLA_DK, GLA_HEADS * GLA_DV
    q, k, v, r, a_lr = jnp.split(h @ w_in, [qd, 2 * qd, 2 * qd + vd, 2 * qd + 2 * vd], axis=-1)
    log_a = jax.nn.log_sigmoid((a_lr @ w_alpha + b_alpha).astype(jnp.float32)) / GLA_TAU
    q = q.reshape(B, T, GLA_HEADS, GLA_DK) * (GLA_DK ** -0.5)
    k = k.reshape(B, T, GLA_HEADS, GLA_DK)
    v = v.reshape(B, T, GLA_HEADS, GLA_DV)
    o, s_new = gla_recurrence(q, k, v, log_a.reshape(B, T, GLA_HEADS, GLA_DK), s0)
    o = o * lax.rsqrt(jnp.mean(jnp.square(o), axis=-1, keepdims=True) + LN_EPS)
    o = (o.reshape(B, T, vd) * norm_g).astype(h.dtype)
    return (o * jax.nn.silu(r)) @ w_out, s_new.astype(s0.dtype)


def nsa_project(h, w_in, b_gate):
    B, T, _ = h.shape
    qd = NSA_HEADS * NSA_HEAD_DIM
    kvd = 2 * NSA_KV_HEADS * NSA_HEAD_DIM
    q, kv_c, kv_s, kv_w, g = jnp.split(h @ w_in, [qd, qd + kvd, qd + 2 * kvd, qd + 3 * kvd], axis=-1)
    kvshape = (B, T, 2, NSA_KV_HEADS, NSA_HEAD_DIM)
    return (q.reshape(B, T, NSA_HEADS, NSA_HEAD_DIM), kv_c.reshape(kvshape), kv_s.reshape(kvshape),
            kv_w.reshape(kvshape), (g + b_gate).reshape(B, T, NSA_HEADS, 3))


def compress_blocks(kv, n_blocks):
    B = kv.shape[0]
    c = kv[:, :n_blocks * NSA_BLOCK].astype(jnp.float32).reshape(
        B, n_blocks, NSA_BLOCK, 2, NSA_KV_HEADS, NSA_HEAD_DIM).mean(axis=2)
    return c[:, :, 0], c[:, :, 1]


def selection_blocks(kv, n_blocks):
    B, L = kv.shape[:2]
    kv = jnp.pad(kv, ((0, 0), (0, n_blocks * NSA_BLOCK - L), (0, 0), (0, 0), (0, 0)))
    kv = kv.reshape(B, n_blocks, NSA_BLOCK, 2, NSA_KV_HEADS, NSA_HEAD_DIM)
    return kv[:, :, :, 0].transpose(0, 3, 1, 2, 4), kv[:, :, :, 1].transpose(0, 3, 1, 2, 4)


def nsa_attend(q, gates, p0, kc, vc, ksel, vsel, kw, vw, kw_pos):
    B, T = q.shape[:2]
    qg = q.reshape(B, T, NSA_KV_HEADS, NSA_GROUP, NSA_HEAD_DIM).astype(jnp.float32) * (NSA_HEAD_DIM ** -0.5)
    qpos = p0 + jnp.arange(T, dtype=jnp.int32)
    nc = kc.shape[1]
    s_c = jnp.einsum('btkgd,bjkd->bkgtj', qg, kc.astype(jnp.float32))
    mask_c = ((jnp.arange(nc) + 1) * NSA_BLOCK - 1)[None, :] <= qpos[:, None]
    p_c = masked_softmax(s_c, mask_c)
    o_c = jnp.einsum('bkgtj,bjkd->btkgd', p_c, vc.astype(jnp.float32))
    ns = ksel.shape[2]
    imp = jnp.pad(jnp.sum(p_c, axis=2), ((0, 0), (0, 0), (0, 0), (0, ns - nc)))
    blk = jnp.arange(ns)[None, :]
    cur = (qpos // NSA_BLOCK)[:, None]
    forced = (blk == 0) | ((blk >= cur - 1) & (blk <= cur))
    score = jnp.where(blk > cur, -jnp.inf, jnp.where(forced, jnp.inf, imp))
    top_s, idx = lax.top_k(score, min(NSA_TOPK, ns))
    valid = top_s > -jnp.inf
    bi = jnp.arange(B)[:, None, None, None]
    ki = jnp.arange(NSA_KV_HEADS)[None, :, None, None]
    n_sel = idx.shape[-1] * NSA_BLOCK
    ks = ksel[bi, ki, idx].reshape(B, NSA_KV_HEADS, T, n_sel, NSA_HEAD_DIM).astype(jnp.float32)
    vs = vsel[bi, ki, idx].reshape(B, NSA_KV_HEADS, T, n_sel, NSA_HEAD_DIM).astype(jnp.float32)
    kpos = idx[..., None] * NSA_BLOCK + jnp.arange(NSA_BLOCK)
    mask_s = (valid[..., None] & (kpos <= qpos[None, None, :, None, None])).reshape(B, NSA_KV_HEADS, T, n_sel)
    s_s = jnp.einsum('btkgd,bktnd->bkgtn', qg, ks)
    p_s = masked_softmax(s_s, mask_s[:, :, None])
    o_s = jnp.einsum('bkgtn,bktnd->btkgd', p_s, vs)
    s_w = jnp.einsum('btkgd,bwkd->bkgtw', qg, kw.astype(jnp.float32))
    dpos = qpos[:, None] - kw_pos[None, :]
    mask_w = (dpos >= 0) & (dpos < NSA_WINDOW) & (kw_pos[None, :] >= 0)
    p_w = masked_softmax(s_w, mask_w)
    o_w = jnp.einsum('bkgtw,bwkd->btkgd', p_w, vw.astype(jnp.float32))
    g = jax.nn.sigmoid(gates.astype(jnp.float32)).reshape(B, T, NSA_KV_HEADS, NSA_GROUP, 3)
    o = g[..., 0:1] * o_c + g[..., 1:2] * o_s + g[..., 2:3] * o_w
    return o.reshape(B, T, NSA_HEADS * NSA_HEAD_DIM).astype(q.dtype)


def nsa_prompt(h, w_in, b_gate, w_out):
    B, T, _ = h.shape
    q, kv_c, kv_s, kv_w, gates = nsa_project(h, w_in, b_gate)
    kc, vc = compress_blocks(kv_c, T // NSA_BLOCK)
    ksel, vsel = selection_blocks(kv_s, -(-T // NSA_BLOCK))
    kw_pad = jnp.pad(kv_w, ((0, 0), (NSA_WINDOW, 0), (0, 0), (0, 0), (0, 0)))
    n_qb = T // NSA_QBLOCK
    qb = q.reshape(B, n_qb, NSA_QBLOCK, NSA_HEADS, NSA_HEAD_DIM).transpose(1, 0, 2, 3, 4)
    gb = gates.reshape(B, n_qb, NSA_QBLOCK, NSA_HEADS, 3).transpose(1, 0, 2, 3, 4)
    starts = jnp.arange(n_qb, dtype=jnp.int32) * NSA_QBLOCK

    def block(xs):
        q_blk, g_blk, p0 = xs
        kvw = lax.dynamic_slice_in_dim(kw_pad, p0, NSA_WINDOW + NSA_QBLOCK, axis=1)
        kw_pos = p0 - NSA_WINDOW + jnp.arange(NSA_WINDOW + NSA_QBLOCK, dtype=jnp.int32)
        return nsa_attend(q_blk, g_blk, p0, kc, vc, ksel, vsel, kvw[:, :, 0], kvw[:, :, 1], kw_pos)

    o = lax.map(block, (qb, gb, starts)).transpose(1, 0, 2, 3).reshape(B, T, NSA_HEADS * NSA_HEAD_DIM)
    win_state = kv_w[:, T - min(NSA_WINDOW, T):]
    return o @ w_out, kv_c, kv_s, win_state


def nsa_sample(h, past_cmp_pages, past_slc_pages, win_buf, w_in, b_gate, w_out):
    B, T, _ = h.shape
    past = past_cmp_pages.shape[1] * past_cmp_pages.shape[2]
    q, kv_c, kv_s, kv_w, gates = nsa_project(h, w_in, b_gate)
    kvshape = (B, past, 2, NSA_KV_HEADS, NSA_HEAD_DIM)
    full_c = jnp.concatenate([past_cmp_pages.reshape(kvshape), kv_c], axis=1)
    full_s = jnp.concatenate([past_slc_pages.reshape(kvshape), kv_s], axis=1)
    L = past + T
    kc, vc = compress_blocks(full_c, L // NSA_BLOCK)
    ksel, vsel = selection_blocks(full_s, -(-L // NSA_BLOCK))
    wb = win_buf.shape[1]
    all_w = jnp.concatenate([win_buf, kv_w], axis=1)
    kw_pos = past - wb + jnp.arange(wb + T, dtype=jnp.int32)
    o = nsa_attend(q, gates, past, kc, vc, ksel, vsel, all_w[:, :, 0], all_w[:, :, 1], kw_pos)
    keep = min(NSA_WINDOW, wb + T)
    return o @ w_out, kv_c, kv_s, all_w[:, wb + T - keep:]


def setup_inputs(seed: int = 0) -> dict:
    key = jax.random.key(seed)
    ks = jax.random.split(key, 32)

    def nrm(k, shape, scale):
        return jax.random.normal(k, shape, jnp.float32) * scale

    n_pages = PAST_LEN // PAGE_SIZE
    n_used = DEC_BATCH * n_pages
    n_phys = n_used + max(1, n_used // 4)
    wbuf = min(NSA_WINDOW, PAST_LEN)
    kvrow = (2, NSA_KV_HEADS, NSA_HEAD_DIM)
    page_table = jax.random.permutation(ks[8], n_phys)[:n_used].reshape(DEC_BATCH, n_pages).astype(jnp.int32)
    return {
        "x_prompt": nrm(ks[0], (BATCH, SEQ, D_MODEL), 1.0),
        "x_sample": nrm(ks[1], (DEC_BATCH, DEC_SEQ, D_MODEL), 1.0),
        "c_prompt": nrm(ks[2], (BATCH, D_MODEL), 1.0),
        "c_sample": nrm(ks[3], (DEC_BATCH, D_MODEL), 1.0),
        "state_gla": nrm(ks[4], (DEC_BATCH, N_GLA_LAYERS, GLA_HEADS, GLA_DK, GLA_DV), 1.0),
        "cache_cmp_kv": nrm(ks[5], (n_phys, N_NSA_LAYERS, PAGE_SIZE) + kvrow, 1.0),
        "cache_slc_kv": nrm(ks[6], (n_phys, N_NSA_LAYERS, PAGE_SIZE) + kvrow, 1.0),
        "cache_win_kv": nrm(ks[7], (DEC_BATCH, N_NSA_LAYERS, wbuf) + kvrow, 1.0),
        "page_table": page_table,
        "ada_w": nrm(ks[9], (DEPTH, D_MODEL, 6 * D_MODEL), 0.5 * D_MODEL ** -0.5),
        "ada_b": nrm(ks[10], (DEPTH, 6 * D_MODEL), 0.01),
        "gla_w_in": nrm(ks[11], (N_GLA_LAYERS, D_MODEL, GLA_IN_W), D_MODEL ** -0.5),
        "gla_w_alpha": nrm(ks[12], (N_GLA_LAYERS, GLA_GATE_RANK, GLA_HEADS * GLA_DK), GLA_GATE_RANK ** -0.5),
        "gla_b_alpha": nrm(ks[13], (N_GLA_LAYERS, GLA_HEADS * GLA_DK), 0.1),
        "gla_norm_g": 1.0 + nrm(ks[14], (N_GLA_LAYERS, GLA_HEADS * GLA_DV), 0.01),
        "gla_w_out": nrm(ks[15], (N_GLA_LAYERS, GLA_HEADS * GLA_DV, D_MODEL), DN_BETA * (GLA_HEADS * GLA_DV) ** -0.5),
        "nsa_w_in": nrm(ks[16], (N_NSA_LAYERS, D_MODEL, NSA_IN_W), D_MODEL ** -0.5),
        "nsa_b_gate": nrm(ks[17], (N_NSA_LAYERS, 3 * NSA_HEADS), 0.1),
        "nsa_w_out": nrm(ks[18], (N_NSA_LAYERS, NSA_HEADS * NSA_HEAD_DIM, D_MODEL), DN_BETA * (NSA_HEADS * NSA_HEAD_DIM) ** -0.5),
        "ln_mix_g": 1.0 + nrm(ks[19], (DEPTH, D_MODEL), 0.01),
        "ln_mix_b": nrm(ks[20], (DEPTH, D_MODEL), 0.01),
        "ffn_w_in": nrm(ks[21], (DEPTH, D_MODEL, 2 * D_FF), D_MODEL ** -0.5),
        "ffn_w_out": nrm(ks[22], (DEPTH, D_FF, D_MODEL), DN_BETA * D_FF ** -0.5),
        "ln_ffn_g": 1.0 + nrm(ks[23], (DEPTH, D_MODEL), 0.01),
        "ln_ffn_b": nrm(ks[24], (DEPTH, D_MODEL), 0.01),
    }


def reference(x_prompt, x_sample, c_prompt, c_sample, state_gla, cache_cmp_kv, cache_slc_kv,
              cache_win_kv, page_table, ada_w, ada_b, gla_w_in, gla_w_alpha, gla_b_alpha, gla_norm_g,
              gla_w_out, nsa_w_in, nsa_b_gate, nsa_w_out, ln_mix_g, ln_mix_b, ffn_w_in, ffn_w_out,
              ln_ffn_g, ln_ffn_b):
    xp, xd = x_prompt, x_sample
    gla_p, gla_s, cmp_p, cmp_s, slc_p, slc_s, win_p, win_s = [], [], [], [], [], [], [], []
    for i in range(DEPTH):
        j = i // 2
        shp1, scp1, gtp1, shp2, scp2, gtp2 = adaln(c_prompt, ada_w[i], ada_b[i])
        shd1, scd1, gtd1, shd2, scd2, gtd2 = adaln(c_sample, ada_w[i], ada_b[i])
        hp = xp * (1.0 + scp1) + shp1
        hd = xd * (1.0 + scd1) + shd1
        if i % 2 == 0:
            s0 = jnp.zeros((xp.shape[0], GLA_HEADS, GLA_DK, GLA_DV), xp.dtype)
            mp, sp = gla_mixer(hp, s0, gla_w_in[j], gla_w_alpha[j], gla_b_alpha[j], gla_norm_g[j], gla_w_out[j])
            md, sd = gla_mixer(hd, state_gla[:, j], gla_w_in[j], gla_w_alpha[j], gla_b_alpha[j], gla_norm_g[j], gla_w_out[j])
            gla_p.append(sp)
            gla_s.append(sd)
        else:
            mp, cp, lp, wp = nsa_prompt(hp, nsa_w_in[j], nsa_b_gate[j], nsa_w_out[j])
            md, cd, ld, wd = nsa_sample(hd, cache_cmp_kv[page_table, j], cache_slc_kv[page_table, j],
                                        cache_win_kv[:, j], nsa_w_in[j], nsa_b_gate[j], nsa_w_out[j])
            cmp_p.append(cp)
            cmp_s.append(cd)
            slc_p.append(lp)
            slc_s.append(ld)
            win_p.append(wp)
            win_s.append(wd)
        xp = layer_norm(DN_ALPHA * xp + gtp1 * mp, ln_mix_g[i], ln_mix_b[i])
        xd = layer_norm(DN_ALPHA * xd + gtd1 * md, ln_mix_g[i], ln_mix_b[i])
        fp = swiglu(xp * (1.0 + scp2) + shp2, ffn_w_in[i], ffn_w_out[i])
        fd = swiglu(xd * (1.0 + scd2) + shd2, ffn_w_in[i], ffn_w_out[i])
        xp = layer_norm(DN_ALPHA * xp + gtp2 * fp, ln_ffn_g[i], ln_ffn_b[i])
        xd = layer_norm(DN_ALPHA * xd + gtd2 * fd, ln_ffn_g[i], ln_ffn_b[i])
    gla_state_p = jnp.stack(gla_p, axis=1)
    gla_state_s = jnp.stack(gla_s, axis=1)
    cmp_kv_p = jnp.stack(cmp_p, axis=1)
    cmp_kv_s = jnp.stack(cmp_s, axis=1)
    slc_kv_p = jnp.stack(slc_p, axis=1)
    slc_kv_s = jnp.stack(slc_s, axis=1)
    win_kv_p = jnp.stack(win_p, axis=1)
    win_kv_s = jnp.stack(win_s, axis=1)
    return (xp, xd, gla_state_p, gla_state_s, cmp_kv_p, cmp_kv_s, slc_kv_p, slc_kv_s, win_kv_p, win_kv_s)
```

```python
import contextlib
import numpy as np
import concourse.bass as bass
import concourse.mybir as mybir
from concourse.bass_utils import run_bass_kernel_spmd

F32 = mybir.dt.float32; BF16 = mybir.dt.bfloat16; I32 = mybir.dt.int32
AF = mybir.ActivationFunctionType
ALU = mybir.AluOpType
AX = mybir.AxisListType

D = 2048; NCH = 16; T = 2048; DFF = 5632
GLA_IN_W = 6160; NSA_IN_W = 5168
ALPHA = 4.0 ** 0.25
EPS = 1e-5
EPS_LN = EPS / (ALPHA * ALPHA)
NEG = -30000.0
WSLOT_ELEMS = 4096
NSLOT = 3
import os
SYNC_RAW_DIST = int(os.environ.get('SYNC_RAW_DIST', '1'))
SYNC_WAW = int(os.environ.get('SYNC_WAW', '1'))
SYNC_PE = int(os.environ.get('SYNC_PE', '0'))


class Eng:
    def __init__(s, fw, name, obj):
        s.name = name; s.obj = obj; s.n = 0; s.known = {}
        s.sem = fw.es.enter_context(fw.nc.semaphore("pg_" + name))


class DSem:
    def __init__(s, fw, name, serial=True):
        s.sem = fw.es.enter_context(fw.nc.semaphore("ds_" + name)); s.n = 0; s.serial = serial


class Res:
    __slots__ = ("name", "w", "rd")

    def __init__(s, name=""):
        s.name = name; s.w = None; s.rd = {}


class FW:
    def __init__(s, nc, es):
        s.nc = nc; s.es = es
        s.PE = Eng(s, "pe", nc.tensor); s.ACT = Eng(s, "act", nc.scalar)
        s.DVE = Eng(s, "dve", nc.vector); s.POOL = Eng(s, "pool", nc.gpsimd)
        s.SP = Eng(s, "sp", nc.sync)
        s.nops = 0; s.nwaits = 0

    def sb(s, name, shape, dt):
        return s.es.enter_context(s.nc.sbuf_tensor(name, list(shape), dt))

    def ps(s, name, shape, dt=F32):
        return s.es.enter_context(s.nc.psum_tensor(name, list(shape), dt))

    def _wait(s, E, tok, force=False):
        kind, k, v = tok
        if kind == 'e' and k is E and not force:
            return
        kk = id(k)
        if E.known.get(kk, 0) >= v:
            return
        E.obj.wait_ge(k.sem, v); E.known[kk] = v; s.nwaits += 1

    def _deps(s, E, R, W):
        for r in R:
            if r.w is not None:
                near = (r.w[0] == 'e' and r.w[1] is E and E.name != "pe" and (E.n - r.w[2]) <= SYNC_RAW_DIST)
                s._wait(E, r.w, force=near)
        for w in W:
            if w.w is not None:
                near = (SYNC_WAW and w.w[0] == 'e' and w.w[1] is E and (E.name != "pe" or SYNC_PE) and (E.n - w.w[2]) <= 2)
                s._wait(E, w.w, force=near)
            for t in w.rd.values():
                near = (SYNC_WAW and t[0] == 'e' and t[1] is E and (E.name != "pe" or SYNC_PE) and (E.n - t[2]) <= 2)
                s._wait(E, t, force=near)

    def op(s, E, fn, R=(), W=(), RS=()):
        s._deps(E, R, W)
        for r in RS:
            if r.w is not None:
                s._wait(E, r.w, force=True)
        ins = fn(E.obj)
        E.n += 1; ins.then_inc(E.sem, 1); s.nops += 1
        tok = ('e', E, E.n)
        for r in R:
            r.rd[id(E)] = tok
        for r in RS:
            r.rd[id(E)] = tok
        for w in W:
            w.w = tok; w.rd = {}
        return ins

    def dma(s, Q, dsem, out, in_, R=(), W=()):
        s._deps(Q, R, W)
        if dsem.serial and dsem.n > 0:
            s._wait(Q, ('d', dsem, dsem.n))
        ins = Q.obj.dma_start(out=out, in_=in_)
        dsem.n += 16; ins.then_inc(dsem.sem, 16)
        tok = ('d', dsem, dsem.n)
        for r in R:
            r.rd[id(dsem)] = tok
        for w in W:
            w.w = tok; w.rd = {}
        return ins

    def finish(s, E, resources):
        for r in resources:
            if r.w is not None:
                s._wait(E, r.w)
            for t in r.rd.values():
                s._wait(E, t)


def _const_tables():
    j = np.arange(128)[:, None]; i = np.arange(128)[None, :]
    c = {}
    c["tri"] = np.where(j <= i, -1.0 / 16.0, 0.0).astype(np.float32)
    c["caus01"] = (j <= i).astype(np.float32)
    c["causb"] = np.tile(np.where(j <= i, 0.0, NEG), (1, 4)).astype(np.float32)
    c["winb"] = np.tile(np.where(j > i, 0.0, NEG), (1, 4)).astype(np.float32)
    key = np.arange(2048)[None, :]; blk = np.arange(32)[:, None]
    c["eall"] = (key // 64 == blk).astype(np.float32)
    kk = np.arange(128)[:, None]; cc = np.arange(62)[None, :]
    c["avgA"] = np.where(cc == 30 + (kk >= 64), 1.0 / 64.0, 0.0).astype(np.float32)
    ii = np.arange(128)[:, None]; m = np.arange(62)[None, :]
    fl = (ii + 1) // 64
    ok = (m - 30) <= fl - 1
    c["Tc"] = np.where(ok, 0.0, NEG).astype(np.float32)
    c["Tm"] = ok.astype(np.float32)
    d = (m - 30) - (ii >= 64)
    c["Tf"] = np.where(d > 0, -1e4, np.where(d >= -1, 1e4, 0.0)).astype(np.float32)
    k1 = np.arange(128)[None, :]
    c["h2"] = np.stack([(k1[0] < 64), (k1[0] >= 64)]).astype(np.float32)
    jq = np.arange(4)[:, None]; qq = np.tile(np.arange(4), 4)[None, :]
    c["cn"] = (jq <= qq).astype(np.float32)
    rr = np.arange(128)[:, None]
    c["wf"] = (rr > qq).astype(np.float32)
    cb = np.arange(254)[None, :]
    c["avgB"] = np.where(cb == 126 + (kk >= 64), 1.0 / 64.0, 0.0).astype(np.float32)
    return c


def build_program(n_ptiles=4, do_sample=True, stage="full", dbg=False):
    nc = bass.Bass("TRN2", target_bir_lowering=False)
    dram = {}

    def din(name, shape, dt=F32):
        dram[name] = nc.dram_tensor(name, list(shape), dt, kind="ExternalInput").ap()
        return dram[name]

    def dout(name, shape, dt=F32):
        dram[name] = nc.dram_tensor(name, list(shape), dt, kind="ExternalOutput").ap()
        return dram[name]

    xp = din("xp", [T, D]); xs = din("xs", [4, D])
    cvec = din("cvec", [32, 128])
    vtab_ln = din("vtab_ln", [128, 128])
    vtab_ada = din("vtab_ada", [2, 96, 128])
    vtab_g = din("vtab_g", [16, 128])
    ada_w = din("ada_w", [2, D, 6 * D])
    gla_w_in = din("gla_w_in", [D, GLA_IN_W]); gla_w_out = din("gla_w_out", [D, D])
    gla_wa = din("gla_wa", [33, 1024])
    ffn_w_in = din("ffn_w_in", [2, D, 2 * DFF]); ffn_w_out = din("ffn_w_out", [2, DFF, D])
    state_in = din("state_in", [4, 256, 512])
    nsa_w_in = din("nsa_w_in", [D, NSA_IN_W]); nsa_w_out = din("nsa_w_out", [D, D])
    nsa_bg = din("nsa_bg", [1, 48])
    cache_cmp = din("cache_cmp", [1280 * 128, 1024]); cache_slc = din("cache_slc", [1280 * 128, 1024])
    winbuf = din("winbuf", [512, 1024]); ptab = din("ptab", [1, 128], I32)
    selscr = nc.dram_tensor("selscr", [4, 4, 256], F32).ap()
    iota_in = din("iota_in", [128, 1], I32)
    ctab = {k: din("c_" + k, list(v.shape)) for k, v in _const_tables().items()}

    y_p = dout("y_p", [T, D]); y_s = dout("y_s", [4, D])
    st_p = dout("st_p", [4, 256, 512]); st_s = dout("st_s", [4, 256, 512])
    cmp_p = dout("cmp_p", [T, 1024]); slc_p = dout("slc_p", [T, 1024]); win_p = dout("win_p", [512, 1024])
    cmp_s = dout("cmp_s", [4, 1024]); slc_s = dout("slc_s", [4, 1024]); win_s = dout("win_s", [512, 1024])

    es = contextlib.ExitStack()
    with es:
        fw = FW(nc, es)
        PE, ACT, DVE, POOL, SP = fw.PE, fw.ACT, fw.DVE, fw.POOL, fw.SP
        sb = fw.sb
        phase = {"es": None}
        dbgd = {"sem": None, "done": set()}

        def dump(name, ap, R=()):
            if not dbg or name in dbgd["done"]:
                return
            dbgd["done"].add(name)
            if dbgd["sem"] is None:
                dbgd["sem"] = DSem(fw, "dbg")
            shp = list(ap.shape)
            o = nc.dram_tensor("dbg_" + name, shp, ap.dtype, kind="ExternalOutput").ap()
            fw.dma(SP, dbgd["sem"], o, ap, R=list(R))
            for E in (PE, ACT, DVE):
                fw._wait(E, ('d', dbgd["sem"], dbgd["sem"].n))

        def psb(name, shape, dt):
            phase["n"] = phase.get("n", 0) + 1
            return phase["es"].enter_context(nc.sbuf_tensor(f"{name}_{phase['n']}", list(shape), dt))

        def barrier(dsems=()):
            engs = (PE, ACT, DVE)
            snap = [(E, E.n) for E in engs]
            for E in engs:
                for (O, n) in snap:
                    if O is not E and n > 0:
                        fw._wait(E, ('e', O, n))
                for ds in dsems:
                    if ds.n:
                        fw._wait(E, ('d', ds, ds.n))

        pb = [fw.ps(f"pb{i}", [128, 512], F32) for i in range(8)]
        rpb = [Res(f"pb{i}") for i in range(8)]

        def pbf(i):
            return pb[i]

        def pbb(i):
            return pb[i][:].bitcast(BF16)

        identf = sb("identf", [128, 128], F32); identb = sb("identb", [128, 128], BF16); rid = Res("ident")
        onesb = sb("onesb", [128, 128], BF16); rones = Res("ones")
        onesf = sb("onesf", [128, 128], F32)
        tri = sb("tri", [128, 128], F32); caus01 = sb("caus01", [128, 128], F32); rmask = Res("masks")
        dconst = DSem(fw, "const")
        fw.op(POOL, lambda e: e.memset(identf[:], 1.0), W=[rid])
        fw.op(POOL, lambda e: e.affine_select(out=identf[:], in_=identf[:], pattern=[[-1, 128]], compare_op=ALU.is_equal,
                                              fill=0.0, base=0, channel_multiplier=1), R=[rid], W=[rid])
        fw.op(DVE, lambda e: e.tensor_copy(out=identb[:], in_=identf[:]), R=[rid], W=[rid])
        fw.op(DVE, lambda e: e.memset(onesb[:], 1.0), W=[rones])
        fw.op(DVE, lambda e: e.memset(onesf[:], 1.0), W=[rones])
        fw.dma(SP, dconst, tri[:], ctab["tri"][:, :], W=[rmask])
        fw.dma(SP, dconst, caus01[:], ctab["caus01"][:, :], W=[rmask])

        vT_ln = sb("vT_ln", [128, 128], F32); vT_g = sb("vT_g", [128, 16], F32); rvt = Res("vt")
        adab = sb("adab", [128, 2, 96], F32)
        cT = sb("cT", [128, 32], F32); scT = sb("scT", [128, 16, 2], BF16); rct = Res("cT")
        ld0 = sb("ld0", [128, 128], F32); ld1 = sb("ld1", [128, 128], F32); rld0 = Res(); rld1 = Res()
        dld = DSem(fw, "ld")

        def load_T(dst_ap, src_ap, rows, rdst, stg, rstg, bank):
            fw.dma(SP, dld, stg[0:rows, :], src_ap, W=[rstg])
            fw.op(PE, lambda e: e.transpose(out=pbf(bank)[:, 0:rows], in_=stg[0:rows, :], identity=identf[0:rows, 0:rows]),
                  R=[rstg, rid], W=[rpb[bank]])
            fw.op(DVE, lambda e: e.tensor_copy(out=dst_ap, in_=pbf(bank)[:, 0:rows]), R=[rpb[bank]], W=[rdst])

        load_T(vT_ln[:, :], vtab_ln[:, :], 128, rvt, ld0, rld0, 0)
        load_T(vT_g[:, :], vtab_g[:, :], 16, rvt, ld1, rld1, 1)
        load_T(adab[:, 0, :], vtab_ada[0, :, :], 96, rvt, ld0, rld0, 0)
        load_T(adab[:, 1, :], vtab_ada[1, :, :], 96, rvt, ld1, rld1, 1)
        load_T(cT[:, :], cvec[:, :], 32, rct, ld0, rld0, 0)
        fw.op(ACT, lambda e: e.activation(out=scT[:].rearrange("p c v -> p v c"), in_=cT[:].rearrange("p (v c) -> p v c", v=2),
                                          func=AF.Silu), R=[rct], W=[rct])

        wsl = [sb(f"wsl{i}", [128, WSLOT_ELEMS], BF16) for i in range(NSLOT)]
        rws = [Res(f"wsl{i}") for i in range(NSLOT)]
        dws = [DSem(fw, f"w{i}", serial=False) for i in range(NSLOT)]
        wstate = {"next": 0, "issued": {}}

        def wissue(g):
            if g is None or g[0] in wstate["issued"]:
                return
            key, kch, nct, parts = g
            assert kch * nct <= WSLOT_ELEMS
            si = wstate["next"]; wstate["next"] = (si + 1) % NSLOT
            view = wsl[si][:, 0:kch * nct].rearrange("p (k c) -> p k c", k=kch)
            for (ap, co) in parts:
                ncols = ap.shape[1]
                fw.dma(POOL, dws[si], view[:, :, co:co + ncols], ap.rearrange("(k p) c -> p k c", p=128), W=[rws[si]])
            wstate["issued"][key] = si

        def wget(g):
            wissue(g)
            si = wstate["issued"].pop(g[0])
            key, kch, nct, parts = g
            return si, wsl[si][:, 0:kch * nct].rearrange("p (k c) -> p k c", k=kch)

        def wgroup(key, W2d, col0, ncols, kch=NCH, row0=0):
            return (key, kch, ncols, [(W2d[row0:row0 + kch * 128, col0:col0 + ncols], 0)])

        dense_banks = [0, 1, 2, 3]
        dstate = {"i": 0}

        def next_bank():
            b = dense_banks[dstate["i"] % len(dense_banks)]; dstate["i"] += 1
            return b

        def dense_fm(groups, inT, rin, ntok, evac, nxt=None):
            for gi, g in enumerate(groups):
                si, wv = wget(g)
                wissue(groups[gi + 1] if gi + 1 < len(groups) else nxt)
                if gi + 2 < len(groups):
                    wissue(groups[gi + 2])
                key, kch, nct, parts = g
                for c0 in range(0, nct, 128):
                    M = min(128, nct - c0)
                    b = next_bank()
                    for k in range(kch):
                        fw.op(PE, lambda e: e.matmul(pbf(b)[0:M, 0:ntok], wv[:, k, c0:c0 + M], inT[:, k, 0:ntok],
                                                     start=(k == 0), stop=(k == kch - 1)),
                              R=[rws[si], rin], W=[rpb[b]])
                    evac(key, c0 // 128, pbf(b)[0:M, 0:ntok], rpb[b])

        def dense_tm(groups, inT, rin, ntok, evac, nxt=None):
            nst = (ntok + 127) // 128
            for gi, g in enumerate(groups):
                si, wv = wget(g)
                wissue(groups[gi + 1] if gi + 1 < len(groups) else nxt)
                if gi + 2 < len(groups):
                    wissue(groups[gi + 2])
                key, kch, nct, parts = g
                for st in range(nst):
                    nt = min(128, ntok - st * 128)
                    b = next_bank()
                    for k in range(kch):
                        fw.op(PE, lambda e: e.matmul(pbf(b)[0:nt, 0:nct], inT[:, k, st * 128:st * 128 + nt], wv[:, k, 0:nct],
                                                     start=(k == 0), stop=(k == kch - 1)),
                              R=[rws[si], rin], W=[rpb[b]])
                    evac(key, st, pbf(b)[0:nt, 0:nct], rpb[b], nt)

        modT = sb("modT", [128, 2, 96, 2], F32); rmod = Res("mod")

        def ada_evac(l):
            def f(key, ci, ps, rps):
                c = key[2] * 2 + ci
                fw.op(DVE, lambda e: e.tensor_scalar(out=modT[:, l, c, :], in0=ps, scalar1=adab[:, l, c:c + 1], scalar2=None,
                                                     op0=ALU.add), R=[rps, rvt], W=[rmod])
            return f

        def ada_groups(l):
            return [wgroup(("ada", l, gi), ada_w[l], gi * 256, 256) for gi in range(48)]

        def ada_post(l):
            for v in (1, 4):
                fw.op(DVE, lambda e: e.tensor_scalar(out=modT[:, l, v * 16:(v + 1) * 16, :], in0=modT[:, l, v * 16:(v + 1) * 16, :],
                                                     scalar1=1.0, scalar2=None, op0=ALU.add), R=[rmod], W=[rmod])
            for v in (2, 5):
                fw.op(DVE, lambda e: e.tensor_scalar(out=modT[:, l, v * 16:(v + 1) * 16, :], in0=modT[:, l, v * 16:(v + 1) * 16, :],
                                                     scalar1=1.0 / ALPHA, scalar2=None, op0=ALU.mult), R=[rmod], W=[rmod])

        xT = sb("xT", [128, NCH, 512], F32); rxT = Res("xT")
        hT = sb("hT", [128, NCH, 512], BF16); rhT = Res("hT")
        oT = hT; roT = rhT
        iobuf = sb("iobuf", [128, D], F32); rio = Res("iobuf"); dio = DSem(fw, "io")
        kvstag = sb("kvstag", [128, 2, 256], F32)
        epsln = sb("epsln", [128, 1], F32); epsb = sb("epsb", [128, 1], F32)
        fw.op(DVE, lambda e: e.memset(epsln[:], EPS_LN), W=[rones])
        fw.op(DVE, lambda e: e.memset(epsb[:], EPS), W=[rones])

        def load_x_tile(src, ntok):
            nst = (ntok + 127) // 128
            for st in range(nst):
                nt = min(128, ntok - st * 128)
                fw.dma(SP, dio, iobuf[0:nt, :], src[st * 128:st * 128 + nt, :], W=[rio])
                for c4 in range(4):
                    b = next_bank()
                    for cc in range(4):
                        c = c4 * 4 + cc
                        fw.op(PE, lambda e: e.transpose(out=pbf(b)[:, cc * 128:cc * 128 + nt], in_=iobuf[0:nt, c * 128:(c + 1) * 128],
                                                        identity=identf[0:nt, 0:nt]), R=[rio, rid], W=[rpb[b]])
                    src_v = pbf(b)[:, :].rearrange("p (a t) -> p a t", a=4)[:, :, 0:nt]
                    dst_v = xT[:, c4 * 4:(c4 + 1) * 4, st * 128:st * 128 + nt]
                    if c4 % 2 == 0:
                        fw.op(ACT, lambda e: e.activation(out=dst_v, in_=src_v, func=AF.Copy), R=[rpb[b]], W=[rxT])
                    else:
                        fw.op(DVE, lambda e: e.tensor_copy(out=dst_v, in_=src_v), R=[rpb[b]], W=[rxT])

        def store_x_tile(dst, ntok):
            nst = (ntok + 127) // 128
            for st in range(nst):
                nt = min(128, ntok - st * 128)
                for c4 in range(4):
                    b = next_bank()
                    for cc in range(4):
                        c = c4 * 4 + cc
                        fw.op(PE, lambda e: e.transpose(out=pbf(b)[0:nt, cc * 128:(cc + 1) * 128], in_=xT[:, c, st * 128:st * 128 + nt], identity=identf[:, :]),
                              R=[rxT, rid], W=[rpb[b]])
                    if c4 % 2 == 0:
                        fw.op(ACT, lambda e: e.activation(out=iobuf[0:nt, c4 * 512:(c4 + 1) * 512], in_=pbf(b)[0:nt, :], func=AF.Copy), R=[rpb[b]], W=[rio])
                    else:
                        fw.op(DVE, lambda e: e.tensor_copy(out=iobuf[0:nt, c4 * 512:(c4 + 1) * 512], in_=pbf(b)[0:nt, :]), R=[rpb[b]], W=[rio])
                fw.dma(SP, dio, dst[st * 128:st * 128 + nt, :], iobuf[0:nt, :], R=[rio])

        def modulate_from_x(l, vsh, vsc, grp, ntok):
            for c in range(NCH):
                fw.op(ACT, lambda e: e.activation(out=hT[:, c, 0:ntok], in_=xT[:, c, 0:ntok], func=AF.Identity,
                                                  scale=modT[:, l, vsc * 16 + c, grp:grp + 1], bias=modT[:, l, vsh * 16 + c, grp:grp + 1]),
                      R=[rxT, rmod], W=[rhT])

        def layer_norm_T(l_ln, vg, vb, ntok, mod=None):
            zt = psb("zt", [128, 2, 512], F32); rzt = [Res("zt0"), Res("zt1")]
            lnst = psb("lnst", [128, 3, 512], F32); rlnst = Res("lnst")
            ub = psb("ub", [128, 2, 2, 512], BF16); rub = [Res("ub0"), Res("ub1")]
            b1, b2 = 4, 5
            for c in range(NCH):
                u = c % 2
                fw.op(DVE, lambda e: e.tensor_copy(out=ub[:, u, 0, 0:ntok], in_=xT[:, c, 0:ntok]), R=[rxT], W=[rub[u]])
                fw.op(ACT, lambda e: e.activation(out=ub[:, u, 1, 0:ntok], in_=xT[:, c, 0:ntok], func=AF.Square), R=[rxT], W=[rub[u]])
                fw.op(PE, lambda e: e.matmul(pbf(b1)[:, 0:ntok], onesb[:, :], ub[:, u, 0, 0:ntok], start=(c == 0), stop=(c == NCH - 1)),
                      R=[rub[u], rones], W=[rpb[b1]])
                fw.op(PE, lambda e: e.matmul(pbf(b2)[:, 0:ntok], onesb[:, :], ub[:, u, 1, 0:ntok], start=(c == 0), stop=(c == NCH - 1)),
                      R=[rub[u], rones], W=[rpb[b2]])
            mean = lnst[:, 0, 0:ntok]; var = lnst[:, 1, 0:ntok]; rstd = lnst[:, 2, 0:ntok]
            fw.op(DVE, lambda e: e.tensor_scalar(out=mean, in0=pbf(b1)[:, 0:ntok], scalar1=1.0 / D, scalar2=None, op0=ALU.mult),
                  R=[rpb[b1]], W=[rlnst])
            fw.op(DVE, lambda e: e.tensor_tensor(out=var, in0=mean, in1=mean, op=ALU.mult), R=[rlnst], W=[rlnst])
            fw.op(DVE, lambda e: e.scalar_tensor_tensor(out=var, in0=pbf(b2)[:, 0:ntok], scalar=1.0 / D, in1=var, op0=ALU.mult, op1=ALU.subtract),
                  R=[rpb[b2], rlnst], W=[rlnst])
            fw.op(ACT, lambda e: e.activation(out=var, in_=var, func=AF.Sqrt, bias=epsln[:, 0:1]), R=[rlnst, rones], W=[rlnst])
            fw.op(DVE, lambda e: e.reciprocal(out=rstd, in_=var), R=[rlnst], W=[rlnst])
            for c in range(NCH):
                u = c % 2
                z = zt[:, u, 0:ntok]
                fw.op(DVE, lambda e: e.tensor_tensor(out=z, in0=xT[:, c, 0:ntok], in1=mean, op=ALU.subtract), R=[rxT, rlnst], W=[rzt[u]])
                fw.op(DVE, lambda e: e.tensor_tensor(out=z, in0=z, in1=rstd, op=ALU.mult), R=[rlnst, rzt[u]], W=[rzt[u]])
                gcol = vT_ln[:, (l_ln * 4 + vg) * 16 + c:(l_ln * 4 + vg) * 16 + c + 1]
                bcol = vT_ln[:, (l_ln * 4 + vb) * 16 + c:(l_ln * 4 + vb) * 16 + c + 1]
                fw.op(ACT, lambda e: e.activation(out=xT[:, c, 0:ntok], in_=z, func=AF.Identity, scale=gcol, bias=bcol),
                      R=[rzt[u], rvt], W=[rxT])
                if mod is not None:
                    l, vsh, vsc, grp = mod
                    fw.op(DVE, lambda e: e.tensor_scalar(out=hT[:, c, 0:ntok], in0=xT[:, c, 0:ntok],
                                                         scalar1=modT[:, l, vsc * 16 + c, grp:grp + 1], scalar2=modT[:, l, vsh * 16 + c, grp:grp + 1],
                                                         op0=ALU.mult, op1=ALU.add), R=[rxT, rmod], W=[rhT])

        def resid_evac(l, vgt, grp, ntok):
            def f(key, ci, ps, rps):
                c = key[2] * 2 + ci
                fw.op(DVE, lambda e: e.scalar_tensor_tensor(out=xT[:, c, 0:ntok], in0=ps, scalar=modT[:, l, vgt * 16 + c, grp:grp + 1],
                                                            in1=xT[:, c, 0:ntok], op0=ALU.mult, op1=ALU.add), R=[rps, rmod, rxT], W=[rxT])
            return f

        def ffn(l, grp, ntok, nxt=None):
            gT = psb("gT", [128, 22, 512], BF16); rgT = Res("gT")
            sa = psb("sa", [128, 2, 512], F32); rsa = [Res("sa0"), Res("sa1")]
            cnt = {"i": 0}
            for hh in range(2):
                groups = []
                for j in range(22):
                    c0 = hh * 2816 + j * 128
                    groups.append((("fi", l, hh, j), NCH, 256, [(ffn_w_in[l][:, c0:c0 + 128], 0), (ffn_w_in[l][:, DFF + c0:DFF + c0 + 128], 128)]))
                out_groups = [(("fo", l, hh, og), 22, 128, [(ffn_w_out[l][hh * 2816:(hh + 1) * 2816, og * 128:(og + 1) * 128], 0)]) for og in range(16)]

                def evac_in(key, ci, ps, rps):
                    j = key[3]; u = j % 2
                    if ci == 0:
                        fw.op(ACT, lambda e: e.activation(out=sa[:, u, 0:ntok], in_=ps, func=AF.Silu), R=[rps], W=[rsa[u]])
                    else:
                        fw.op(DVE, lambda e: e.tensor_tensor(out=gT[:, j, 0:ntok], in0=ps, in1=sa[:, u, 0:ntok], op=ALU.mult),
                              R=[rps, rsa[u]], W=[rgT])
                dense_fm(groups, hT, rhT, ntok, evac_in, nxt=out_groups[0])

                def evac_out(key, ci, ps, rps):
                    c = key[3]
                    fw.op(DVE, lambda e: e.scalar_tensor_tensor(out=xT[:, c, 0:ntok], in0=ps, scalar=modT[:, l, 5 * 16 + c, grp:grp + 1],
                                                                in1=xT[:, c, 0:ntok], op0=ALU.mult, op1=ALU.add), R=[rps, rmod, rxT], W=[rxT])
                dense_fm(out_groups, gT, rgT, ntok, evac_out, nxt=(nxt if hh == 1 else None))

        alr = sb("alr", [33, 512], F32); ralr = Res("alr")
        wa = sb("wa", [33, 1024], F32); rwa = Res("wa")
        fw.dma(SP, dconst, wa[:, :], gla_wa[:, :], W=[rwa])
        fw.op(DVE, lambda e: e.memset(alr[:, :], 0.0), W=[ralr])
        fw.op(DVE, lambda e: e.memset(alr[32:33, :], 1.0), W=[ralr])
        dst = DSem(fw, "state")
        gq_groups = [wgroup(("gq", 0, gi), gla_w_in, gi * 256, 256) for gi in range(8)] + [wgroup(("ga", 0, 0), gla_w_in, 6144, 16)]
        gv_groups = [wgroup(("gv", 0, gi), gla_w_in, 2048 + gi * 256, 256) for gi in range(16)]
        gout_groups = [wgroup(("go", 0, gi), gla_w_out, gi * 256, 256) for gi in range(8)]

        def gla_mixer(ntok, S, rS):
            qT = psb("qT", [128, 8, 512], BF16); kT = psb("kT", [128, 8, 512], BF16); rqk = Res("qk")
            vtm = psb("vtm", [128, 4, D], BF16); rvtm = Res("vtm")
            grtm = psb("grtm", [128, 4, D], BF16); rgr = Res("grtm")
            Sb = psb("Sb", [128, 2, 512], BF16); rSb = Res("Sb")
            lsp = psb("lsp", [128, 1024], F32); rlsp = Res("lsp")
            bT = psb("bT", [128, 8, 128], F32); rbT = Res("bT")
            dd = psb("dd", [128, 8, 128], F32); rdd = Res("dd")
            ee = psb("ee", [128, 8, 128], F32); ree = Res("ee")
            qtl = psb("qtl", [128, 8, 128], BF16); qh = psb("qh", [128, 8, 128], BF16)
            kh = psb("kh", [128, 8, 128], BF16); kbT = psb("kbT", [128, 8, 128], BF16); rqq = Res("qq")
            kbtm = psb("kbtm", [128, 1024], BF16); rkb = Res("kbtm")
            dec = psb("dec", [128, 8], F32); rdec = Res("dec")
            aT = psb("aT", [128, 2, 128], BF16); raT = [Res("aT0"), Res("aT1")]
            ssq = psb("ssq", [128, 8], F32); rssq = Res("ssq")
            junk = psb("junk", [128, 512], BF16); rjunk = Res("junk")
            on = psb("on", [128, D], BF16); ron = Res("on")
            if ntok == 4:
                print("[build] gla phase sbuf_left", nc.sbuf_bytes_remaining)

            def qk_evac(key, ci, ps, rps):
                if key[0] == "ga":
                    fw.op(DVE, lambda e: e.tensor_copy(out=alr[0:16, 0:ntok], in_=ps), R=[rps], W=[ralr])
                    return
                c = key[2] * 2 + ci
                if c < 8:
                    fw.op(ACT, lambda e: e.activation(out=qT[:, c, 0:ntok], in_=ps, func=AF.Copy, scale=256.0 ** -0.5), R=[rps], W=[rqk])
                else:
                    fw.op(DVE, lambda e: e.tensor_copy(out=kT[:, c - 8, 0:ntok], in_=ps), R=[rps], W=[rqk])

            def vr_evac(key, st, ps, rps, nt):
                gi = key[2]
                if gi < 8:
                    fw.op(DVE, lambda e: e.tensor_copy(out=vtm[0:nt, st, gi * 256:(gi + 1) * 256], in_=ps), R=[rps], W=[rvtm])
                else:
                    fw.op(ACT, lambda e: e.activation(out=grtm[0:nt, st, (gi - 8) * 256:(gi - 7) * 256], in_=ps, func=AF.Silu), R=[rps], W=[rgr])

            dense_fm(gq_groups, hT, rhT, ntok, qk_evac, nxt=gv_groups[0])
            dense_tm(gv_groups, hT, rhT, ntok, vr_evac, nxt=gout_groups[0])

            for ci in range((ntok + 127) // 128):
                C = min(128, ntok - ci * 128); t0 = ci * 128
                mid = max(C // 2 - 1, 0)
                for hf in range(2):
                    b = 6 + hf
                    fw.op(PE, lambda e: e.matmul(pbf(b)[0:C, :], alr[:, t0:t0 + C], wa[:, hf * 512:(hf + 1) * 512], start=True, stop=True),
                          R=[ralr, rwa], W=[rpb[b]])
                    fw.op(ACT, lambda e: e.activation(out=lsp[0:C, hf * 512:(hf + 1) * 512], in_=pbf(b)[0:C, :], func=AF.Exp, scale=-1.0),
                          R=[rpb[b]], W=[rlsp])
                fw.op(ACT, lambda e: e.activation(out=lsp[0:C, :], in_=lsp[0:C, :], func=AF.Ln, bias=onesf[0:C, 0:1]), R=[rlsp, rones], W=[rlsp])
                for dc in range(8):
                    b = 6 + dc // 4
                    fw.op(PE, lambda e: e.matmul(pbf(b)[:, (dc % 4) * 128:(dc % 4) * 128 + C], lsp[0:C, dc * 128:(dc + 1) * 128], tri[0:C, 0:C],
                                                 start=True, stop=True), R=[rlsp, rmask], W=[rpb[b]])
                for hf in range(2):
                    b = 6 + hf
                    fw.op(DVE, lambda e: e.tensor_copy(out=bT[:, hf * 4:(hf + 1) * 4, 0:C],
                                                       in_=pbf(b)[:, :].rearrange("p (a t) -> p a t", a=4)[:, :, 0:C]), R=[rpb[b]], W=[rbT])
                q3 = qT[:, :, t0:t0 + C]; k3 = kT[:, :, t0:t0 + C]
                bmid = bT[:, :, mid:mid + 1].to_broadcast([128, 8, C]); blast = bT[:, :, C - 1:C].to_broadcast([128, 8, C])
                fw.op(ACT, lambda e: e.activation(out=ee[:, :, 0:C], in_=bT[:, :, 0:C], func=AF.Exp), R=[rbT], W=[ree])
                fw.op(DVE, lambda e: e.tensor_tensor(out=qtl[:, :, 0:C], in0=q3, in1=ee[:, :, 0:C], op=ALU.mult), R=[rqk, ree], W=[rqq])
                fw.op(ACT, lambda e: e.activation(out=dec[:, :], in_=bT[:, :, C - 1], func=AF.Exp), R=[rbT], W=[rdec])
                fw.op(DVE, lambda e: e.tensor_tensor(out=dd[:, :, 0:C], in0=bT[:, :, 0:C], in1=bmid, op=ALU.subtract), R=[rbT], W=[rdd])
                fw.op(ACT, lambda e: e.activation(out=ee[:, :, 0:C], in_=dd[:, :, 0:C], func=AF.Exp), R=[rdd], W=[ree])
                fw.op(DVE, lambda e: e.tensor_tensor(out=qh[:, :, 0:C], in0=q3, in1=ee[:, :, 0:C], op=ALU.mult), R=[rqk, ree], W=[rqq])
                fw.op(ACT, lambda e: e.activation(out=ee[:, :, 0:C], in_=dd[:, :, 0:C], func=AF.Exp, scale=-1.0), R=[rdd], W=[ree])
                fw.op(DVE, lambda e: e.tensor_tensor(out=kh[:, :, 0:C], in0=k3, in1=ee[:, :, 0:C], op=ALU.mult), R=[rqk, ree], W=[rqq])
                fw.op(DVE, lambda e: e.tensor_tensor(out=dd[:, :, 0:C], in0=bT[:, :, 0:C], in1=blast, op=ALU.subtract), R=[rbT], W=[rdd])
                fw.op(ACT, lambda e: e.activation(out=ee[:, :, 0:C], in_=dd[:, :, 0:C], func=AF.Exp, scale=-1.0), R=[rdd], W=[ree])
                fw.op(DVE, lambda e: e.tensor_tensor(out=kbT[:, :, 0:C], in0=k3, in1=ee[:, :, 0:C], op=ALU.mult), R=[rqk, ree], W=[rqq])
                for dc in range(8):
                    fw.op(PE, lambda e: e.transpose(out=pbb(6)[0:C, dc * 128:(dc + 1) * 128], in_=kbT[:, dc, 0:C], identity=identb[:, :]),
                          R=[rqq, rid], W=[rpb[6]])
                fw.op(DVE, lambda e: e.tensor_copy(out=kbtm[0:C, :], in_=pbb(6)[0:C, :]), R=[rpb[6]], W=[rkb])
                fw.op(DVE, lambda e: e.memset(ssq[:, :], 0.0), W=[rssq])
                for h in range(4):
                    u = h % 2
                    bo = 4 + u; ba = 7
                    fw.op(ACT, lambda e: e.activation(out=Sb[:, :, :], in_=S[:, 2 * h:2 * h + 2, :], func=AF.Copy), R=[rS], W=[rSb])
                    for half in range(2):
                        fw.op(PE, lambda e: e.matmul(pbf(ba)[0:C, 0:C], kh[:, 2 * h + half, 0:C], qh[:, 2 * h + half, 0:C],
                                                     start=(half == 0), stop=(half == 1)), R=[rqq], W=[rpb[ba]])
                    fw.op(DVE, lambda e: e.tensor_tensor(out=aT[0:C, u, 0:C], in0=pbf(ba)[0:C, 0:C], in1=caus01[0:C, 0:C], op=ALU.mult),
                          R=[rpb[ba], rmask], W=[raT[u]])
                    for half in range(2):
                        fw.op(PE, lambda e: e.matmul(pbf(bo)[0:C, :], qtl[:, 2 * h + half, 0:C], Sb[:, half, :], start=(half == 0), stop=False),
                              R=[rqq, rSb], W=[rpb[bo]])
                    fw.op(PE, lambda e: e.matmul(pbf(bo)[0:C, :], aT[0:C, u, 0:C], vtm[0:C, ci, h * 512:(h + 1) * 512], start=False, stop=True),
                          R=[raT[u], rvtm], W=[rpb[bo]])
                    for half in range(2):
                        bs = 6 if half == 0 else 7
                        fw.op(PE, lambda e: e.matmul(pbf(bs)[:, :], kbtm[0:C, (2 * h + half) * 128:(2 * h + half + 1) * 128],
                                                     vtm[0:C, ci, h * 512:(h + 1) * 512], start=True, stop=True), R=[rkb, rvtm], W=[rpb[bs]])
                        fw.op(DVE, lambda e: e.scalar_tensor_tensor(out=S[:, 2 * h + half, :], in0=S[:, 2 * h + half, :],
                                                                    scalar=dec[:, 2 * h + half:2 * h + half + 1], in1=pbf(bs)[:, :],
                                                                    op0=ALU.mult, op1=ALU.add), R=[rS, rdec, rpb[bs]], W=[rS])
                    fw.op(ACT, lambda e: e.activation(out=junk[0:C, :], in_=pbf(bo)[0:C, :], func=AF.Square, accum_out=ssq[0:C, h:h + 1]),
                          R=[rpb[bo]], W=[rjunk, rssq])
                    fw.op(ACT, lambda e: e.activation(out=ssq[0:C, 4 + h:5 + h], in_=ssq[0:C, h:h + 1], func=AF.Sqrt, scale=1.0 / 512, bias=epsb[0:C, 0:1]),
                          R=[rssq, rones], W=[rssq])
                    fw.op(DVE, lambda e: e.reciprocal(out=ssq[0:C, 4 + h:5 + h], in_=ssq[0:C, 4 + h:5 + h]), R=[rssq], W=[rssq])
                    fw.op(DVE, lambda e: e.scalar_tensor_tensor(out=on[0:C, h * 512:(h + 1) * 512], in0=pbf(bo)[0:C, :], scalar=ssq[0:C, 4 + h:5 + h],
                                                                in1=grtm[0:C, ci, h * 512:(h + 1) * 512], op0=ALU.mult, op1=ALU.mult),
                          R=[rpb[bo], rgr], W=[ron], RS=[rssq])
                dump("on", on[:, :], [ron]); dump("ssq", ssq[:, :], [rssq]); dump("bT", bT[:, :, :], [rbT]); dump("qT", qT[:, :, :], [rqk]); dump("kT", kT[:, :, :], [rqk])
                dump("vtm", vtm[:, 0, :], [rvtm]); dump("grtm", grtm[:, 0, :], [rgr]); dump("lsp", lsp[:, :], [rlsp]); dump("aT", aT[:, :, :], raT); dump("qh", qh[:, :, :], [rqq]); dump("kh", kh[:, :, :], [rqq])
                dump("alr", alr[:, :], [ralr]); dump("hT", hT[:, :, :], [rhT])
                for half in range(2):
                    b = 4 + half
                    for cc in range(8):
                        c = half * 8 + cc
                        fw.op(PE, lambda e: e.transpose(out=pbb(b)[:, cc * 128:cc * 128 + C], in_=on[0:C, c * 128:(c + 1) * 128], identity=identb[0:C, 0:C]),
                              R=[ron, rid], W=[rpb[b]])
                    fw.op(DVE, lambda e: e.tensor_tensor(out=oT[:, half * 8:(half + 1) * 8, t0:t0 + C],
                                                         in0=pbb(b)[:, :].rearrange("p (a t) -> p a t", a=8)[:, :, 0:C],
                                                         in1=vT_g[:, half * 8:(half + 1) * 8].unsqueeze(2).to_broadcast([128, 8, C]), op=ALU.mult),
                          R=[rpb[b], rvt], W=[roT])

        def run_phase(fn, dsems=()):
            with contextlib.ExitStack() as pes:
                phase["es"] = pes
                fn()
                barrier(dsems)
            phase["es"] = None

        if os.environ.get("SKIP_L0"):
            fw.op(DVE, lambda e: e.memset(modT[:, :, :, :], 0.5), W=[rmod])
        for l in range(2 if not os.environ.get("SKIP_L0") else 0):
            dense_fm(ada_groups(l), scT, rct, 2, ada_evac(l))
            ada_post(l)

        x1s = nc.dram_tensor("x1_scratch", [T + 128, D], F32).ap()
        with contextlib.ExitStack() as les:
            S = les.enter_context(nc.sbuf_tensor("S", [128, 8, 512], F32)); rS = Res("S")
            fw.op(DVE, lambda e: e.memset(S[:, :, :], 0.0), W=[rS])

            def layer0_tile(src, ntok, grp, dst):
                load_x_tile(src, ntok)
                modulate_from_x(0, 0, 1, grp, ntok)
                run_phase(lambda: gla_mixer(ntok, S, rS))

                def rest():
                    dense_fm(gout_groups, oT, roT, ntok, resid_evac(0, 2, grp, ntok))
                    if stage == "pre0":
                        store_x_tile(dst, ntok); return
                    layer_norm_T(0, 0, 1, ntok, mod=(0, 3, 4, grp))
                    if stage == "mix0":
                        store_x_tile(dst, ntok); return
                    ffn(0, grp, ntok)
                    layer_norm_T(0, 2, 3, ntok, mod=None)
                    store_x_tile(dst, ntok)
                run_phase(rest)

            for ti in range(n_ptiles if not os.environ.get("SKIP_L0") else 0):
                layer0_tile(xp[ti * 512:(ti + 1) * 512, :], 512, 0, (y_p if stage in ("l0", "mix0", "pre0") else x1s)[ti * 512:(ti + 1) * 512, :])
            for h in range(4):
                fw.dma(SP, dst, st_p[h].rearrange("(a p) e -> p a e", p=128), S[:, 2 * h:2 * h + 2, :], R=[rS])
            if do_sample and not os.environ.get("SKIP_L0"):
                for h in range(4):
                    fw.dma(SP, dst, S[:, 2 * h:2 * h + 2, :], state_in[h].rearrange("(a p) e -> p a e", p=128), W=[rS])
                layer0_tile(xs[:, :], 4, 1, (y_s[0:4, :] if stage in ("l0", "mix0", "pre0") else x1s[T:T + 4, :]))
                for h in range(4):
                    fw.dma(SP, dst, st_s[h].rearrange("(a p) e -> p a e", p=128), S[:, 2 * h:2 * h + 2, :], R=[rS])
            barrier([dst, dio])
        if stage in ("full", "mix1", "pre1"):
          with contextlib.ExitStack() as les:
            def lsb(name, shape, dt):
                return les.enter_context(nc.sbuf_tensor(name, list(shape), dt))
            KsT = lsb("KsT", [128, 4, T], BF16); KwT = lsb("KwT", [128, 4, 1024], BF16); rKs = Res("KsT"); rKw = Res("KwT")
            Vs = lsb("Vs", [128, 16, 4, 130], BF16); Vw = lsb("Vw", [128, 8, 4, 130], BF16); rVs = Res("Vs"); rVw = Res("Vw")
            kcacc = lsb("kcacc", [128, 4, 32], F32); vcacc = lsb("vcacc", [32, 512], F32); rkc = Res("kc"); rvc = Res("vc")
            kcT = lsb("kcT", [128, 4, 32], BF16); vcb = lsb("vcb", [32, 512], BF16)
            causb = lsb("causb", [128, 512], BF16); winb = lsb("winb", [128, 512], BF16); eall = lsb("eall", [32, T], BF16)
            avgA = lsb("avgA", [128, 62], BF16)
            Tc = lsb("Tc", [128, 62], F32); Tm = lsb("Tm", [128, 62], F32); Tf = lsb("Tf", [128, 62], F32)
            bg = lsb("bg", [128, 48], F32); rk1 = Res("l1const"); rk1p = Res("l1constp")
            dk1 = DSem(fw, "l1const"); dk1p = DSem(fw, "l1constp"); dkv = DSem(fw, "kvout")
            for (dst_t, nm, rows, cols) in ((causb, "causb", 128, 512), (winb, "winb", 128, 512), (eall, "eall", 32, T), (avgA, "avgA", 128, 62)):
                fw.dma(SP, dio, iobuf[0:rows, 0:cols], ctab[nm][:, :], W=[rio])
                fw.op(DVE, lambda e: e.tensor_copy(out=dst_t[:, :], in_=iobuf[0:rows, 0:cols]), R=[rio], W=[rk1p])
            fw.dma(SP, dk1, Tc[:, :], ctab["Tc"][:, :], W=[rk1])
            fw.dma(SP, dk1, Tm[:, :], ctab["Tm"][:, :], W=[rk1])
            fw.dma(SP, dk1, Tf[:, :], ctab["Tf"][:, :], W=[rk1])
            fw.dma(SP, dk1, bg[:, :], nsa_bg[0:1, :].partition_broadcast(128), W=[rk1])
            fw.op(DVE, lambda e: e.memset(Vs[:, :, :, :], 1.0), W=[rVs])
            fw.op(DVE, lambda e: e.memset(Vw[:, :, :, :], 1.0), W=[rVw])
            fw.op(DVE, lambda e: e.memset(kcacc[:, :, :], 0.0), W=[rkc])
            fw.op(DVE, lambda e: e.memset(vcacc[:, :], 0.0), W=[rvc])

            nq_groups = [wgroup(("nq", 1, gi), nsa_w_in, gi * 256, 256) for gi in range(8)]
            nkv_groups = [wgroup(("nkv", 1, gi), nsa_w_in, 2048 + gi * 256, 256) for gi in range(12)] + [wgroup(("ng", 1, 0), nsa_w_in, 5120, 48)]
            nout_groups = [wgroup(("no", 1, gi), nsa_w_out, gi * 256, 256) for gi in range(8)]
            kv_outs = [cmp_p, slc_p, win_p]

            def nsa_prompt_tile(ti):
                ntok = 512
                qTn = psb("qTn", [128, 16, 512], BF16); rq = Res("qTn")
                gsg = psb("gsg", [128, 4, 48], F32); rgs = Res("gsg")
                o_tm = psb("o_tm", [128, D], F32); ro = Res("o_tm")
                obf = psb("obf", [128, D], BF16); rob = Res("obf")
                stag = kvstag; rstag = [Res("stag0"), Res("stag1")]
                ktmp = psb("ktmp", [128, 2, 256], BF16); rkt = [Res("kt0"), Res("kt1")]
                sm = psb("sm", [128, 16, 32], F32); rsm = Res("sm")
                pcb = psb("pcb", [128, 16, 32], BF16); rpcb = Res("pcb")
                pcT = psb("pcT", [32, 16, 128], BF16); rpcT = Res("pcT")
                st16 = psb("st16", [128, 4, 16], F32); rst16 = Res("st16")
                imp = psb("imp", [128, 4, 32], F32); rimp = Res("imp")
                m8 = psb("m8", [128, 2, 8], F32); rm8 = Res("m8")
                sct = psb("sct", [128, 32], F32); rsct = Res("sct")
                selb = psb("selb", [128, 32], F32); rselb = Res("selb")
                selbT = psb("selbT", [32, 4, 128], BF16); rselT = Res("selbT")
                pT = psb("pT", [128, 2, 512], BF16); rpT = [Res("pT0"), Res("pT1")]
                coef = psb("coef", [128, 8], F32); rcoef = Res("coef")
                cnt = {"stag": 0, "kt": 0, "pt": 0}
                if ti == 0:
                    print("[build] nsa phase sbuf_left", nc.sbuf_bytes_remaining)

                def q_evac(key, ci, ps, rps):
                    hd = key[2] * 2 + ci
                    fw.op(ACT, lambda e: e.activation(out=qTn[:, hd, 0:ntok], in_=ps, func=AF.Copy, scale=128.0 ** -0.5), R=[rps], W=[rq])

                KVP = os.environ.get("KV_PARTS", "dma,cmp,kt,v,ng").split(",")

                def kv_evac(key, st, ps, rps, nt):
                    kt_g = ti * 4 + st
                    if key[0] == "ng":
                        fw.op(DVE, lambda e: e.tensor_tensor(out=gsg[:, st, :], in0=ps, in1=bg[:, :], op=ALU.add), R=[rps, rk1], W=[rgs])
                        fw.op(ACT, lambda e: e.activation(out=gsg[:, st, :], in_=gsg[:, st, :], func=AF.Sigmoid), R=[rgs], W=[rgs])
                        return
                    gi = key[2]; br = gi // 4; sub = gi % 4; isv = sub // 2; gp = sub % 2
                    u = cnt["stag"] % 2; cnt["stag"] += 1
                    fw.op(ACT, lambda e: e.activation(out=stag[:, u, :], in_=ps, func=AF.Copy), R=[rps], W=[rstag[u]])
                    src = stag[:, u, :]; rsrc = rstag[u]
                    if br < 2 or ti == 3:
                        row0 = (ti * 512 + st * 128) if br < 2 else st * 128
                        fw.dma(SP, dkv, kv_outs[br][row0:row0 + 128, sub * 256:(sub + 1) * 256], src, R=[rsrc])
                    if br == 0:
                        uk = cnt["kt"] % 2; cnt["kt"] += 1
                        fw.op(DVE, lambda e: e.tensor_copy(out=ktmp[:, uk, :], in_=src), R=[rsrc], W=[rkt[uk]])
                        a0 = 30 - 2 * kt_g
                        if isv == 0:
                            for gg in range(2):
                                g = gp * 2 + gg
                                fw.op(PE, lambda e: e.matmul(pbf(4)[:, 0:32], ktmp[:, uk, gg * 128:(gg + 1) * 128], avgA[:, a0:a0 + 32], start=True, stop=True),
                                      R=[rkt[uk], rk1p], W=[rpb[4]])
                                fw.op(DVE, lambda e: e.tensor_tensor(out=kcacc[:, g, :], in0=kcacc[:, g, :], in1=pbf(4)[:, 0:32], op=ALU.add),
                                      R=[rpb[4], rkc], W=[rkc])
                        else:
                            fw.op(PE, lambda e: e.matmul(pbf(5)[0:32, 0:256], avgA[:, a0:a0 + 32], ktmp[:, uk, :], start=True, stop=True),
                                  R=[rkt[uk], rk1p], W=[rpb[5]])
                            fw.op(DVE, lambda e: e.tensor_tensor(out=vcacc[:, gp * 256:(gp + 1) * 256], in0=vcacc[:, gp * 256:(gp + 1) * 256],
                                                                 in1=pbf(5)[0:32, 0:256], op=ALU.add), R=[rpb[5], rvc], W=[rvc])
                    else:
                        KT, rK, VV, rV = (KsT, rKs, Vs, rVs) if br == 1 else (KwT, rKw, Vw, rVw)
                        ks_ = kt_g if br == 1 else kt_g % 8
                        if isv == 0:
                            uk = cnt["kt"] % 2; cnt["kt"] += 1
                            fw.op(DVE, lambda e: e.tensor_copy(out=ktmp[:, uk, :], in_=src), R=[rsrc], W=[rkt[uk]])
                            for gg in range(2):
                                fw.op(PE, lambda e: e.transpose(out=pbb(6)[:, gg * 128:(gg + 1) * 128], in_=ktmp[:, uk, gg * 128:(gg + 1) * 128], identity=identb[:, :]),
                                      R=[rkt[uk], rid], W=[rpb[6]])
                            fw.op(ACT, lambda e: e.activation(out=KT[:, gp * 2:gp * 2 + 2, ks_ * 128:(ks_ + 1) * 128],
                                                              in_=pbb(6)[:, 0:256].rearrange("p (a t) -> p a t", a=2), func=AF.Copy), R=[rpb[6]], W=[rK])
                        else:
                            fw.op(DVE, lambda e: e.tensor_copy(out=VV[:, ks_, gp * 2:gp * 2 + 2, 0:128],
                                                               in_=src.rearrange("p (a t) -> p a t", a=2)), R=[rsrc], W=[rV])

                CUT = int(os.environ.get("L1_CUT", "9"))
                if CUT <= 1:
                    return
                dense_fm(nq_groups, hT, rhT, ntok, q_evac, nxt=nkv_groups[0] if CUT > 2 else nout_groups[0])
                if CUT <= 2:
                    return
                dense_tm(nkv_groups, hT, rhT, ntok, kv_evac, nxt=nout_groups[0])
                if CUT <= 3:
                    return
                fw.op(ACT, lambda e: e.activation(out=kcT[:, :, :], in_=kcacc[:, :, :], func=AF.Copy), R=[rkc], W=[rkc])
                fw.op(ACT, lambda e: e.activation(out=vcb[:, :], in_=vcacc[:, :], func=AF.Copy), R=[rvc], W=[rvc])

                for st in range(4 if not os.environ.get("L1_SKIP_ATTN") else 0):
                    qt = ti * 4 + st
                    qs = slice(st * 128, (st + 1) * 128)
                    m0 = 30 - 2 * qt
                    for hd in range(16):
                        fw.op(PE, lambda e: e.matmul(pbf(4)[:, hd * 32:(hd + 1) * 32], qTn[:, hd, qs], kcT[:, hd // 4, :], start=True, stop=True),
                              R=[rq, rkc], W=[rpb[4]])
                    ps3 = pbf(4)[:, :].rearrange("p (h j) -> p h j", h=16)
                    fw.op(DVE, lambda e: e.tensor_tensor(out=sm[:, :, :], in0=ps3, in1=Tc[:, m0:m0 + 32].unsqueeze(1).to_broadcast([128, 16, 32]), op=ALU.add),
                          R=[rpb[4], rk1], W=[rsm])
                    fw.op(DVE, lambda e: e.tensor_reduce(out=st16[:, 0, :], in_=sm[:, :, :], axis=AX.X, op=ALU.max), R=[rsm], W=[rst16])
                    fw.op(DVE, lambda e: e.tensor_tensor(out=sm[:, :, :], in0=sm[:, :, :], in1=st16[:, 0, :].unsqueeze(2).to_broadcast([128, 16, 32]), op=ALU.subtract),
                          R=[rsm, rst16], W=[rsm])
                    fw.op(ACT, lambda e: e.activation(out=sm[:, :, :], in_=sm[:, :, :], func=AF.Exp), R=[rsm], W=[rsm])
                    fw.op(DVE, lambda e: e.tensor_tensor(out=sm[:, :, :], in0=sm[:, :, :], in1=Tm[:, m0:m0 + 32].unsqueeze(1).to_broadcast([128, 16, 32]), op=ALU.mult),
                          R=[rsm, rk1], W=[rsm])
                    fw.op(DVE, lambda e: e.tensor_reduce(out=st16[:, 1, :], in_=sm[:, :, :], axis=AX.X, op=ALU.add), R=[rsm], W=[rst16])
                    fw.op(DVE, lambda e: e.tensor_scalar(out=st16[:, 1, :], in0=st16[:, 1, :], scalar1=1e-30, scalar2=None, op0=ALU.max), R=[rst16], W=[rst16])
                    fw.op(DVE, lambda e: e.reciprocal(out=st16[:, 2, :], in_=st16[:, 1, :]), R=[rst16], W=[rst16])
                    fw.op(DVE, lambda e: e.tensor_tensor(out=sm[:, :, :], in0=sm[:, :, :], in1=st16[:, 2, :].unsqueeze(2).to_broadcast([128, 16, 32]), op=ALU.mult),
                          R=[rsm, rst16], W=[rsm])
                    if qt >= 8:
                        fw.op(DVE, lambda e: e.tensor_reduce(out=imp[:, :, :], in_=sm[:, :, :].rearrange("p (g h) j -> p g j h", g=4), axis=AX.X, op=ALU.add),
                              R=[rsm], W=[rimp])
                        fw.op(DVE, lambda e: e.tensor_tensor(out=imp[:, :, :], in0=imp[:, :, :], in1=Tf[:, m0:m0 + 32].unsqueeze(1).to_broadcast([128, 4, 32]), op=ALU.add),
                              R=[rimp, rk1], W=[rimp])
                        fw.op(DVE, lambda e: e.memset(imp[:, :, 0:1], 1e4), W=[rimp])
                        for g in range(4):
                            fw.op(DVE, lambda e: e.max(out=m8[:, 0, :], in_=imp[:, g, :]), R=[rimp], W=[rm8])
                            fw.op(DVE, lambda e: e.match_replace(out=sct[:, :], in_to_replace=m8[:, 0, :], in_values=imp[:, g, :], imm_value=-1e9), R=[rimp, rm8], W=[rsct])
                            fw.op(DVE, lambda e: e.max(out=m8[:, 1, :], in_=sct[:, :]), R=[rsct], W=[rm8])
                            fw.op(DVE, lambda e: e.tensor_scalar(out=selb[:, :], in0=imp[:, g, :], scalar1=m8[:, 1, 7:8], scalar2=-NEG, op0=ALU.is_ge, op1=ALU.mult),
                                  R=[rimp], W=[rselb], RS=[rm8])
                            fw.op(DVE, lambda e: e.tensor_scalar(out=selb[:, :], in0=selb[:, :], scalar1=NEG, scalar2=None, op0=ALU.add), R=[rselb], W=[rselb])
                            fw.op(PE, lambda e: e.transpose(out=pbf(6)[0:32, 0:128], in_=selb[:, :], identity=identf[:, :]), R=[rselb, rid], W=[rpb[6]])
                            fw.op(ACT, lambda e: e.activation(out=selbT[:, g, :], in_=pbf(6)[0:32, 0:128], func=AF.Copy), R=[rpb[6]], W=[rselT])
                    gc = gsg[:, st, :].rearrange("p (h b) -> p h b", b=3)[:, :, 0:1].to_broadcast([128, 16, 32])
                    fw.op(DVE, lambda e: e.tensor_tensor(out=pcb[:, :, :], in0=sm[:, :, :], in1=gc, op=ALU.mult), R=[rsm, rgs], W=[rpcb])
                    for half in range(2):
                        b = 6 + half
                        for hh in range(8):
                            hd = half * 8 + hh
                            fw.op(PE, lambda e: e.transpose(out=pbb(b)[0:32, hh * 128:(hh + 1) * 128], in_=pcb[:, hd, :], identity=identb[:, :]),
                                  R=[rpcb, rid], W=[rpb[b]])
                        fw.op(ACT, lambda e: e.activation(out=pcT[:, half * 8:(half + 1) * 8, :], in_=pbb(b)[0:32, :].rearrange("p (a t) -> p a t", a=8), func=AF.Copy),
                              R=[rpb[b]], W=[rpcT])
                    for g in range(4):
                        for h in range(4):
                            fw.op(PE, lambda e: e.matmul(pbf(5)[:, h * 128:(h + 1) * 128], pcT[:, 4 * g + h, :], vcb[:, g * 128:(g + 1) * 128], start=True, stop=True),
                                  R=[rpcT, rvc], W=[rpb[5]])
                        fw.op(ACT, lambda e: e.activation(out=o_tm[:, g * 512:(g + 1) * 512], in_=pbf(5)[:, :], func=AF.Copy), R=[rpb[5]], W=[ro])
                    for (KT, rK, VV, rV, kts, bi) in ((KsT, rKs, Vs, rVs, list(range(0, qt + 1)), 1), (KwT, rKw, Vw, rVw, list(range(max(0, qt - 4), qt + 1)), 2)):
                        for g in range(4):
                            rhsQ = qTn[:, 4 * g:4 * g + 4, qs]
                            for ki, kt in enumerate(kts):
                                ksl = kt if bi == 1 else kt % 8
                                u = cnt["pt"] % 2; cnt["pt"] += 1
                                sb_ = u
                                mask = None
                                if kt == qt:
                                    mask = ("id", causb[:, :])
                                elif bi == 2 and kt == qt - 4:
                                    mask = ("id", winb[:, :])
                                elif bi == 1 and qt >= 8:
                                    mask = ("sel", None)
                                out3 = pbf(sb_)[:, :].rearrange("p (a t) -> p a t", a=4)
                                fw.op(PE, lambda e: e.matmul(out3, KT[:, g, ksl * 128:(ksl + 1) * 128], rhsQ, start=True, stop=(mask is None)),
                                      R=[rK, rq], W=[rpb[sb_]])
                                if mask is not None and mask[0] == "id":
                                    fw.op(PE, lambda e: e.matmul(pbf(sb_)[:, :], identb[:, :], mask[1], start=False, stop=True), R=[rid, rk1p], W=[rpb[sb_]])
                                elif mask is not None:
                                    fw.op(PE, lambda e: e.matmul(out3, eall[:, kt * 128:(kt + 1) * 128], selbT[:, g, :].unsqueeze(1).to_broadcast([32, 4, 128]),
                                                                 start=False, stop=True), R=[rk1p, rselT], W=[rpb[sb_]])
                                fw.op(ACT, lambda e: e.activation(out=pT[:, u, :], in_=pbf(sb_)[:, :], func=AF.Exp), R=[rpb[sb_]], W=[rpT[u]])
                                for h in range(4):
                                    ob = 2 + h
                                    fw.op(PE, lambda e: e.matmul(pbf(ob)[:, 0:130], pT[:, u, h * 128:(h + 1) * 128], VV[:, ksl, g, 0:130],
                                                                 start=(ki == 0), stop=(ki == len(kts) - 1)), R=[rpT[u], rV], W=[rpb[ob]])
                            for h in range(4):
                                ob = 2 + h; c0 = 0; hd = 4 * g + h
                                fw.op(DVE, lambda e: e.reciprocal(out=coef[:, h:h + 1], in_=pbf(ob)[:, c0 + 128:c0 + 129]), R=[rpb[ob]], W=[rcoef])
                                fw.op(DVE, lambda e: e.tensor_tensor(out=coef[:, 4 + h:5 + h], in0=coef[:, h:h + 1], in1=gsg[:, st, hd * 3 + bi:hd * 3 + bi + 1], op=ALU.mult),
                                      R=[rcoef, rgs], W=[rcoef])
                                fw.op(DVE, lambda e: e.scalar_tensor_tensor(out=o_tm[:, hd * 128:(hd + 1) * 128], in0=pbf(ob)[:, c0:c0 + 128], scalar=coef[:, 4 + h:5 + h],
                                                                            in1=o_tm[:, hd * 128:(hd + 1) * 128], op0=ALU.mult, op1=ALU.add),
                                      R=[rpb[ob], ro], W=[ro], RS=[rcoef])
                    dump("o_tm", o_tm[:, :], [ro]); dump("gsg", gsg[:, :, :], [rgs]); dump("sm", sm[:, :, :], [rsm]); dump("qTn", qTn[:, :, :], [rq])
                    dump("KsT", KsT[:, :, 0:512], [rKs]); dump("Vs", Vs[:, 0:4, :, :], [rVs]); dump("kcT", kcT[:, :, :], [rkc]); dump("vcb", vcb[:, :], [rvc]); dump("coef", coef[:, :], [rcoef])
                    fw.op(ACT, lambda e: e.activation(out=obf[:, :], in_=o_tm[:, :], func=AF.Copy), R=[ro], W=[rob])
                    for half in range(2):
                        b = 6 + half
                        for cc in range(8):
                            c = half * 8 + cc
                            fw.op(PE, lambda e: e.transpose(out=pbb(b)[:, cc * 128:(cc + 1) * 128], in_=obf[:, c * 128:(c + 1) * 128], identity=identb[:, :]),
                                  R=[rob, rid], W=[rpb[b]])
                        fw.op(DVE, lambda e: e.tensor_copy(out=oT[:, half * 8:(half + 1) * 8, qs], in_=pbb(b)[:, :].rearrange("p (a t) -> p a t", a=8)), R=[rpb[b]], W=[roT])

            def layer1_tile_prompt(ti):
                load_x_tile((xp if os.environ.get('SKIP_L0') else x1s)[ti * 512:(ti + 1) * 512, :], 512)
                modulate_from_x(1, 0, 1, 0, 512)
                run_phase(lambda: nsa_prompt_tile(ti), dsems=[dkv])

                def rest():
                    dense_fm(nout_groups, oT, roT, 512, resid_evac(1, 2, 0, 512))
                    if stage == "pre1":
                        store_x_tile(y_p[ti * 512:(ti + 1) * 512, :], 512); return
                    layer_norm_T(1, 0, 1, 512, mod=(1, 3, 4, 0))
                    if stage == "mix1":
                        store_x_tile(y_p[ti * 512:(ti + 1) * 512, :], 512); return
                    ffn(1, 0, 512)
                    layer_norm_T(1, 2, 3, 512, mod=None)
                    store_x_tile(y_p[ti * 512:(ti + 1) * 512, :], 512)
                run_phase(rest)

            for ti in range(n_ptiles):
                layer1_tile_prompt(ti)
            def nsa_sample_tile():
                NT_ = 4
                qS = psb("qS", [128, 16, 4], BF16); rqS = Res("qS")
                gs4 = psb("gs4", [128, 48], F32); rgs4 = Res("gs4")
                o4 = psb("o4", [128, D], F32); ro4 = Res("o4")
                nkb = psb("nkb", [4, 3, 1024], BF16); rnkb = Res("nkb")
                stag = kvstag; rstag = [Res("sstag0"), Res("sstag1")]
                idxt = psb("idxt", [128, 128], I32); iot = psb("iot", [128, 1], I32); rpt = Res("ptab")
                cpage = psb("cpage", [128, 2, 1024], BF16); rcp = [Res("cp0"), Res("cp1")]
                dcp = [DSem(fw, "cp0", serial=False), DSem(fw, "cp1", serial=False)]
                kcS = psb("kcS", [128, 4, 256], BF16); vcS = psb("vcS", [128, 2, 512], BF16); rkcS = Res("kcS"); rvcS = Res("vcS")
                smS = psb("smS", [4, 4, 256], F32); rsmS = Res("smS")
                pcS = psb("pcS", [4, 4, 256], BF16); rpcS = Res("pcS")
                pTs = psb("pTs", [128, 8, 4], BF16); rpTs = Res("pTs")
                s4 = psb("s4", [4, 3, 4], F32); rs4 = Res("s4")
                scS = psb("scS", [4, 4, 264], F32); rscS = Res("scS")
                m8s = psb("m8s", [4, 2, 8], F32); rm8s = Res("m8s")
                scts = psb("scts", [4, 264], F32); rscts = Res("scts")
                sel01 = smS; rsel = rsmS
                sel2 = psb("sel2", [2, 2048], BF16); rsel2 = Res("sel2")
                Msb = psb("Msb", [128, 128, 16], BF16); rM = Res("Msb")
                KTp = psb("KTp", [128, 2, 512], BF16); rKTp = [Res("KTp0"), Res("KTp1")]
                Va = psb("Va", [128, 1, 4, 130], BF16); rVa = [Res("Va0"), Res("Va0b")]; rVa[1] = rVa[0]
                PT = psb("PT", [128, 2, 64], BF16); rPT = [Res("PT0"), Res("PT1")]
                accS = psb("accS", [16, 2, 4, 130], F32); racc = Res("accS")
                knT = psb("knT", [128, 2, 4, 4], BF16); rknT = Res("knT")
                vn = psb("vn", [4, 2, 4, 130], BF16); rvn = Res("vn")
                h2 = psb("h2", [2, 128], BF16); cn = psb("cn", [4, 16], BF16); wf = psb("wf", [128, 16], BF16); avgB = psb("avgB", [128, 254], BF16); rsc = Res("sconst")
                coef4 = psb("coef4", [4, 8], F32); rcoef4 = Res("coef4")
                dsm = DSem(fw, "smisc"); dsm2 = DSem(fw, "smisc2")
                cnt = {"stag": 0}
                for (dst_t, nm, rows, cols) in ((h2, "h2", 2, 128), (cn, "cn", 4, 16), (wf, "wf", 128, 16), (avgB, "avgB", 128, 254)):
                    fw.dma(SP, dio, iobuf[0:rows, 0:cols], ctab[nm][:, :], W=[rio])
                    fw.op(DVE, lambda e: e.tensor_copy(out=dst_t[:, :], in_=iobuf[0:rows, 0:cols]), R=[rio], W=[rsc])
                fw.dma(SP, dsm, idxt[:, :], ptab[0:1, :].partition_broadcast(128), W=[rpt])
                fw.dma(SP, dsm, iot[:, :], iota_in[:, :], W=[rpt])
                fw.op(DVE, lambda e: e.tensor_scalar(out=idxt[:, :], in0=idxt[:, :], scalar1=128, scalar2=iot[:, 0:1], op0=ALU.mult, op1=ALU.add), R=[rpt], W=[rpt])
                fw.op(DVE, lambda e: e.memset(Va[:, :, :, :], 1.0), W=[rVa[0]])
                fw.op(DVE, lambda e: e.memset(vn[:, :, :, :], 1.0), W=[rvn])
                fw.op(DVE, lambda e: e.memset(accS[:, :, :, :], 0.0), W=[racc])
                fw.dma(SP, dsm, win_s[0:508, :], winbuf[4:512, :])

                def q_evac(key, ci, ps, rps):
                    hd = key[2] * 2 + ci
                    fw.op(ACT, lambda e: e.activation(out=qS[:, hd, 0:4], in_=ps, func=AF.Copy, scale=128.0 ** -0.5), R=[rps], W=[rqS])

                def kv_evac(key, st, ps, rps, nt):
                    if key[0] == "ng":
                        fw.op(DVE, lambda e: e.tensor_tensor(out=gs4[0:4, :], in0=ps, in1=bg[0:4, :], op=ALU.add), R=[rps, rk1], W=[rgs4])
                        fw.op(ACT, lambda e: e.activation(out=gs4[0:4, :], in_=gs4[0:4, :], func=AF.Sigmoid), R=[rgs4], W=[rgs4])
                        return
                    gi = key[2]; br = gi // 4; sub = gi % 4
                    u = cnt["stag"] % 2; cnt["stag"] += 1
                    fw.op(ACT, lambda e: e.activation(out=stag[0:4, u, :], in_=ps, func=AF.Copy), R=[rps], W=[rstag[u]])
                    if br < 2:
                        fw.dma(SP, dkv, (cmp_s, slc_s)[br][0:4, sub * 256:(sub + 1) * 256], stag[0:4, u, :], R=[rstag[u]])
                    else:
                        fw.dma(SP, dkv, win_s[508:512, sub * 256:(sub + 1) * 256], stag[0:4, u, :], R=[rstag[u]])
                    fw.op(DVE, lambda e: e.tensor_copy(out=nkb[0:4, br, sub * 256:(sub + 1) * 256], in_=stag[0:4, u, :]), R=[rstag[u]], W=[rnkb])

                dense_fm(nq_groups, hT, rhT, 4, q_evac, nxt=nkv_groups[0])
                dense_tm(nkv_groups, hT, rhT, 4, kv_evac, nxt=nout_groups[0])

                for bi in (1, 2):
                    for g in range(4):
                        fw.op(PE, lambda e: e.transpose(out=pbb(6)[:, g * 4:g * 4 + 4], in_=nkb[0:4, bi, g * 128:(g + 1) * 128], identity=identb[0:4, 0:4]),
                              R=[rnkb, rid], W=[rpb[6]])
                    fw.op(ACT, lambda e: e.activation(out=knT[:, bi - 1, :, :], in_=pbb(6)[:, 0:16].rearrange("p (g t) -> p g t", g=4), func=AF.Copy), R=[rpb[6]], W=[rknT])
                    fw.op(DVE, lambda e: e.tensor_copy(out=vn[0:4, bi - 1, :, 0:128], in_=nkb[0:4, bi, 512:1024].rearrange("p (g d) -> p g d", g=4)), R=[rnkb], W=[rvn])

                def page_dma(cache, p, u):
                    fw._deps(POOL, [rpt], [rcp[u]])
                    ins = nc.gpsimd.indirect_dma_start(out=cpage[:, u, :], out_offset=None, in_=cache[:, :],
                                                       in_offset=bass.IndirectOffsetOnAxis(ap=idxt[:, p:p + 1], axis=0))
                    dcp[u].n += 16; ins.then_inc(dcp[u].sem, 16)
                    rcp[u].w = ('d', dcp[u], dcp[u].n); rcp[u].rd = {}
                    rpt.rd[id(dcp[u])] = rcp[u].w

                for p in range(128):
                    u = p % 2
                    page_dma(cache_cmp, p, u)
                    for g in range(4):
                        fw.op(PE, lambda e: e.matmul(pbf(g // 2)[:, (g % 2) * 256 + 2 * p:(g % 2) * 256 + 2 * p + 2], cpage[:, u, g * 128:(g + 1) * 128], avgA[:, 30:32],
                                                     start=True, stop=True), R=[rcp[u], rk1p], W=[rpb[g // 2]])
                    off = 126 - 2 * (p % 64)
                    fw.op(PE, lambda e: e.matmul(pbf(2 + p // 64)[:, :], avgB[:, off:off + 128], cpage[:, u, 512:1024], start=(p % 64 == 0), stop=(p % 64 == 63)),
                          R=[rcp[u], rsc], W=[rpb[2 + p // 64]])
                for b2 in range(2):
                    fw.op(ACT, lambda e: e.activation(out=kcS[:, 2 * b2:2 * b2 + 2, :], in_=pbf(b2)[:, :].rearrange("p (g j) -> p g j", g=2), func=AF.Copy), R=[rpb[b2]], W=[rkcS])
                    fw.op(DVE, lambda e: e.tensor_copy(out=vcS[:, b2, :], in_=pbf(2 + b2)[:, :]), R=[rpb[2 + b2]], W=[rvcS])
                fw.op(DVE, lambda e: e.memset(scS[:, :, :], 0.0), W=[rscS])
                for g in range(4):
                    for h in range(4):
                        fw.op(PE, lambda e: e.matmul(pbf(4 + h // 2)[0:4, (h % 2) * 256:(h % 2) * 256 + 256], qS[:, 4 * g + h, 0:4], kcS[:, g, :], start=True, stop=True),
                              R=[rqS, rkcS], W=[rpb[4 + h // 2]])
                    for b2 in range(2):
                        fw.op(DVE, lambda e: e.tensor_copy(out=smS[:, 2 * b2:2 * b2 + 2, :], in_=pbf(4 + b2)[0:4, :].rearrange("p (h j) -> p h j", h=2)), R=[rpb[4 + b2]], W=[rsmS])
                    fw.op(DVE, lambda e: e.tensor_reduce(out=s4[:, 0, :], in_=smS[:, :, :], axis=AX.X, op=ALU.max), R=[rsmS], W=[rs4])
                    fw.op(DVE, lambda e: e.tensor_tensor(out=smS[:, :, :], in0=smS[:, :, :], in1=s4[:, 0, :].unsqueeze(2).to_broadcast([4, 4, 256]), op=ALU.subtract), R=[rsmS, rs4], W=[rsmS])
                    fw.op(ACT, lambda e: e.activation(out=smS[:, :, :], in_=smS[:, :, :], func=AF.Exp), R=[rsmS], W=[rsmS])
                    fw.op(DVE, lambda e: e.tensor_reduce(out=s4[:, 1, :], in_=smS[:, :, :], axis=AX.X, op=ALU.add), R=[rsmS], W=[rs4])
                    fw.op(DVE, lambda e: e.reciprocal(out=s4[:, 2, :], in_=s4[:, 1, :]), R=[rs4], W=[rs4])
                    fw.op(DVE, lambda e: e.tensor_tensor(out=smS[:, :, :], in0=smS[:, :, :], in1=s4[:, 2, :].unsqueeze(2).to_broadcast([4, 4, 256]), op=ALU.mult), R=[rsmS, rs4], W=[rsmS])
                    fw.op(DVE, lambda e: e.tensor_reduce(out=scS[:, g, 0:256], in_=smS[:, :, :].rearrange("p h j -> p j h"), axis=AX.X, op=ALU.add), R=[rsmS], W=[rscS])
                    gc = gs4[0:4, :].rearrange("p (h b) -> p h b", b=3)[:, 4 * g:4 * g + 4, 0:1].to_broadcast([4, 4, 256])
                    fw.op(DVE, lambda e: e.tensor_tensor(out=pcS[:, :, :], in0=smS[:, :, :], in1=gc, op=ALU.mult), R=[rsmS, rgs4], W=[rpcS])
                    for h in range(4):
                        for c in range(2):
                            fw.op(PE, lambda e: e.transpose(out=pbb(6)[:, (h * 2 + c) * 4:(h * 2 + c) * 4 + 4], in_=pcS[0:4, h, c * 128:(c + 1) * 128], identity=identb[0:4, 0:4]),
                                  R=[rpcS, rid], W=[rpb[6]])
                    fw.op(ACT, lambda e: e.activation(out=pTs[:, :, :], in_=pbb(6)[:, 0:32].rearrange("p (a t) -> p a t", a=8), func=AF.Copy), R=[rpb[6]], W=[rpTs])
                    for h in range(4):
                        for c in range(2):
                            fw.op(PE, lambda e: e.matmul(pbf(7)[0:4, h * 128:(h + 1) * 128], pTs[:, h * 2 + c, :], vcS[:, c, g * 128:(g + 1) * 128], start=(c == 0), stop=(c == 1)),
                                  R=[rpTs, rvcS], W=[rpb[7]])
                    fw.op(ACT, lambda e: e.activation(out=o4[0:4, g * 512:(g + 1) * 512], in_=pbf(7)[0:4, :], func=AF.Copy), R=[rpb[7]], W=[ro4])
                for col in (0, 255, 256):
                    fw.op(DVE, lambda e: e.memset(scS[:, :, col:col + 1], 1e4), W=[rscS])
                for g in range(4):
                    fw.op(DVE, lambda e: e.max(out=m8s[:, 0, :], in_=scS[:, g, :]), R=[rscS], W=[rm8s])
                    fw.op(DVE, lambda e: e.match_replace(out=scts[:, :], in_to_replace=m8s[:, 0, :], in_values=scS[:, g, :], imm_value=-1e9), R=[rscS, rm8s], W=[rscts])
                    fw.op(DVE, lambda e: e.max(out=m8s[:, 1, :], in_=scts[:, :]), R=[rscts], W=[rm8s])
                    fw.op(DVE, lambda e: e.tensor_scalar(out=sel01[:, g, :], in0=scS[:, g, 0:256], scalar1=m8s[:, 1, 7:8], scalar2=None, op0=ALU.is_ge),
                          R=[rscS], W=[rsel], RS=[rm8s])
                rscr = Res('selscr')
                fw.dma(SP, dsm, selscr[:, :, :], sel01[:, :, :], R=[rsel], W=[rscr])
                with nc.allow_non_contiguous_dma(reason="4096-element selection-mask relayout (query-major -> page-parity-major)"):
                    s2v = sel2[:, :].rearrange("t (p g q) -> t p g q", p=128, g=4)
                    for q_ in range(4):
                        for g_ in range(4):
                            fw.dma(POOL, dsm2, s2v[:, :, g_, q_], selscr[q_, g_, :].rearrange("(p t) -> t p", t=2), W=[rsel2], R=[rscr])
                for c4 in range(4):
                    fw.op(PE, lambda e: e.matmul(pbf(c4)[:, :], h2[:, :], sel2[:, c4 * 512:(c4 + 1) * 512], start=True, stop=True), R=[rsc, rsel2], W=[rpb[c4]])
                    fw.op(ACT, lambda e: e.activation(out=Msb[:, c4 * 32:(c4 + 1) * 32, :], in_=pbf(c4)[:, :].rearrange("p (a b) -> p a b", b=16), func=AF.Copy), R=[rpb[c4]], W=[rM])

                def attend_tile(KT_ap, V_ap, nk, mask_ap, rdeps, bsel, it):
                    u = it % 2
                    sbk = u
                    for g in range(4):
                        fw.op(PE, lambda e: e.matmul(pbf(sbk)[0:nk, g * 16:(g + 1) * 16].rearrange("p (h q) -> p h q", h=4), KT_ap(g), qS[:, 4 * g:4 * g + 4, 0:4], start=True, stop=True),
                              R=list(rdeps) + [rqS], W=[rpb[sbk]])
                    fw.op(ACT, lambda e: e.activation(out=PT[0:nk, u, :], in_=pbf(sbk)[0:nk, 0:64], func=AF.Exp), R=[rpb[sbk]], W=[rPT[u]])
                    if mask_ap is not None:
                        fw.op(DVE, lambda e: e.tensor_tensor(out=PT[0:nk, u, :].rearrange("p (g h q) -> p g h q", g=4, h=4), in0=PT[0:nk, u, :].rearrange("p (g h q) -> p g h q", g=4, h=4),
                                                             in1=mask_ap, op=ALU.mult), R=[rPT[u], rM, rsc], W=[rPT[u]])
                    for g in range(4):
                        ob = 2 + g // 2
                        fw.op(PE, lambda e: e.matmul(pbf(ob)[0:16, (g % 2) * 130:(g % 2) * 130 + 130], PT[0:nk, u, g * 16:(g + 1) * 16], V_ap(g), start=True, stop=True),
                              R=list(rdeps) + [rPT[u]], W=[rpb[ob]])
                    for gp in range(2):
                        fw.op(DVE, lambda e: e.tensor_tensor(out=accS[:, bsel, 2 * gp:2 * gp + 2, :], in0=accS[:, bsel, 2 * gp:2 * gp + 2, :],
                                                             in1=pbf(2 + gp)[0:16, 0:260].rearrange("p (g d) -> p g d", g=2), op=ALU.add), R=[rpb[2 + gp], racc], W=[racc])

                def cached_tile(u, rsrcs, mask_ap, bsel, it):
                    for g in range(4):
                        fw.op(PE, lambda e: e.transpose(out=pbb(6)[:, g * 128:(g + 1) * 128], in_=cpage[:, u, g * 128:(g + 1) * 128], identity=identb[:, :]),
                              R=[rcp[u], rid], W=[rpb[6]])
                    fw.op(ACT, lambda e: e.activation(out=KTp[:, u, :], in_=pbb(6)[:, 0:512], func=AF.Copy), R=[rpb[6]], W=[rKTp[u]])
                    fw.op(DVE, lambda e: e.tensor_copy(out=Va[:, 0, :, 0:128], in_=cpage[:, u, 512:1024].rearrange("p (g d) -> p g d", g=4)), R=[rcp[u]], W=[rVa[u]])
                    attend_tile(lambda g: KTp[:, u, g * 128:(g + 1) * 128], lambda g: Va[:, 0, g, :], 128, mask_ap, [rKTp[u], rVa[u]], bsel, it)

                it = 0
                for p in range(128):
                    u = p % 2
                    page_dma(cache_slc, p, u)
                    m_ap = Msb[:, p, :].rearrange("p (g q) -> p g q", g=4).unsqueeze(2).to_broadcast([128, 4, 4, 4])
                    cached_tile(u, None, m_ap, 0, it); it += 1
                cn_ap = cn[0:4, :].rearrange("p (h q) -> p h q", h=4).unsqueeze(1).to_broadcast([4, 4, 4, 4])
                attend_tile(lambda g: knT[:, 0, g, :], lambda g: vn[0:4, 0, g, :], 4, cn_ap, [rknT, rvn], 0, it); it += 1
                for t4 in range(4):
                    u = t4 % 2
                    fw.dma(POOL, dcp[u], cpage[:, u, :], winbuf[t4 * 128:(t4 + 1) * 128, :], W=[rcp[u]])
                    m_ap = wf[:, :].rearrange("p (h q) -> p h q", h=4).unsqueeze(1).to_broadcast([128, 4, 4, 4]) if t4 == 0 else None
                    cached_tile(u, None, m_ap, 1, it); it += 1
                attend_tile(lambda g: knT[:, 1, g, :], lambda g: vn[0:4, 1, g, :], 4, cn_ap, [rknT, rvn], 1, it); it += 1

                for bsel in range(2):
                    for h in range(4):
                        for gp in range(2):
                            b = 4 + gp
                            fw.op(PE, lambda e: e.matmul(pbf(b)[0:4, 0:260], identf[0:16, h * 4:(h + 1) * 4], accS[:, bsel, 2 * gp:2 * gp + 2, :].rearrange("p g d -> p (g d)"),
                                                         start=True, stop=True), R=[racc, rid], W=[rpb[b]])
                            for gg in range(2):
                                hd = 4 * (2 * gp + gg) + h; c0 = gg * 130
                                fw.op(DVE, lambda e: e.reciprocal(out=coef4[:, 0:1], in_=pbf(b)[0:4, c0 + 128:c0 + 129]), R=[rpb[b]], W=[rcoef4])
                                fw.op(DVE, lambda e: e.tensor_tensor(out=coef4[:, 1:2], in0=coef4[:, 0:1], in1=gs4[0:4, hd * 3 + 1 + bsel:hd * 3 + 2 + bsel], op=ALU.mult),
                                      R=[rcoef4, rgs4], W=[rcoef4])
                                fw.op(DVE, lambda e: e.scalar_tensor_tensor(out=o4[0:4, hd * 128:(hd + 1) * 128], in0=pbf(b)[0:4, c0:c0 + 128], scalar=coef4[:, 1:2],
                                                                            in1=o4[0:4, hd * 128:(hd + 1) * 128], op0=ALU.mult, op1=ALU.add), R=[rpb[b], ro4], W=[ro4], RS=[rcoef4])
                ob4 = cpage[:, :, :].rearrange("p a b -> p (a b)")
                fw.op(ACT, lambda e: e.activation(out=ob4[0:4, :], in_=o4[0:4, :], func=AF.Copy), R=[ro4], W=rcp)
                for c in range(16):
                    fw.op(PE, lambda e: e.transpose(out=pbb(6)[:, c * 4:c * 4 + 4], in_=ob4[0:4, c * 128:(c + 1) * 128], identity=identb[0:4, 0:4]), R=rcp + [rid], W=[rpb[6]])
                fw.op(DVE, lambda e: e.tensor_copy(out=oT[:, :, 0:4], in_=pbb(6)[:, 0:64].rearrange("p (a t) -> p a t", a=16)), R=[rpb[6]], W=[roT])
                return [dsm, dsm2, dcp[0], dcp[1]]

            if do_sample:
                load_x_tile((xs[0:4, :] if os.environ.get('SKIP_L0') else x1s[T:T + 4, :]), 4)
                modulate_from_x(1, 0, 1, 1, 4)
                extra = {}
                def phase_s():
                    extra["ds"] = nsa_sample_tile()
                with contextlib.ExitStack() as pes:
                    phase["es"] = pes
                    phase_s()
                    barrier([dkv] + extra["ds"])
                phase["es"] = None

                def rest_s():
                    dense_fm(nout_groups, oT, roT, 4, resid_evac(1, 2, 1, 4))
                    layer_norm_T(1, 0, 1, 4, mod=(1, 3, 4, 1))
                    ffn(1, 1, 4)
                    layer_norm_T(1, 2, 3, 4, mod=None)
                    store_x_tile(y_s[0:4, :], 4)
                run_phase(rest_s)
            barrier([dio, dkv])
            for ds in (dkv,):
                if ds.n:
                    SP.obj.wait_ge(ds.sem, ds.n)
        dump("modT", modT[:, :, :, :], [rmod]); dump("vT_ln", vT_ln[:, :], [rvt]); dump("vT_g", vT_g[:, :], [rvt])

        for ds in (dio, dst, dbgd["sem"]):
            if ds is not None and ds.n:
                SP.obj.wait_ge(ds.sem, ds.n)
        print(f"[build] ops={fw.nops} waits={fw.nwaits} sbuf_left={nc.sbuf_bytes_remaining}")
    return nc


def make_in_map(inputs, core, consts):
    b = core % 4; s = core
    f = lambda a: np.ascontiguousarray(a, dtype=np.float32)
    m = {}
    m["xp"] = f(inputs["x_prompt"][b]); m["xs"] = f(inputs["x_sample"][s])
    m["cvec"] = f(np.concatenate([inputs["c_prompt"][b].reshape(16, 128), inputs["c_sample"][s].reshape(16, 128)], 0))
    rows = []
    for l in range(2):
        for nm in ("ln_mix_g", "ln_mix_b", "ln_ffn_g", "ln_ffn_b"):
            rows.append(inputs[nm][l].reshape(16, 128))
    m["vtab_ln"] = f(np.concatenate(rows, 0))
    m["vtab_ada"] = f(inputs["ada_b"].reshape(2, 96, 128))
    m["vtab_g"] = f(inputs["gla_norm_g"][0].reshape(16, 128))
    m["ada_w"] = f(inputs["ada_w"])
    m["gla_w_in"] = f(inputs["gla_w_in"][0]); m["gla_w_out"] = f(inputs["gla_w_out"][0])
    wa = np.zeros((33, 1024), np.float32); wa[0:16] = inputs["gla_w_alpha"][0]; wa[32] = inputs["gla_b_alpha"][0]
    m["gla_wa"] = wa
    m["ffn_w_in"] = f(inputs["ffn_w_in"]); m["ffn_w_out"] = f(inputs["ffn_w_out"])
    m["state_in"] = f(inputs["state_gla"][s, 0])
    m["nsa_w_in"] = f(inputs["nsa_w_in"][0]); m["nsa_w_out"] = f(inputs["nsa_w_out"][0])
    m["nsa_bg"] = f(inputs["nsa_b_gate"][0].reshape(1, 48))
    m["cache_cmp"] = inputs["cache_cmp_kv"].reshape(1280 * 128, 1024)
    m["cache_slc"] = inputs["cache_slc_kv"].reshape(1280 * 128, 1024)
    m["winbuf"] = f(inputs["cache_win_kv"][s, 0].reshape(512, 1024))
    m["ptab"] = np.ascontiguousarray(inputs["page_table"][s].reshape(1, 128), dtype=np.int32)
    m["iota_in"] = np.arange(128, dtype=np.int32).reshape(128, 1)
    for k, v in consts.items():
        m["c_" + k] = v
    return m


_PROGRAM = {}


def kernel(**inputs):
    inputs = {k: np.asarray(v) for k, v in inputs.items()}
    if "nc" not in _PROGRAM:
        _PROGRAM["nc"] = build_program(n_ptiles=4, do_sample=True, stage="full", dbg=False)
    nc = _PROGRAM["nc"]
    consts = _const_tables()
    n = 8
    in_maps = [make_in_map(inputs, c, consts) for c in range(n)]
    res = run_bass_kernel_spmd(nc, in_maps, core_ids=list(range(n)))
    r = res.results
    f32 = np.float32
    y_prompt = np.stack([np.asarray(r[b]["y_p"], f32) for b in range(4)], 0)
    y_sample = np.stack([np.asarray(r[s]["y_s"], f32) for s in range(8)], 0)
    gla_p = np.stack([np.asarray(r[b]["st_p"], f32) for b in range(4)], 0)[:, None]
    gla_s = np.stack([np.asarray(r[s]["st_s"], f32) for s in range(8)], 0)[:, None]
    kv = lambda name, idx, rows: np.stack([np.asarray(r[i][name], f32).reshape(rows, 2, 4, 128) for i in idx], 0)[:, None]
    cmp_p = kv("cmp_p", range(4), 2048); cmp_s = kv("cmp_s", range(8), 4)
    slc_p = kv("slc_p", range(4), 2048); slc_s = kv("slc_s", range(8), 4)
    win_p = kv("win_p", range(4), 512); win_s = kv("win_s", range(8), 512)
    return (y_prompt, y_sample, gla_p, gla_s, cmp_p, cmp_s, slc_p, slc_s, win_p, win_s)
```

```python
import contextlib
import numpy as np
import concourse.bass as bass
import concourse.mybir as mybir
from concourse.bass_utils import run_bass_kernel_spmd

F32 = mybir.dt.float32; BF16 = mybir.dt.bfloat16; I32 = mybir.dt.int32
AF = mybir.ActivationFunctionType
ALU = mybir.AluOpType
AX = mybir.AxisListType

D = 2048; NCH = 16; T = 2048; DFF = 5632
GLA_IN_W = 6160; NSA_IN_W = 5168
ALPHA = 4.0 ** 0.25
EPS = 1e-5
EPS_LN = EPS / (ALPHA * ALPHA)
NEG = -30000.0
WSLOT_ELEMS = 4096
NSLOT = 4
import os
SYNC_RAW_DIST = int(os.environ.get('SYNC_RAW_DIST', '1'))
SYNC_WAW = int(os.environ.get('SYNC_WAW', '1'))
SYNC_PE = int(os.environ.get('SYNC_PE', '0'))


class Eng:
    def __init__(s, fw, name, obj):
        s.name = name; s.obj = obj; s.n = 0; s.known = {}
        s.sem = fw.es.enter_context(fw.nc.semaphore("pg_" + name))


class DSem:
    def __init__(s, fw, name, serial=True):
        s.sem = fw.es.enter_context(fw.nc.semaphore("ds_" + name)); s.n = 0; s.serial = serial


class Res:
    __slots__ = ("name", "w", "rd")

    def __init__(s, name=""):
        s.name = name; s.w = None; s.rd = {}


class FW:
    def __init__(s, nc, es):
        s.nc = nc; s.es = es
        s.PE = Eng(s, "pe", nc.tensor); s.ACT = Eng(s, "act", nc.scalar)
        s.DVE = Eng(s, "dve", nc.vector); s.POOL = Eng(s, "pool", nc.gpsimd)
        s.SP = Eng(s, "sp", nc.sync)
        s.nops = 0; s.nwaits = 0

    def sb(s, name, shape, dt):
        return s.es.enter_context(s.nc.sbuf_tensor(name, list(shape), dt))

    def ps(s, name, shape, dt=F32):
        return s.es.enter_context(s.nc.psum_tensor(name, list(shape), dt))

    def _wait(s, E, tok, force=False):
        kind, k, v = tok
        if kind == 'e' and k is E and not force:
            return
        kk = id(k)
        if E.known.get(kk, 0) >= v:
            return
        E.obj.wait_ge(k.sem, v); E.known[kk] = v; s.nwaits += 1

    def _deps(s, E, R, W):
        for r in R:
            if r.w is not None:
                near = (r.w[0] == 'e' and r.w[1] is E and E.name != "pe" and (E.n - r.w[2]) <= SYNC_RAW_DIST)
                s._wait(E, r.w, force=near)
        for w in W:
            if w.w is not None:
                near = (SYNC_WAW and w.w[0] == 'e' and w.w[1] is E and (E.name != "pe" or SYNC_PE) and (E.n - w.w[2]) <= 2)
                s._wait(E, w.w, force=near)
            for t in w.rd.values():
                near = (SYNC_WAW and t[0] == 'e' and t[1] is E and (E.name != "pe" or SYNC_PE) and (E.n - t[2]) <= 2)
                s._wait(E, t, force=near)

    def op(s, E, fn, R=(), W=(), RS=()):
        s._deps(E, R, W)
        for r in RS:
            if r.w is not None:
                s._wait(E, r.w, force=True)
        ins = fn(E.obj)
        E.n += 1; ins.then_inc(E.sem, 1); s.nops += 1
        tok = ('e', E, E.n)
        for r in R:
            r.rd[id(E)] = tok
        for r in RS:
            r.rd[id(E)] = tok
        for w in W:
            w.w = tok; w.rd = {}
        return ins

    def dma(s, Q, dsem, out, in_, R=(), W=()):
        s._deps(Q, R, W)
        if dsem.serial and dsem.n > 0:
            s._wait(Q, ('d', dsem, dsem.n))
        ins = Q.obj.dma_start(out=out, in_=in_)
        dsem.n += 16; ins.then_inc(dsem.sem, 16)
        tok = ('d', dsem, dsem.n)
        for r in R:
            r.rd[id(dsem)] = tok
        for w in W:
            w.w = tok; w.rd = {}
        return ins

    def finish(s, E, resources):
        for r in resources:
            if r.w is not None:
                s._wait(E, r.w)
            for t in r.rd.values():
                s._wait(E, t)


def _const_tables():
    j = np.arange(128)[:, None]; i = np.arange(128)[None, :]
    c = {}
    c["tri"] = np.where(j <= i, -1.0 / 16.0, 0.0).astype(np.float32)
    c["caus01"] = (j <= i).astype(np.float32)
    c["causb"] = np.tile(np.where(j <= i, 0.0, NEG), (1, 4)).astype(np.float32)
    c["winb"] = np.tile(np.where(j > i, 0.0, NEG), (1, 4)).astype(np.float32)
    key = np.arange(2048)[None, :]; blk = np.arange(32)[:, None]
    c["eall"] = (key // 64 == blk).astype(np.float32)
    kk = np.arange(128)[:, None]; cc = np.arange(62)[None, :]
    c["avgA"] = np.where(cc == 30 + (kk >= 64), 1.0 / 64.0, 0.0).astype(np.float32)
    ii = np.arange(128)[:, None]; m = np.arange(62)[None, :]
    fl = (ii + 1) // 64
    ok = (m - 30) <= fl - 1
    c["Tc"] = np.where(ok, 0.0, NEG).astype(np.float32)
    c["Tm"] = ok.astype(np.float32)
    d = (m - 30) - (ii >= 64)
    c["Tf"] = np.where(d > 0, -1e4, np.where(d >= -1, 1e4, 0.0)).astype(np.float32)
    k1 = np.arange(128)[None, :]
    c["h2"] = np.stack([(k1[0] < 64), (k1[0] >= 64)]).astype(np.float32)
    jq = np.arange(4)[:, None]; qq = np.tile(np.arange(4), 4)[None, :]
    c["cn"] = (jq <= qq).astype(np.float32)
    rr = np.arange(128)[:, None]
    c["wf"] = (rr > qq).astype(np.float32)
    cb = np.arange(254)[None, :]
    c["avgB"] = np.where(cb == 126 + (kk >= 64), 1.0 / 64.0, 0.0).astype(np.float32)
    return c


def build_program(n_ptiles=4, do_sample=True, stage="full", dbg=False):
    nc = bass.Bass("TRN2", target_bir_lowering=False)
    dram = {}

    def din(name, shape, dt=F32):
        dram[name] = nc.dram_tensor(name, list(shape), dt, kind="ExternalInput").ap()
        return dram[name]

    def dout(name, shape, dt=F32):
        dram[name] = nc.dram_tensor(name, list(shape), dt, kind="ExternalOutput").ap()
        return dram[name]

    xp = din("xp", [T, D]); xs = din("xs", [4, D])
    cvec = din("cvec", [32, 128])
    vtab_ln = din("vtab_ln", [128, 128])
    vtab_ada = din("vtab_ada", [2, 96, 128])
    vtab_g = din("vtab_g", [16, 128])
    ada_w = din("ada_w", [2, D, 6 * D])
    gla_w_in = din("gla_w_in", [D, GLA_IN_W]); gla_w_out = din("gla_w_out", [D, D])
    gla_wa = din("gla_wa", [33, 1024])
    ffn_w_in = din("ffn_w_in", [2, D, 2 * DFF]); ffn_w_out = din("ffn_w_out", [2, DFF, D])
    state_in = din("state_in", [4, 256, 512])
    nsa_w_in = din("nsa_w_in", [D, NSA_IN_W]); nsa_w_out = din("nsa_w_out", [D, D])
    nsa_bg = din("nsa_bg", [1, 48])
    cache_cmp = din("cache_cmp", [1280 * 128, 1024]); cache_slc = din("cache_slc", [1280 * 128, 1024])
    winbuf = din("winbuf", [512, 1024]); ptab = din("ptab", [1, 128], I32)
    selscr = nc.dram_tensor("selscr", [4, 4, 256], F32).ap()
    iota_in = din("iota_in", [128, 1], I32)
    ctab = {k: din("c_" + k, list(v.shape)) for k, v in _const_tables().items()}

    y_p = dout("y_p", [T, D]); y_s = dout("y_s", [4, D])
    st_p = dout("st_p", [4, 256, 512]); st_s = dout("st_s", [4, 256, 512])
    cmp_p = dout("cmp_p", [T, 1024]); slc_p = dout("slc_p", [T, 1024]); win_p = dout("win_p", [512, 1024])
    cmp_s = dout("cmp_s", [4, 1024]); slc_s = dout("slc_s", [4, 1024]); win_s = dout("win_s", [512, 1024])

    es = contextlib.ExitStack()
    with es:
        fw = FW(nc, es)
        PE, ACT, DVE, POOL, SP = fw.PE, fw.ACT, fw.DVE, fw.POOL, fw.SP
        sb = fw.sb
        phase = {"es": None}
        dbgd = {"sem": None, "done": set()}

        def dump(name, ap, R=()):
            if not dbg or name in dbgd["done"]:
                return
            dbgd["done"].add(name)
            if dbgd["sem"] is None:
                dbgd["sem"] = DSem(fw, "dbg")
            shp = list(ap.shape)
            o = nc.dram_tensor("dbg_" + name, shp, ap.dtype, kind="ExternalOutput").ap()
            fw.dma(SP, dbgd["sem"], o, ap, R=list(R))
            for E in (PE, ACT, DVE):
                fw._wait(E, ('d', dbgd["sem"], dbgd["sem"].n))

        def psb(name, shape, dt):
            phase["n"] = phase.get("n", 0) + 1
            return phase["es"].enter_context(nc.sbuf_tensor(f"{name}_{phase['n']}", list(shape), dt))

        def barrier(dsems=()):
            engs = (PE, ACT, DVE)
            snap = [(E, E.n) for E in engs]
            for E in engs:
                for (O, n) in snap:
                    if O is not E and n > 0:
                        fw._wait(E, ('e', O, n))
                for ds in dsems:
                    if ds.n:
                        fw._wait(E, ('d', ds, ds.n))

        pb = [fw.ps(f"pb{i}", [128, 512], F32) for i in range(8)]
        rpb = [Res(f"pb{i}") for i in range(8)]

        def pbf(i):
            return pb[i]

        def pbb(i):
            return pb[i][:].bitcast(BF16)

        identf = sb("identf", [128, 128], F32); identb = sb("identb", [128, 128], BF16); rid = Res("ident")
        onesb = sb("onesb", [128, 128], BF16); rones = Res("ones")
        onesf = sb("onesf", [128, 128], F32)
        tri = sb("tri", [128, 128], F32); caus01 = sb("caus01", [128, 128], F32); rmask = Res("masks")
        dconst = DSem(fw, "const")
        fw.op(POOL, lambda e: e.memset(identf[:], 1.0), W=[rid])
        fw.op(POOL, lambda e: e.affine_select(out=identf[:], in_=identf[:], pattern=[[-1, 128]], compare_op=ALU.is_equal,
                                              fill=0.0, base=0, channel_multiplier=1), R=[rid], W=[rid])
        fw.op(DVE, lambda e: e.tensor_copy(out=identb[:], in_=identf[:]), R=[rid], W=[rid])
        fw.op(DVE, lambda e: e.memset(onesb[:], 1.0), W=[rones])
        fw.op(DVE, lambda e: e.memset(onesf[:], 1.0), W=[rones])
        fw.dma(SP, dconst, tri[:], ctab["tri"][:, :], W=[rmask])
        fw.dma(SP, dconst, caus01[:], ctab["caus01"][:, :], W=[rmask])

        vT_ln = sb("vT_ln", [128, 128], F32); vT_g = sb("vT_g", [128, 16], F32); rvt = Res("vt")
        adab = sb("adab", [128, 2, 96], F32)
        cT = sb("cT", [128, 32], F32); scT = sb("scT", [128, 16, 2], BF16); rct = Res("cT")
        ld0 = sb("ld0", [128, 128], F32); ld1 = sb("ld1", [128, 128], F32); rld0 = Res(); rld1 = Res()
        dld = DSem(fw, "ld")

        def load_T(dst_ap, src_ap, rows, rdst, stg, rstg, bank):
            fw.dma(SP, dld, stg[0:rows, :], src_ap, W=[rstg])
            fw.op(PE, lambda e: e.transpose(out=pbf(bank)[:, 0:rows], in_=stg[0:rows, :], identity=identf[0:rows, 0:rows]),
                  R=[rstg, rid], W=[rpb[bank]])
            fw.op(DVE, lambda e: e.tensor_copy(out=dst_ap, in_=pbf(bank)[:, 0:rows]), R=[rpb[bank]], W=[rdst])

        load_T(vT_ln[:, :], vtab_ln[:, :], 128, rvt, ld0, rld0, 0)
        load_T(vT_g[:, :], vtab_g[:, :], 16, rvt, ld1, rld1, 1)
        load_T(adab[:, 0, :], vtab_ada[0, :, :], 96, rvt, ld0, rld0, 0)
        load_T(adab[:, 1, :], vtab_ada[1, :, :], 96, rvt, ld1, rld1, 1)
        load_T(cT[:, :], cvec[:, :], 32, rct, ld0, rld0, 0)
        fw.op(ACT, lambda e: e.activation(out=scT[:].rearrange("p c v -> p v c"), in_=cT[:].rearrange("p (v c) -> p v c", v=2),
                                          func=AF.Silu), R=[rct], W=[rct])

        wsl = [sb(f"wsl{i}", [128, WSLOT_ELEMS], BF16) for i in range(NSLOT)]
        rws = [Res(f"wsl{i}") for i in range(NSLOT)]
        dws = [DSem(fw, f"w{i}", serial=False) for i in range(NSLOT)]
        wstate = {"next": 0, "issued": {}}

        def wissue(g):
            if g is None or g[0] in wstate["issued"]:
                return
            key, kch, nct, parts = g
            assert kch * nct <= WSLOT_ELEMS
            si = wstate["next"]; wstate["next"] = (si + 1) % NSLOT
            view = wsl[si][:, 0:kch * nct].rearrange("p (k c) -> p k c", k=kch)
            for (ap, co) in parts:
                ncols = ap.shape[1]
                fw.dma(POOL, dws[si], view[:, :, co:co + ncols], ap.rearrange("(k p) c -> p k c", p=128), W=[rws[si]])
            wstate["issued"][key] = si

        def wget(g):
            wissue(g)
            si = wstate["issued"].pop(g[0])
            key, kch, nct, parts = g
            return si, wsl[si][:, 0:kch * nct].rearrange("p (k c) -> p k c", k=kch)

        def wgroup(key, W2d, col0, ncols, kch=NCH, row0=0):
            return (key, kch, ncols, [(W2d[row0:row0 + kch * 128, col0:col0 + ncols], 0)])

        dense_banks = [0, 1, 2, 3]
        dstate = {"i": 0}

        def next_bank():
            b = dense_banks[dstate["i"] % len(dense_banks)]; dstate["i"] += 1
            return b

        def dense_fm(groups, inT, rin, ntok, evac, nxt=None):
            nxl = list(nxt) if isinstance(nxt, list) else ([nxt] if nxt is not None else [])
            seq = list(groups) + nxl
            for gi, g in enumerate(groups):
                si, wv = wget(g)
                for a in range(1, NSLOT):
                    if gi + a < len(seq):
                        wissue(seq[gi + a])
                key, kch, nct, parts = g
                for c0 in range(0, nct, 128):
                    M = min(128, nct - c0)
                    b = next_bank()
                    for k in range(kch):
                        fw.op(PE, lambda e: e.matmul(pbf(b)[0:M, 0:ntok], wv[:, k, c0:c0 + M], inT[:, k, 0:ntok],
                                                     start=(k == 0), stop=(k == kch - 1)),
                              R=[rws[si], rin], W=[rpb[b]])
                    evac(key, c0 // 128, pbf(b)[0:M, 0:ntok], rpb[b])

        def dense_tm(groups, inT, rin, ntok, evac, nxt=None):
            nst = (ntok + 127) // 128
            nxl = list(nxt) if isinstance(nxt, list) else ([nxt] if nxt is not None else [])
            seq = list(groups) + nxl
            for gi, g in enumerate(groups):
                si, wv = wget(g)
                for a in range(1, NSLOT):
                    if gi + a < len(seq):
                        wissue(seq[gi + a])
                key, kch, nct, parts = g
                for st in range(nst):
                    nt = min(128, ntok - st * 128)
                    b = next_bank()
                    for k in range(kch):
                        fw.op(PE, lambda e: e.matmul(pbf(b)[0:nt, 0:nct], inT[:, k, st * 128:st * 128 + nt], wv[:, k, 0:nct],
                                                     start=(k == 0), stop=(k == kch - 1)),
                              R=[rws[si], rin], W=[rpb[b]])
                    evac(key, st, pbf(b)[0:nt, 0:nct], rpb[b], nt)

        modT = sb("modT", [128, 2, 96, 2], F32); rmod = Res("mod")

        def ada_evac(l):
            def f(key, ci, ps, rps):
                c = key[2] * 2 + ci
                fw.op(DVE, lambda e: e.tensor_scalar(out=modT[:, l, c, :], in0=ps, scalar1=adab[:, l, c:c + 1], scalar2=None,
                                                     op0=ALU.add), R=[rps, rvt], W=[rmod])
            return f

        def ada_groups(l):
            return [wgroup(("ada", l, gi), ada_w[l], gi * 256, 256) for gi in range(48)]

        def ada_post(l):
            for v in (1, 4):
                fw.op(DVE, lambda e: e.tensor_scalar(out=modT[:, l, v * 16:(v + 1) * 16, :], in0=modT[:, l, v * 16:(v + 1) * 16, :],
                                                     scalar1=1.0, scalar2=None, op0=ALU.add), R=[rmod], W=[rmod])
            for v in (2, 5):
                fw.op(DVE, lambda e: e.tensor_scalar(out=modT[:, l, v * 16:(v + 1) * 16, :], in0=modT[:, l, v * 16:(v + 1) * 16, :],
                                                     scalar1=1.0 / ALPHA, scalar2=None, op0=ALU.mult), R=[rmod], W=[rmod])

        xT = sb("xT", [128, NCH, 512], F32); rxT = Res("xT")
        hT = sb("hT", [128, NCH, 512], BF16); rhT = Res("hT")
        oT = hT; roT = rhT
        iobuf = sb("iobuf", [128, D], F32); rio = Res("iobuf"); dio = DSem(fw, "io")
        kvstag = sb("kvstag", [128, 2, 256], F32)
        epsln = sb("epsln", [128, 1], F32); epsb = sb("epsb", [128, 1], F32)
        fw.op(DVE, lambda e: e.memset(epsln[:], EPS_LN), W=[rones])
        fw.op(DVE, lambda e: e.memset(epsb[:], EPS), W=[rones])

        def load_x_tile(src, ntok):
            nst = (ntok + 127) // 128
            for st in range(nst):
                nt = min(128, ntok - st * 128)
                fw.dma(SP, dio, iobuf[0:nt, :], src[st * 128:st * 128 + nt, :], W=[rio])
                for c4 in range(4):
                    b = next_bank()
                    for cc in range(4):
                        c = c4 * 4 + cc
                        fw.op(PE, lambda e: e.transpose(out=pbf(b)[:, cc * 128:cc * 128 + nt], in_=iobuf[0:nt, c * 128:(c + 1) * 128],
                                                        identity=identf[0:nt, 0:nt]), R=[rio, rid], W=[rpb[b]])
                    src_v = pbf(b)[:, :].rearrange("p (a t) -> p a t", a=4)[:, :, 0:nt]
                    dst_v = xT[:, c4 * 4:(c4 + 1) * 4, st * 128:st * 128 + nt]
                    if c4 % 2 == 0:
                        fw.op(ACT, lambda e: e.activation(out=dst_v, in_=src_v, func=AF.Copy), R=[rpb[b]], W=[rxT])
                    else:
                        fw.op(DVE, lambda e: e.tensor_copy(out=dst_v, in_=src_v), R=[rpb[b]], W=[rxT])

        def store_x_tile(dst, ntok):
            nst = (ntok + 127) // 128
            for st in range(nst):
                nt = min(128, ntok - st * 128)
                for c4 in range(4):
                    b = next_bank()
                    for cc in range(4):
                        c = c4 * 4 + cc
                        fw.op(PE, lambda e: e.transpose(out=pbf(b)[0:nt, cc * 128:(cc + 1) * 128], in_=xT[:, c, st * 128:st * 128 + nt], identity=identf[:, :]),
                              R=[rxT, rid], W=[rpb[b]])
                    if c4 % 2 == 0:
                        fw.op(ACT, lambda e: e.activation(out=iobuf[0:nt, c4 * 512:(c4 + 1) * 512], in_=pbf(b)[0:nt, :], func=AF.Copy), R=[rpb[b]], W=[rio])
                    else:
                        fw.op(DVE, lambda e: e.tensor_copy(out=iobuf[0:nt, c4 * 512:(c4 + 1) * 512], in_=pbf(b)[0:nt, :]), R=[rpb[b]], W=[rio])
                fw.dma(SP, dio, dst[st * 128:st * 128 + nt, :], iobuf[0:nt, :], R=[rio])

        def modulate_from_x(l, vsh, vsc, grp, ntok):
            for c in range(NCH):
                fw.op(ACT, lambda e: e.activation(out=hT[:, c, 0:ntok], in_=xT[:, c, 0:ntok], func=AF.Identity,
                                                  scale=modT[:, l, vsc * 16 + c, grp:grp + 1], bias=modT[:, l, vsh * 16 + c, grp:grp + 1]),
                      R=[rxT, rmod], W=[rhT])

        def layer_norm_T(l_ln, vg, vb, ntok, mod=None):
            if phase.get("ln_owner") is not phase["es"]:
                phase["ln_owner"] = phase["es"]
                phase["ln"] = (psb("zt", [128, 2, 512], F32), [Res("zt0"), Res("zt1")], psb("lnst", [128, 3, 512], F32), Res("lnst"),
                               psb("ub", [128, 2, 2, 512], BF16), [Res("ub0"), Res("ub1")])
            zt, rzt, lnst, rlnst, ub, rub = phase["ln"]
            b1, b2 = 4, 5
            for c in range(NCH):
                u = c % 2
                fw.op(DVE, lambda e: e.tensor_copy(out=ub[:, u, 0, 0:ntok], in_=xT[:, c, 0:ntok]), R=[rxT], W=[rub[u]])
                fw.op(ACT, lambda e: e.activation(out=ub[:, u, 1, 0:ntok], in_=xT[:, c, 0:ntok], func=AF.Square), R=[rxT], W=[rub[u]])
                fw.op(PE, lambda e: e.matmul(pbf(b1)[:, 0:ntok], onesb[:, :], ub[:, u, 0, 0:ntok], start=(c == 0), stop=(c == NCH - 1)),
                      R=[rub[u], rones], W=[rpb[b1]])
                fw.op(PE, lambda e: e.matmul(pbf(b2)[:, 0:ntok], onesb[:, :], ub[:, u, 1, 0:ntok], start=(c == 0), stop=(c == NCH - 1)),
                      R=[rub[u], rones], W=[rpb[b2]])
            mean = lnst[:, 0, 0:ntok]; var = lnst[:, 1, 0:ntok]; rstd = lnst[:, 2, 0:ntok]
            fw.op(DVE, lambda e: e.tensor_scalar(out=mean, in0=pbf(b1)[:, 0:ntok], scalar1=1.0 / D, scalar2=None, op0=ALU.mult),
                  R=[rpb[b1]], W=[rlnst])
            fw.op(DVE, lambda e: e.tensor_tensor(out=var, in0=mean, in1=mean, op=ALU.mult), R=[rlnst], W=[rlnst])
            fw.op(DVE, lambda e: e.scalar_tensor_tensor(out=var, in0=pbf(b2)[:, 0:ntok], scalar=1.0 / D, in1=var, op0=ALU.mult, op1=ALU.subtract),
                  R=[rpb[b2], rlnst], W=[rlnst])
            fw.op(ACT, lambda e: e.activation(out=var, in_=var, func=AF.Sqrt, bias=epsln[:, 0:1]), R=[rlnst, rones], W=[rlnst])
            fw.op(DVE, lambda e: e.reciprocal(out=rstd, in_=var), R=[rlnst], W=[rlnst])
            for c in range(NCH):
                u = c % 2
                z = zt[:, u, 0:ntok]
                fw.op(DVE, lambda e: e.tensor_tensor(out=z, in0=xT[:, c, 0:ntok], in1=mean, op=ALU.subtract), R=[rxT, rlnst], W=[rzt[u]])
                fw.op(DVE, lambda e: e.tensor_tensor(out=z, in0=z, in1=rstd, op=ALU.mult), R=[rlnst, rzt[u]], W=[rzt[u]])
                gcol = vT_ln[:, (l_ln * 4 + vg) * 16 + c:(l_ln * 4 + vg) * 16 + c + 1]
                bcol = vT_ln[:, (l_ln * 4 + vb) * 16 + c:(l_ln * 4 + vb) * 16 + c + 1]
                fw.op(ACT, lambda e: e.activation(out=xT[:, c, 0:ntok], in_=z, func=AF.Identity, scale=gcol, bias=bcol),
                      R=[rzt[u], rvt], W=[rxT])
                if mod is not None:
                    l, vsh, vsc, grp = mod
                    fw.op(DVE, lambda e: e.tensor_scalar(out=hT[:, c, 0:ntok], in0=xT[:, c, 0:ntok],
                                                         scalar1=modT[:, l, vsc * 16 + c, grp:grp + 1], scalar2=modT[:, l, vsh * 16 + c, grp:grp + 1],
                                                         op0=ALU.mult, op1=ALU.add), R=[rxT, rmod], W=[rhT])

        def resid_evac(l, vgt, grp, ntok):
            def f(key, ci, ps, rps):
                c = key[2] * 2 + ci
                fw.op(DVE, lambda e: e.scalar_tensor_tensor(out=xT[:, c, 0:ntok], in0=ps, scalar=modT[:, l, vgt * 16 + c, grp:grp + 1],
                                                            in1=xT[:, c, 0:ntok], op0=ALU.mult, op1=ALU.add), R=[rps, rmod, rxT], W=[rxT])
            return f

        def ffn(l, grp, ntok, nxt=None):
            gT = psb("gT", [128, 22, 512], BF16); rgT = Res("gT")
            sa = psb("sa", [128, 2, 512], F32); rsa = [Res("sa0"), Res("sa1")]
            cnt = {"i": 0}
            for hh in range(2):
                groups = []
                for j in range(22):
                    c0 = hh * 2816 + j * 128
                    groups.append((("fi", l, hh, j), NCH, 256, [(ffn_w_in[l][:, c0:c0 + 128], 0), (ffn_w_in[l][:, DFF + c0:DFF + c0 + 128], 128)]))
                out_groups = [(("fo", l, hh, og), 22, 128, [(ffn_w_out[l][hh * 2816:(hh + 1) * 2816, og * 128:(og + 1) * 128], 0)]) for og in range(16)]

                def evac_in(key, ci, ps, rps):
                    j = key[3]; u = j % 2
                    if ci == 0:
                        fw.op(ACT, lambda e: e.activation(out=sa[:, u, 0:ntok], in_=ps, func=AF.Silu), R=[rps], W=[rsa[u]])
                    else:
                        fw.op(DVE, lambda e: e.tensor_tensor(out=gT[:, j, 0:ntok], in0=ps, in1=sa[:, u, 0:ntok], op=ALU.mult),
                              R=[rps, rsa[u]], W=[rgT])
                dense_fm(groups, hT, rhT, ntok, evac_in, nxt=out_groups[0:3])

                def evac_out(key, ci, ps, rps):
                    c = key[3]
                    fw.op(DVE, lambda e: e.scalar_tensor_tensor(out=xT[:, c, 0:ntok], in0=ps, scalar=modT[:, l, 5 * 16 + c, grp:grp + 1],
                                                                in1=xT[:, c, 0:ntok], op0=ALU.mult, op1=ALU.add), R=[rps, rmod, rxT], W=[rxT])
                dense_fm(out_groups, gT, rgT, ntok, evac_out, nxt=(nxt if hh == 1 else None))

        alr = sb("alr", [33, 512], F32); ralr = Res("alr")
        wa = sb("wa", [33, 1024], F32); rwa = Res("wa")
        fw.dma(SP, dconst, wa[:, :], gla_wa[:, :], W=[rwa])
        fw.op(DVE, lambda e: e.memset(alr[:, :], 0.0), W=[ralr])
        fw.op(DVE, lambda e: e.memset(alr[32:33, :], 1.0), W=[ralr])
        dst = DSem(fw, "state")
        gq_groups = [wgroup(("gq", 0, gi), gla_w_in, gi * 256, 256) for gi in range(8)] + [wgroup(("ga", 0, 0), gla_w_in, 6144, 16)]
        gv_groups = [wgroup(("gv", 0, gi), gla_w_in, 2048 + gi * 256, 256) for gi in range(16)]
        gout_groups = [wgroup(("go", 0, gi), gla_w_out, gi * 256, 256) for gi in range(8)]

        def gla_mixer(ntok, S, rS):
            qT = psb("qT", [128, 8, 512], BF16); kT = psb("kT", [128, 8, 512], BF16); rqk = Res("qk")
            vtm = psb("vtm", [128, 4, D], BF16); rvtm = Res("vtm")
            grtm = psb("grtm", [128, 4, D], BF16); rgr = Res("grtm")
            Sb = psb("Sb", [128, 2, 512], BF16); rSb = Res("Sb")
            lsp = psb("lsp", [128, 1024], F32); rlsp = Res("lsp")
            bT = psb("bT", [128, 8, 128], F32); rbT = Res("bT")
            dd = psb("dd", [128, 8, 128], F32); rdd = Res("dd")
            ee = psb("ee", [128, 8, 128], F32); ree = Res("ee")
            qtl = psb("qtl", [128, 8, 128], BF16); qh = psb("qh", [128, 8, 128], BF16)
            kh = psb("kh", [128, 8, 128], BF16); kbT = psb("kbT", [128, 8, 128], BF16); rqq = Res("qq")
            kbtm = psb("kbtm", [128, 1024], BF16); rkb = Res("kbtm")
            dec = psb("dec", [128, 8], F32); rdec = Res("dec")
            aT = psb("aT", [128, 2, 128], BF16); raT = [Res("aT0"), Res("aT1")]
            ssq = psb("ssq", [128, 8], F32); rssq = Res("ssq")
            junk = psb("junk", [128, 512], BF16); rjunk = Res("junk")
            on = psb("on", [128, D], BF16); ron = Res("on")
            if ntok == 4:
                print("[build] gla phase sbuf_left", nc.sbuf_bytes_remaining)

            def qk_evac(key, ci, ps, rps):
                if key[0] == "ga":
                    fw.op(DVE, lambda e: e.tensor_copy(out=alr[0:16, 0:ntok], in_=ps), R=[rps], W=[ralr])
                    return
                c = key[2] * 2 + ci
                if c < 8:
                    fw.op(ACT, lambda e: e.activation(out=qT[:, c, 0:ntok], in_=ps, func=AF.Copy, scale=256.0 ** -0.5), R=[rps], W=[rqk])
                else:
                    fw.op(DVE, lambda e: e.tensor_copy(out=kT[:, c - 8, 0:ntok], in_=ps), R=[rps], W=[rqk])

            def vr_evac(key, st, ps, rps, nt):
                gi = key[2]
                if gi < 8:
                    fw.op(DVE, lambda e: e.tensor_copy(out=vtm[0:nt, st, gi * 256:(gi + 1) * 256], in_=ps), R=[rps], W=[rvtm])
                else:
                    fw.op(ACT, lambda e: e.activation(out=grtm[0:nt, st, (gi - 8) * 256:(gi - 7) * 256], in_=ps, func=AF.Silu), R=[rps], W=[rgr])

            dense_fm(gq_groups, hT, rhT, ntok, qk_evac, nxt=gv_groups[0:3])
            dense_tm(gv_groups, hT, rhT, ntok, vr_evac, nxt=gout_groups[0:3])

            for ci in range((ntok + 127) // 128):
                C = min(128, ntok - ci * 128); t0 = ci * 128
                mid = max(C // 2 - 1, 0)
                for hf in range(2):
                    b = 6 + hf
                    fw.op(PE, lambda e: e.matmul(pbf(b)[0:C, :], alr[:, t0:t0 + C], wa[:, hf * 512:(hf + 1) * 512], start=True, stop=True),
                          R=[ralr, rwa], W=[rpb[b]])
                    fw.op(ACT, lambda e: e.activation(out=lsp[0:C, hf * 512:(hf + 1) * 512], in_=pbf(b)[0:C, :], func=AF.Exp, scale=-1.0),
                          R=[rpb[b]], W=[rlsp])
                fw.op(ACT, lambda e: e.activation(out=lsp[0:C, :], in_=lsp[0:C, :], func=AF.Ln, bias=onesf[0:C, 0:1]), R=[rlsp, rones], W=[rlsp])
                for dc in range(8):
                    b = 6 + dc // 4
                    fw.op(PE, lambda e: e.matmul(pbf(b)[:, (dc % 4) * 128:(dc % 4) * 128 + C], lsp[0:C, dc * 128:(dc + 1) * 128], tri[0:C, 0:C],
                                                 start=True, stop=True), R=[rlsp, rmask], W=[rpb[b]])
                for hf in range(2):
                    b = 6 + hf
                    fw.op(DVE, lambda e: e.tensor_copy(out=bT[:, hf * 4:(hf + 1) * 4, 0:C],
                                                       in_=pbf(b)[:, :].rearrange("p (a t) -> p a t", a=4)[:, :, 0:C]), R=[rpb[b]], W=[rbT])
                q3 = qT[:, :, t0:t0 + C]; k3 = kT[:, :, t0:t0 + C]
                bmid = bT[:, :, mid:mid + 1].to_broadcast([128, 8, C]); blast = bT[:, :, C - 1:C].to_broadcast([128, 8, C])
                fw.op(ACT, lambda e: e.activation(out=ee[:, :, 0:C], in_=bT[:, :, 0:C], func=AF.Exp), R=[rbT], W=[ree])
                fw.op(DVE, lambda e: e.tensor_tensor(out=qtl[:, :, 0:C], in0=q3, in1=ee[:, :, 0:C], op=ALU.mult), R=[rqk, ree], W=[rqq])
                fw.op(ACT, lambda e: e.activation(out=dec[:, :], in_=bT[:, :, C - 1], func=AF.Exp), R=[rbT], W=[rdec])
                fw.op(DVE, lambda e: e.tensor_tensor(out=dd[:, :, 0:C], in0=bT[:, :, 0:C], in1=bmid, op=ALU.subtract), R=[rbT], W=[rdd])
                fw.op(ACT, lambda e: e.activation(out=ee[:, :, 0:C], in_=dd[:, :, 0:C], func=AF.Exp), R=[rdd], W=[ree])
                fw.op(DVE, lambda e: e.tensor_tensor(out=qh[:, :, 0:C], in0=q3, in1=ee[:, :, 0:C], op=ALU.mult), R=[rqk, ree], W=[rqq])
                fw.op(ACT, lambda e: e.activation(out=ee[:, :, 0:C], in_=dd[:, :, 0:C], func=AF.Exp, scale=-1.0), R=[rdd], W=[ree])
                fw.op(DVE, lambda e: e.tensor_tensor(out=kh[:, :, 0:C], in0=k3, in1=ee[:, :, 0:C], op=ALU.mult), R=[rqk, ree], W=[rqq])
                fw.op(DVE, lambda e: e.tensor_tensor(out=dd[:, :, 0:C], in0=bT[:, :, 0:C], in1=blast, op=ALU.subtract), R=[rbT], W=[rdd])
                fw.op(ACT, lambda e: e.activation(out=ee[:, :, 0:C], in_=dd[:, :, 0:C], func=AF.Exp, scale=-1.0), R=[rdd], W=[ree])
                fw.op(DVE, lambda e: e.tensor_tensor(out=kbT[:, :, 0:C], in0=k3, in1=ee[:, :, 0:C], op=ALU.mult), R=[rqk, ree], W=[rqq])
                for dc in range(8):
                    fw.op(PE, lambda e: e.transpose(out=pbb(6)[0:C, dc * 128:(dc + 1) * 128], in_=kbT[:, dc, 0:C], identity=identb[:, :]),
                          R=[rqq, rid], W=[rpb[6]])
                fw.op(DVE, lambda e: e.tensor_copy(out=kbtm[0:C, :], in_=pbb(6)[0:C, :]), R=[rpb[6]], W=[rkb])
                fw.op(DVE, lambda e: e.memset(ssq[:, :], 0.0), W=[rssq])
                for h in range(4):
                    u = h % 2
                    bo = 4 + u; ba = 7
                    fw.op(ACT, lambda e: e.activation(out=Sb[:, :, :], in_=S[:, 2 * h:2 * h + 2, :], func=AF.Copy), R=[rS], W=[rSb])
                    for half in range(2):
                        fw.op(PE, lambda e: e.matmul(pbf(ba)[0:C, 0:C], kh[:, 2 * h + half, 0:C], qh[:, 2 * h + half, 0:C],
                                                     start=(half == 0), stop=(half == 1)), R=[rqq], W=[rpb[ba]])
                    fw.op(DVE, lambda e: e.tensor_tensor(out=aT[0:C, u, 0:C], in0=pbf(ba)[0:C, 0:C], in1=caus01[0:C, 0:C], op=ALU.mult),
                          R=[rpb[ba], rmask], W=[raT[u]])
                    for half in range(2):
                        fw.op(PE, lambda e: e.matmul(pbf(bo)[0:C, :], qtl[:, 2 * h + half, 0:C], Sb[:, half, :], start=(half == 0), stop=False),
                              R=[rqq, rSb], W=[rpb[bo]])
                    fw.op(PE, lambda e: e.matmul(pbf(bo)[0:C, :], aT[0:C, u, 0:C], vtm[0:C, ci, h * 512:(h + 1) * 512], start=False, stop=True),
                          R=[raT[u], rvtm], W=[rpb[bo]])
                    for half in range(2):
                        bs = 6 if half == 0 else 7
                        fw.op(PE, lambda e: e.matmul(pbf(bs)[:, :], kbtm[0:C, (2 * h + half) * 128:(2 * h + half + 1) * 128],
                                                     vtm[0:C, ci, h * 512:(h + 1) * 512], start=True, stop=True), R=[rkb, rvtm], W=[rpb[bs]])
                        fw.op(DVE, lambda e: e.scalar_tensor_tensor(out=S[:, 2 * h + half, :], in0=S[:, 2 * h + half, :],
                                                                    scalar=dec[:, 2 * h + half:2 * h + half + 1], in1=pbf(bs)[:, :],
                                                                    op0=ALU.mult, op1=ALU.add), R=[rS, rdec, rpb[bs]], W=[rS])
                    fw.op(ACT, lambda e: e.activation(out=junk[0:C, :], in_=pbf(bo)[0:C, :], func=AF.Square, accum_out=ssq[0:C, h:h + 1]),
                          R=[rpb[bo]], W=[rjunk, rssq])
                    fw.op(ACT, lambda e: e.activation(out=ssq[0:C, 4 + h:5 + h], in_=ssq[0:C, h:h + 1], func=AF.Sqrt, scale=1.0 / 512, bias=epsb[0:C, 0:1]),
                          R=[rssq, rones], W=[rssq])
                    fw.op(DVE, lambda e: e.reciprocal(out=ssq[0:C, 4 + h:5 + h], in_=ssq[0:C, 4 + h:5 + h]), R=[rssq], W=[rssq])
                    fw.op(DVE, lambda e: e.scalar_tensor_tensor(out=on[0:C, h * 512:(h + 1) * 512], in0=pbf(bo)[0:C, :], scalar=ssq[0:C, 4 + h:5 + h],
                                                                in1=grtm[0:C, ci, h * 512:(h + 1) * 512], op0=ALU.mult, op1=ALU.mult),
                          R=[rpb[bo], rgr], W=[ron], RS=[rssq])
                dump("on", on[:, :], [ron]); dump("ssq", ssq[:, :], [rssq]); dump("bT", bT[:, :, :], [rbT]); dump("qT", qT[:, :, :], [rqk]); dump("kT", kT[:, :, :], [rqk])
                dump("vtm", vtm[:, 0, :], [rvtm]); dump("grtm", grtm[:, 0, :], [rgr]); dump("lsp", lsp[:, :], [rlsp]); dump("aT", aT[:, :, :], raT); dump("qh", qh[:, :, :], [rqq]); dump("kh", kh[:, :, :], [rqq])
                dump("alr", alr[:, :], [ralr]); dump("hT", hT[:, :, :], [rhT])
                for half in range(2):
                    b = 4 + half
                    for cc in range(8):
                        c = half * 8 + cc
                        fw.op(PE, lambda e: e.transpose(out=pbb(b)[:, cc * 128:cc * 128 + C], in_=on[0:C, c * 128:(c + 1) * 128], identity=identb[0:C, 0:C]),
                              R=[ron, rid], W=[rpb[b]])
                    fw.op(DVE, lambda e: e.tensor_tensor(out=oT[:, half * 8:(half + 1) * 8, t0:t0 + C],
                                                         in0=pbb(b)[:, :].rearrange("p (a t) -> p a t", a=8)[:, :, 0:C],
                                                         in1=vT_g[:, half * 8:(half + 1) * 8].unsqueeze(2).to_broadcast([128, 8, C]), op=ALU.mult),
                          R=[rpb[b], rvt], W=[roT])

        def run_phase(fn, dsems=()):
            with contextlib.ExitStack() as pes:
                phase["es"] = pes
                fn()
                barrier(dsems)
            phase["es"] = None

        if os.environ.get("SKIP_L0"):
            fw.op(DVE, lambda e: e.memset(modT[:, :, :, :], 0.5), W=[rmod])
        for l in range(2 if not os.environ.get("SKIP_L0") else 0):
            dense_fm(ada_groups(l), scT, rct, 2, ada_evac(l))
            ada_post(l)

        x1s = nc.dram_tensor("x1_scratch", [T + 128, D], F32).ap()
        with contextlib.ExitStack() as les:
            S = les.enter_context(nc.sbuf_tensor("S", [128, 8, 512], F32)); rS = Res("S")
            fw.op(DVE, lambda e: e.memset(S[:, :, :], 0.0), W=[rS])

            def layer0_tile(src, ntok, grp, dst):
                load_x_tile(src, ntok)
                modulate_from_x(0, 0, 1, grp, ntok)
                run_phase(lambda: gla_mixer(ntok, S, rS))

                def rest():
                    dense_fm(gout_groups, oT, roT, ntok, resid_evac(0, 2, grp, ntok))
                    if stage == "pre0":
                        store_x_tile(dst, ntok); return
                    layer_norm_T(0, 0, 1, ntok, mod=(0, 3, 4, grp))
                    if stage == "mix0":
                        store_x_tile(dst, ntok); return
                    ffn(0, grp, ntok)
                    layer_norm_T(0, 2, 3, ntok, mod=None)
                    store_x_tile(dst, ntok)
                run_phase(rest)

            for ti in range(n_ptiles if not os.environ.get("SKIP_L0") else 0):
                layer0_tile(xp[ti * 512:(ti + 1) * 512, :], 512, 0, (y_p if stage in ("l0", "mix0", "pre0") else x1s)[ti * 512:(ti + 1) * 512, :])
            for h in range(4):
                fw.dma(SP, dst, st_p[h].rearrange("(a p) e -> p a e", p=128), S[:, 2 * h:2 * h + 2, :], R=[rS])
            if do_sample and not os.environ.get("SKIP_L0"):
                for h in range(4):
                    fw.dma(SP, dst, S[:, 2 * h:2 * h + 2, :], state_in[h].rearrange("(a p) e -> p a e", p=128), W=[rS])
                layer0_tile(xs[:, :], 4, 1, (y_s[0:4, :] if stage in ("l0", "mix0", "pre0") else x1s[T:T + 4, :]))
                for h in range(4):
                    fw.dma(SP, dst, st_s[h].rearrange("(a p) e -> p a e", p=128), S[:, 2 * h:2 * h + 2, :], R=[rS])
            barrier([dst, dio])
        if stage in ("full", "mix1", "pre1"):
          with contextlib.ExitStack() as les:
            def lsb(name, shape, dt):
                return les.enter_context(nc.sbuf_tensor(name, list(shape), dt))
            KsT = lsb("KsT", [128, 4, T], BF16); KwT = lsb("KwT", [128, 4, 1024], BF16); rKs = Res("KsT"); rKw = Res("KwT")
            Vs = lsb("Vs", [128, 16, 4, 130], BF16); Vw = lsb("Vw", [128, 8, 4, 130], BF16); rVs = Res("Vs"); rVw = Res("Vw")
            kcacc = lsb("kcacc", [128, 4, 32], F32); vcacc = lsb("vcacc", [32, 512], F32); rkc = Res("kc"); rvc = Res("vc")
            kcT = lsb("kcT", [128, 4, 32], BF16); vcb = lsb("vcb", [32, 512], BF16)
            causb = lsb("causb", [128, 512], BF16); winb = lsb("winb", [128, 512], BF16); eall = lsb("eall", [32, T], BF16)
            avgA = lsb("avgA", [128, 62], BF16)
            Tc = lsb("Tc", [128, 62], F32); Tm = lsb("Tm", [128, 62], F32); Tf = lsb("Tf", [128, 62], F32)
            bg = lsb("bg", [128, 48], F32); rk1 = Res("l1const"); rk1p = Res("l1constp")
            dk1 = DSem(fw, "l1const"); dk1p = DSem(fw, "l1constp"); dkv = DSem(fw, "kvout")
            for (dst_t, nm, rows, cols) in ((causb, "causb", 128, 512), (winb, "winb", 128, 512), (eall, "eall", 32, T), (avgA, "avgA", 128, 62)):
                fw.dma(SP, dio, iobuf[0:rows, 0:cols], ctab[nm][:, :], W=[rio])
                fw.op(DVE, lambda e: e.tensor_copy(out=dst_t[:, :], in_=iobuf[0:rows, 0:cols]), R=[rio], W=[rk1p])
            fw.dma(SP, dk1, Tc[:, :], ctab["Tc"][:, :], W=[rk1])
            fw.dma(SP, dk1, Tm[:, :], ctab["Tm"][:, :], W=[rk1])
            fw.dma(SP, dk1, Tf[:, :], ctab["Tf"][:, :], W=[rk1])
            fw.dma(SP, dk1, bg[:, :], nsa_bg[0:1, :].partition_broadcast(128), W=[rk1])
            fw.op(DVE, lambda e: e.memset(Vs[:, :, :, :], 1.0), W=[rVs])
            fw.op(DVE, lambda e: e.memset(Vw[:, :, :, :], 1.0), W=[rVw])
            fw.op(DVE, lambda e: e.memset(kcacc[:, :, :], 0.0), W=[rkc])
            fw.op(DVE, lambda e: e.memset(vcacc[:, :], 0.0), W=[rvc])

            nq_groups = [wgroup(("nq", 1, gi), nsa_w_in, gi * 256, 256) for gi in range(8)]
            nkv_groups = [wgroup(("nkv", 1, gi), nsa_w_in, 2048 + gi * 256, 256) for gi in range(12)] + [wgroup(("ng", 1, 0), nsa_w_in, 5120, 48)]
            nout_groups = [wgroup(("no", 1, gi), nsa_w_out, gi * 256, 256) for gi in range(8)]
            kv_outs = [cmp_p, slc_p, win_p]

            def nsa_prompt_tile(ti):
                ntok = 512
                qTn = psb("qTn", [128, 16, 512], BF16); rq = Res("qTn")
                gsg = psb("gsg", [128, 4, 48], F32); rgs = Res("gsg")
                o_tm = psb("o_tm", [128, D], F32); ro = Res("o_tm")
                obf = psb("obf", [128, D], BF16); rob = Res("obf")
                stag = kvstag; rstag = [Res("stag0"), Res("stag1")]
                ktmp = psb("ktmp", [128, 2, 256], BF16); rkt = [Res("kt0"), Res("kt1")]
                sm = psb("sm", [128, 16, 32], F32); rsm = Res("sm")
                pcb = psb("pcb", [128, 16, 32], BF16); rpcb = Res("pcb")
                pcT = psb("pcT", [32, 16, 128], BF16); rpcT = Res("pcT")
                st16 = psb("st16", [128, 4, 16], F32); rst16 = Res("st16")
                imp = psb("imp", [128, 4, 32], F32); rimp = Res("imp")
                m8 = psb("m8", [128, 2, 8], F32); rm8 = Res("m8")
                sct = psb("sct", [128, 32], F32); rsct = Res("sct")
                selb = psb("selb", [128, 32], F32); rselb = Res("selb")
                selbT = psb("selbT", [32, 4, 128], BF16); rselT = Res("selbT")
                pT = psb("pT", [128, 2, 512], BF16); rpT = [Res("pT0"), Res("pT1")]
                coef = psb("coef", [128, 8], F32); rcoef = Res("coef")
                cnt = {"stag": 0, "kt": 0, "pt": 0}
                if ti == 0:
                    print("[build] nsa phase sbuf_left", nc.sbuf_bytes_remaining)

                def q_evac(key, ci, ps, rps):
                    hd = key[2] * 2 + ci
                    fw.op(ACT, lambda e: e.activation(out=qTn[:, hd, 0:ntok], in_=ps, func=AF.Copy, scale=128.0 ** -0.5), R=[rps], W=[rq])

                KVP = os.environ.get("KV_PARTS", "dma,cmp,kt,v,ng").split(",")

                def kv_evac(key, st, ps, rps, nt):
                    kt_g = ti * 4 + st
                    if key[0] == "ng":
                        fw.op(DVE, lambda e: e.tensor_tensor(out=gsg[:, st, :], in0=ps, in1=bg[:, :], op=ALU.add), R=[rps, rk1], W=[rgs])
                        fw.op(ACT, lambda e: e.activation(out=gsg[:, st, :], in_=gsg[:, st, :], func=AF.Sigmoid), R=[rgs], W=[rgs])
                        return
                    gi = key[2]; br = gi // 4; sub = gi % 4; isv = sub // 2; gp = sub % 2
                    u = cnt["stag"] % 2; cnt["stag"] += 1
                    fw.op(ACT, lambda e: e.activation(out=stag[:, u, :], in_=ps, func=AF.Copy), R=[rps], W=[rstag[u]])
                    src = stag[:, u, :]; rsrc = rstag[u]
                    if br < 2 or ti == 3:
                        row0 = (ti * 512 + st * 128) if br < 2 else st * 128
                        fw.dma(SP, dkv, kv_outs[br][row0:row0 + 128, sub * 256:(sub + 1) * 256], src, R=[rsrc])
                    if br == 0:
                        uk = cnt["kt"] % 2; cnt["kt"] += 1
                        fw.op(DVE, lambda e: e.tensor_copy(out=ktmp[:, uk, :], in_=src), R=[rsrc], W=[rkt[uk]])
                        a0 = 30 - 2 * kt_g
                        if isv == 0:
                            for gg in range(2):
                                g = gp * 2 + gg
                                fw.op(PE, lambda e: e.matmul(pbf(4)[:, 0:32], ktmp[:, uk, gg * 128:(gg + 1) * 128], avgA[:, a0:a0 + 32], start=True, stop=True),
                                      R=[rkt[uk], rk1p], W=[rpb[4]])
                                fw.op(DVE, lambda e: e.tensor_tensor(out=kcacc[:, g, :], in0=kcacc[:, g, :], in1=pbf(4)[:, 0:32], op=ALU.add),
                                      R=[rpb[4], rkc], W=[rkc])
                        else:
                            fw.op(PE, lambda e: e.matmul(pbf(5)[0:32, 0:256], avgA[:, a0:a0 + 32], ktmp[:, uk, :], start=True, stop=True),
                                  R=[rkt[uk], rk1p], W=[rpb[5]])
                            fw.op(DVE, lambda e: e.tensor_tensor(out=vcacc[:, gp * 256:(gp + 1) * 256], in0=vcacc[:, gp * 256:(gp + 1) * 256],
                                                                 in1=pbf(5)[0:32, 0:256], op=ALU.add), R=[rpb[5], rvc], W=[rvc])
                    else:
                        KT, rK, VV, rV = (KsT, rKs, Vs, rVs) if br == 1 else (KwT, rKw, Vw, rVw)
                        ks_ = kt_g if br == 1 else kt_g % 8
                        if isv == 0:
                            uk = cnt["kt"] % 2; cnt["kt"] += 1
                            fw.op(DVE, lambda e: e.tensor_copy(out=ktmp[:, uk, :], in_=src), R=[rsrc], W=[rkt[uk]])
                            for gg in range(2):
                                fw.op(PE, lambda e: e.transpose(out=pbb(6)[:, gg * 128:(gg + 1) * 128], in_=ktmp[:, uk, gg * 128:(gg + 1) * 128], identity=identb[:, :]),
                                      R=[rkt[uk], rid], W=[rpb[6]])
                            fw.op(ACT, lambda e: e.activation(out=KT[:, gp * 2:gp * 2 + 2, ks_ * 128:(ks_ + 1) * 128],
                                                              in_=pbb(6)[:, 0:256].rearrange("p (a t) -> p a t", a=2), func=AF.Copy), R=[rpb[6]], W=[rK])
                        else:
                            fw.op(DVE, lambda e: e.tensor_copy(out=VV[:, ks_, gp * 2:gp * 2 + 2, 0:128],
                                                               in_=src.rearrange("p (a t) -> p a t", a=2)), R=[rsrc], W=[rV])

                CUT = int(os.environ.get("L1_CUT", "9"))
                if CUT <= 1:
                    return
                dense_fm(nq_groups, hT, rhT, ntok, q_evac, nxt=nkv_groups[0] if CUT > 2 else nout_groups[0])
                if CUT <= 2:
                    return
                dense_tm(nkv_groups, hT, rhT, ntok, kv_evac, nxt=nout_groups[0:3])
                if CUT <= 3:
                    return
                fw.op(ACT, lambda e: e.activation(out=kcT[:, :, :], in_=kcacc[:, :, :], func=AF.Copy), R=[rkc], W=[rkc])
                fw.op(ACT, lambda e: e.activation(out=vcb[:, :], in_=vcacc[:, :], func=AF.Copy), R=[rvc], W=[rvc])

                for st in range(4 if not os.environ.get("L1_SKIP_ATTN") else 0):
                    qt = ti * 4 + st
                    qs = slice(st * 128, (st + 1) * 128)
                    m0 = 30 - 2 * qt
                    for hd in range(16):
                        fw.op(PE, lambda e: e.matmul(pbf(4)[:, hd * 32:(hd + 1) * 32], qTn[:, hd, qs], kcT[:, hd // 4, :], start=True, stop=True),
                              R=[rq, rkc], W=[rpb[4]])
                    ps3 = pbf(4)[:, :].rearrange("p (h j) -> p h j", h=16)
                    fw.op(DVE, lambda e: e.tensor_tensor(out=sm[:, :, :], in0=ps3, in1=Tc[:, m0:m0 + 32].unsqueeze(1).to_broadcast([128, 16, 32]), op=ALU.add),
                          R=[rpb[4], rk1], W=[rsm])
                    fw.op(DVE, lambda e: e.tensor_reduce(out=st16[:, 0, :], in_=sm[:, :, :], axis=AX.X, op=ALU.max), R=[rsm], W=[rst16])
                    fw.op(DVE, lambda e: e.tensor_tensor(out=sm[:, :, :], in0=sm[:, :, :], in1=st16[:, 0, :].unsqueeze(2).to_broadcast([128, 16, 32]), op=ALU.subtract),
                          R=[rsm, rst16], W=[rsm])
                    fw.op(ACT, lambda e: e.activation(out=sm[:, :, :], in_=sm[:, :, :], func=AF.Exp), R=[rsm], W=[rsm])
                    fw.op(DVE, lambda e: e.tensor_tensor(out=sm[:, :, :], in0=sm[:, :, :], in1=Tm[:, m0:m0 + 32].unsqueeze(1).to_broadcast([128, 16, 32]), op=ALU.mult),
                          R=[rsm, rk1], W=[rsm])
                    fw.op(DVE, lambda e: e.tensor_reduce(out=st16[:, 1, :], in_=sm[:, :, :], axis=AX.X, op=ALU.add), R=[rsm], W=[rst16])
                    fw.op(DVE, lambda e: e.tensor_scalar(out=st16[:, 1, :], in0=st16[:, 1, :], scalar1=1e-30, scalar2=None, op0=ALU.max), R=[rst16], W=[rst16])
                    fw.op(DVE, lambda e: e.reciprocal(out=st16[:, 2, :], in_=st16[:, 1, :]), R=[rst16], W=[rst16])
                    fw.op(DVE, lambda e: e.tensor_tensor(out=sm[:, :, :], in0=sm[:, :, :], in1=st16[:, 2, :].unsqueeze(2).to_broadcast([128, 16, 32]), op=ALU.mult),
                          R=[rsm, rst16], W=[rsm])
                    if qt >= 8:
                        fw.op(DVE, lambda e: e.tensor_reduce(out=imp[:, :, :], in_=sm[:, :, :].rearrange("p (g h) j -> p g j h", g=4), axis=AX.X, op=ALU.add),
                              R=[rsm], W=[rimp])
                        fw.op(DVE, lambda e: e.tensor_tensor(out=imp[:, :, :], in0=imp[:, :, :], in1=Tf[:, m0:m0 + 32].unsqueeze(1).to_broadcast([128, 4, 32]), op=ALU.add),
                              R=[rimp, rk1], W=[rimp])
                        fw.op(DVE, lambda e: e.memset(imp[:, :, 0:1], 1e4), W=[rimp])
                        for g in range(4):
                            fw.op(DVE, lambda e: e.max(out=m8[:, 0, :], in_=imp[:, g, :]), R=[rimp], W=[rm8])
                            fw.op(DVE, lambda e: e.match_replace(out=sct[:, :], in_to_replace=m8[:, 0, :], in_values=imp[:, g, :], imm_value=-1e9), R=[rimp, rm8], W=[rsct])
                            fw.op(DVE, lambda e: e.max(out=m8[:, 1, :], in_=sct[:, :]), R=[rsct], W=[rm8])
                            fw.op(DVE, lambda e: e.tensor_scalar(out=selb[:, :], in0=imp[:, g, :], scalar1=m8[:, 1, 7:8], scalar2=-NEG, op0=ALU.is_ge, op1=ALU.mult),
                                  R=[rimp], W=[rselb], RS=[rm8])
                            fw.op(DVE, lambda e: e.tensor_scalar(out=selb[:, :], in0=selb[:, :], scalar1=NEG, scalar2=None, op0=ALU.add), R=[rselb], W=[rselb])
                            fw.op(PE, lambda e: e.transpose(out=pbf(6)[0:32, 0:128], in_=selb[:, :], identity=identf[:, :]), R=[rselb, rid], W=[rpb[6]])
                            fw.op(ACT, lambda e: e.activation(out=selbT[:, g, :], in_=pbf(6)[0:32, 0:128], func=AF.Copy), R=[rpb[6]], W=[rselT])
                    gc = gsg[:, st, :].rearrange("p (h b) -> p h b", b=3)[:, :, 0:1].to_broadcast([128, 16, 32])
                    fw.op(DVE, lambda e: e.tensor_tensor(out=pcb[:, :, :], in0=sm[:, :, :], in1=gc, op=ALU.mult), R=[rsm, rgs], W=[rpcb])
                    for half in range(2):
                        b = 6 + half
                        for hh in range(8):
                            hd = half * 8 + hh
                            fw.op(PE, lambda e: e.transpose(out=pbb(b)[0:32, hh * 128:(hh + 1) * 128], in_=pcb[:, hd, :], identity=identb[:, :]),
                                  R=[rpcb, rid], W=[rpb[b]])
                        fw.op(ACT, lambda e: e.activation(out=pcT[:, half * 8:(half + 1) * 8, :], in_=pbb(b)[0:32, :].rearrange("p (a t) -> p a t", a=8), func=AF.Copy),
                              R=[rpb[b]], W=[rpcT])
                    for g in range(4):
                        for h in range(4):
                            fw.op(PE, lambda e: e.matmul(pbf(5)[:, h * 128:(h + 1) * 128], pcT[:, 4 * g + h, :], vcb[:, g * 128:(g + 1) * 128], start=True, stop=True),
                                  R=[rpcT, rvc], W=[rpb[5]])
                        fw.op(ACT, lambda e: e.activation(out=o_tm[:, g * 512:(g + 1) * 512], in_=pbf(5)[:, :], func=AF.Copy), R=[rpb[5]], W=[ro])
                    for (KT, rK, VV, rV, kts, bi) in ((KsT, rKs, Vs, rVs, list(range(0, qt + 1)), 1), (KwT, rKw, Vw, rVw, list(range(max(0, qt - 4), qt + 1)), 2)):
                        for g in range(4):
                            rhsQ = qTn[:, 4 * g:4 * g + 4, qs]
                            for ki, kt in enumerate(kts):
                                ksl = kt if bi == 1 else kt % 8
                                u = cnt["pt"] % 2; cnt["pt"] += 1
                                sb_ = u
                                mask = None
                                if kt == qt:
                                    mask = ("id", causb[:, :])
                                elif bi == 2 and kt == qt - 4:
                                    mask = ("id", winb[:, :])
                                elif bi == 1 and qt >= 8:
                                    mask = ("sel", None)
                                out3 = pbf(sb_)[:, :].rearrange("p (a t) -> p a t", a=4)
                                fw.op(PE, lambda e: e.matmul(out3, KT[:, g, ksl * 128:(ksl + 1) * 128], rhsQ, start=True, stop=(mask is None)),
                                      R=[rK, rq], W=[rpb[sb_]])
                                if mask is not None and mask[0] == "id":
                                    fw.op(PE, lambda e: e.matmul(pbf(sb_)[:, :], identb[:, :], mask[1], start=False, stop=True), R=[rid, rk1p], W=[rpb[sb_]])
                                elif mask is not None:
                                    fw.op(PE, lambda e: e.matmul(out3, eall[:, kt * 128:(kt + 1) * 128], selbT[:, g, :].unsqueeze(1).to_broadcast([32, 4, 128]),
                                                                 start=False, stop=True), R=[rk1p, rselT], W=[rpb[sb_]])
                                fw.op(ACT, lambda e: e.activation(out=pT[:, u, :], in_=pbf(sb_)[:, :], func=AF.Exp), R=[rpb[sb_]], W=[rpT[u]])
                                for h in range(4):
                                    ob = 2 + h
                                    fw.op(PE, lambda e: e.matmul(pbf(ob)[:, 0:130], pT[:, u, h * 128:(h + 1) * 128], VV[:, ksl, g, 0:130],
                                                                 start=(ki == 0), stop=(ki == len(kts) - 1)), R=[rpT[u], rV], W=[rpb[ob]])
                            for h in range(4):
                                ob = 2 + h; c0 = 0; hd = 4 * g + h
                                fw.op(DVE, lambda e: e.reciprocal(out=coef[:, h:h + 1], in_=pbf(ob)[:, c0 + 128:c0 + 129]), R=[rpb[ob]], W=[rcoef])
                                fw.op(DVE, lambda e: e.tensor_tensor(out=coef[:, 4 + h:5 + h], in0=coef[:, h:h + 1], in1=gsg[:, st, hd * 3 + bi:hd * 3 + bi + 1], op=ALU.mult),
                                      R=[rcoef, rgs], W=[rcoef])
                                fw.op(DVE, lambda e: e.scalar_tensor_tensor(out=o_tm[:, hd * 128:(hd + 1) * 128], in0=pbf(ob)[:, c0:c0 + 128], scalar=coef[:, 4 + h:5 + h],
                                                                            in1=o_tm[:, hd * 128:(hd + 1) * 128], op0=ALU.mult, op1=ALU.add),
                                      R=[rpb[ob], ro], W=[ro], RS=[rcoef])
                    dump("o_tm", o_tm[:, :], [ro]); dump("gsg", gsg[:, :, :], [rgs]); dump("sm", sm[:, :, :], [rsm]); dump("qTn", qTn[:, :, :], [rq])
                    dump("KsT", KsT[:, :, 0:512], [rKs]); dump("Vs", Vs[:, 0:4, :, :], [rVs]); dump("kcT", kcT[:, :, :], [rkc]); dump("vcb", vcb[:, :], [rvc]); dump("coef", coef[:, :], [rcoef])
                    fw.op(ACT, lambda e: e.activation(out=obf[:, :], in_=o_tm[:, :], func=AF.Copy), R=[ro], W=[rob])
                    for half in range(2):
                        b = 6 + half
                        for cc in range(8):
                            c = half * 8 + cc
                            fw.op(PE, lambda e: e.transpose(out=pbb(b)[:, cc * 128:(cc + 1) * 128], in_=obf[:, c * 128:(c + 1) * 128], identity=identb[:, :]),
                                  R=[rob, rid], W=[rpb[b]])
                        fw.op(DVE, lambda e: e.tensor_copy(out=oT[:, half * 8:(half + 1) * 8, qs], in_=pbb(b)[:, :].rearrange("p (a t) -> p a t", a=8)), R=[rpb[b]], W=[roT])

            def layer1_tile_prompt(ti):
                load_x_tile((xp if os.environ.get('SKIP_L0') else x1s)[ti * 512:(ti + 1) * 512, :], 512)
                modulate_from_x(1, 0, 1, 0, 512)
                run_phase(lambda: nsa_prompt_tile(ti), dsems=[dkv])

                def rest():
                    dense_fm(nout_groups, oT, roT, 512, resid_evac(1, 2, 0, 512))
                    if stage == "pre1":
                        store_x_tile(y_p[ti * 512:(ti + 1) * 512, :], 512); return
                    layer_norm_T(1, 0, 1, 512, mod=(1, 3, 4, 0))
                    if stage == "mix1":
                        store_x_tile(y_p[ti * 512:(ti + 1) * 512, :], 512); return
                    ffn(1, 0, 512)
                    layer_norm_T(1, 2, 3, 512, mod=None)
                    store_x_tile(y_p[ti * 512:(ti + 1) * 512, :], 512)
                run_phase(rest)

            for ti in range(n_ptiles):
                layer1_tile_prompt(ti)
            def nsa_sample_tile():
                NT_ = 4
                qS = psb("qS", [128, 16, 4], BF16); rqS = Res("qS")
                gs4 = psb("gs4", [128, 48], F32); rgs4 = Res("gs4")
                o4 = iobuf; ro4 = rio
                nkb = psb("nkb", [4, 3, 1024], BF16); rnkb = Res("nkb")
                stag = kvstag; rstag = [Res("sstag0"), Res("sstag1")]
                idxt = psb("idxt", [128, 128], I32); iot = psb("iot", [128, 1], I32); rpt = Res("ptab")
                cpage = psb("cpage", [128, 2, 1024], BF16); rcp = [Res("cp0"), Res("cp1")]
                dcp = [DSem(fw, "cp0", serial=False), DSem(fw, "cp1", serial=False)]
                kcS = psb("kcS", [128, 4, 256], BF16); vcS = psb("vcS", [128, 2, 512], BF16); rkcS = Res("kcS"); rvcS = Res("vcS")
                smS = psb("smS", [4, 4, 256], F32); rsmS = Res("smS")
                pcS = psb("pcS", [4, 4, 256], BF16); rpcS = Res("pcS")
                pTs = psb("pTs", [128, 8, 4], BF16); rpTs = Res("pTs")
                s4 = psb("s4", [4, 3, 4], F32); rs4 = Res("s4")
                scS = psb("scS", [4, 4, 264], F32); rscS = Res("scS")
                m8s = psb("m8s", [4, 2, 8], F32); rm8s = Res("m8s")
                scts = psb("scts", [4, 264], F32); rscts = Res("scts")
                sel01 = smS; rsel = rsmS
                sel2 = psb("sel2", [2, 2048], BF16); rsel2 = Res("sel2")
                Msb = psb("Msb", [128, 128, 16], BF16); rM = Res("Msb")
                KTp = psb("KTp", [128, 2, 512], BF16); rKTp = [Res("KTp0"), Res("KTp1")]
                Va = psb("Va", [128, 1, 4, 130], BF16); rVa = [Res("Va0"), Res("Va0b")]; rVa[1] = rVa[0]
                PT = psb("PT", [128, 2, 64], BF16); rPT = [Res("PT0"), Res("PT1")]
                accS = psb("accS", [16, 2, 4, 130], F32); racc = Res("accS")
                knT = psb("knT", [128, 2, 4, 4], BF16); rknT = Res("knT")
                vn = psb("vn", [4, 2, 4, 130], BF16); rvn = Res("vn")
                h2 = psb("h2", [2, 128], BF16); cn = psb("cn", [4, 16], BF16); wf = psb("wf", [128, 16], BF16); avgB = psb("avgB", [128, 254], BF16); rsc = Res("sconst")
                coef4 = psb("coef4", [4, 8], F32); rcoef4 = Res("coef4")
                dsm = DSem(fw, "smisc"); dsm2 = DSem(fw, "smisc2")
                cnt = {"stag": 0}
                for (dst_t, nm, rows, cols) in ((h2, "h2", 2, 128), (cn, "cn", 4, 16), (wf, "wf", 128, 16), (avgB, "avgB", 128, 254)):
                    fw.dma(SP, dio, iobuf[0:rows, 0:cols], ctab[nm][:, :], W=[rio])
                    fw.op(DVE, lambda e: e.tensor_copy(out=dst_t[:, :], in_=iobuf[0:rows, 0:cols]), R=[rio], W=[rsc])
                fw.dma(SP, dsm, idxt[:, :], ptab[0:1, :].partition_broadcast(128), W=[rpt])
                fw.dma(SP, dsm, iot[:, :], iota_in[:, :], W=[rpt])
                fw.op(DVE, lambda e: e.tensor_scalar(out=idxt[:, :], in0=idxt[:, :], scalar1=128, scalar2=iot[:, 0:1], op0=ALU.mult, op1=ALU.add), R=[rpt], W=[rpt])
                fw.op(DVE, lambda e: e.memset(Va[:, :, :, :], 1.0), W=[rVa[0]])
                fw.op(DVE, lambda e: e.memset(vn[:, :, :, :], 1.0), W=[rvn])
                fw.op(DVE, lambda e: e.memset(accS[:, :, :, :], 0.0), W=[racc])
                fw.dma(SP, dsm, win_s[0:508, :], winbuf[4:512, :])

                def q_evac(key, ci, ps, rps):
                    hd = key[2] * 2 + ci
                    fw.op(ACT, lambda e: e.activation(out=qS[:, hd, 0:4], in_=ps, func=AF.Copy, scale=128.0 ** -0.5), R=[rps], W=[rqS])

                def kv_evac(key, st, ps, rps, nt):
                    if key[0] == "ng":
                        fw.op(DVE, lambda e: e.tensor_tensor(out=gs4[0:4, :], in0=ps, in1=bg[0:4, :], op=ALU.add), R=[rps, rk1], W=[rgs4])
                        fw.op(ACT, lambda e: e.activation(out=gs4[0:4, :], in_=gs4[0:4, :], func=AF.Sigmoid), R=[rgs4], W=[rgs4])
                        return
                    gi = key[2]; br = gi // 4; sub = gi % 4
                    u = cnt["stag"] % 2; cnt["stag"] += 1
                    fw.op(ACT, lambda e: e.activation(out=stag[0:4, u, :], in_=ps, func=AF.Copy), R=[rps], W=[rstag[u]])
                    if br < 2:
                        fw.dma(SP, dkv, (cmp_s, slc_s)[br][0:4, sub * 256:(sub + 1) * 256], stag[0:4, u, :], R=[rstag[u]])
                    else:
                        fw.dma(SP, dkv, win_s[508:512, sub * 256:(sub + 1) * 256], stag[0:4, u, :], R=[rstag[u]])
                    fw.op(DVE, lambda e: e.tensor_copy(out=nkb[0:4, br, sub * 256:(sub + 1) * 256], in_=stag[0:4, u, :]), R=[rstag[u]], W=[rnkb])

                dense_fm(nq_groups, hT, rhT, 4, q_evac, nxt=nkv_groups[0:3])
                dense_tm(nkv_groups, hT, rhT, 4, kv_evac, nxt=nout_groups[0:3])

                for bi in (1, 2):
                    for g in range(4):
                        fw.op(PE, lambda e: e.transpose(out=pbb(6)[:, g * 4:g * 4 + 4], in_=nkb[0:4, bi, g * 128:(g + 1) * 128], identity=identb[0:4, 0:4]),
                              R=[rnkb, rid], W=[rpb[6]])
                    fw.op(ACT, lambda e: e.activation(out=knT[:, bi - 1, :, :], in_=pbb(6)[:, 0:16].rearrange("p (g t) -> p g t", g=4), func=AF.Copy), R=[rpb[6]], W=[rknT])
                    fw.op(DVE, lambda e: e.tensor_copy(out=vn[0:4, bi - 1, :, 0:128], in_=nkb[0:4, bi, 512:1024].rearrange("p (g d) -> p g d", g=4)), R=[rnkb], W=[rvn])

                def page_dma(cache, p, u):
                    fw._deps(POOL, [rpt], [rcp[u]])
                    ins = nc.gpsimd.indirect_dma_start(out=cpage[:, u, :], out_offset=None, in_=cache[:, :],
                                                       in_offset=bass.IndirectOffsetOnAxis(ap=idxt[:, p:p + 1], axis=0))
                    dcp[u].n += 16; ins.then_inc(dcp[u].sem, 16)
                    rcp[u].w = ('d', dcp[u], dcp[u].n); rcp[u].rd = {}
                    rpt.rd[id(dcp[u])] = rcp[u].w

                for p in range(128):
                    u = p % 2
                    page_dma(cache_cmp, p, u)
                    for g in range(4):
                        fw.op(PE, lambda e: e.matmul(pbf(g // 2)[:, (g % 2) * 256 + 2 * p:(g % 2) * 256 + 2 * p + 2], cpage[:, u, g * 128:(g + 1) * 128], avgA[:, 30:32],
                                                     start=True, stop=True), R=[rcp[u], rk1p], W=[rpb[g // 2]])
                    off = 126 - 2 * (p % 64)
                    fw.op(PE, lambda e: e.matmul(pbf(2 + p // 64)[:, :], avgB[:, off:off + 128], cpage[:, u, 512:1024], start=(p % 64 == 0), stop=(p % 64 == 63)),
                          R=[rcp[u], rsc], W=[rpb[2 + p // 64]])
                for b2 in range(2):
                    fw.op(ACT, lambda e: e.activation(out=kcS[:, 2 * b2:2 * b2 + 2, :], in_=pbf(b2)[:, :].rearrange("p (g j) -> p g j", g=2), func=AF.Copy), R=[rpb[b2]], W=[rkcS])
                    fw.op(DVE, lambda e: e.tensor_copy(out=vcS[:, b2, :], in_=pbf(2 + b2)[:, :]), R=[rpb[2 + b2]], W=[rvcS])
                fw.op(DVE, lambda e: e.memset(scS[:, :, :], 0.0), W=[rscS])
                for g in range(4):
                    for h in range(4):
                        fw.op(PE, lambda e: e.matmul(pbf(4 + h // 2)[0:4, (h % 2) * 256:(h % 2) * 256 + 256], qS[:, 4 * g + h, 0:4], kcS[:, g, :], start=True, stop=True),
                              R=[rqS, rkcS], W=[rpb[4 + h // 2]])
                    for b2 in range(2):
                        fw.op(DVE, lambda e: e.tensor_copy(out=smS[:, 2 * b2:2 * b2 + 2, :], in_=pbf(4 + b2)[0:4, :].rearrange("p (h j) -> p h j", h=2)), R=[rpb[4 + b2]], W=[rsmS])
                    fw.op(DVE, lambda e: e.tensor_reduce(out=s4[:, 0, :], in_=smS[:, :, :], axis=AX.X, op=ALU.max), R=[rsmS], W=[rs4])
                    fw.op(DVE, lambda e: e.tensor_tensor(out=smS[:, :, :], in0=smS[:, :, :], in1=s4[:, 0, :].unsqueeze(2).to_broadcast([4, 4, 256]), op=ALU.subtract), R=[rsmS, rs4], W=[rsmS])
                    fw.op(ACT, lambda e: e.activation(out=smS[:, :, :], in_=smS[:, :, :], func=AF.Exp), R=[rsmS], W=[rsmS])
                    fw.op(DVE, lambda e: e.tensor_reduce(out=s4[:, 1, :], in_=smS[:, :, :], axis=AX.X, op=ALU.add), R=[rsmS], W=[rs4])
                    fw.op(DVE, lambda e: e.reciprocal(out=s4[:, 2, :], in_=s4[:, 1, :]), R=[rs4], W=[rs4])
                    fw.op(DVE, lambda e: e.tensor_tensor(out=smS[:, :, :], in0=smS[:, :, :], in1=s4[:, 2, :].unsqueeze(2).to_broadcast([4, 4, 256]), op=ALU.mult), R=[rsmS, rs4], W=[rsmS])
                    fw.op(DVE, lambda e: e.tensor_reduce(out=scS[:, g, 0:256], in_=smS[:, :, :].rearrange("p h j -> p j h"), axis=AX.X, op=ALU.add), R=[rsmS], W=[rscS])
                    gc = gs4[0:4, :].rearrange("p (h b) -> p h b", b=3)[:, 4 * g:4 * g + 4, 0:1].to_broadcast([4, 4, 256])
                    fw.op(DVE, lambda e: e.tensor_tensor(out=pcS[:, :, :], in0=smS[:, :, :], in1=gc, op=ALU.mult), R=[rsmS, rgs4], W=[rpcS])
                    for h in range(4):
                        for c in range(2):
                            fw.op(PE, lambda e: e.transpose(out=pbb(6)[:, (h * 2 + c) * 4:(h * 2 + c) * 4 + 4], in_=pcS[0:4, h, c * 128:(c + 1) * 128], identity=identb[0:4, 0:4]),
                                  R=[rpcS, rid], W=[rpb[6]])
                    fw.op(ACT, lambda e: e.activation(out=pTs[:, :, :], in_=pbb(6)[:, 0:32].rearrange("p (a t) -> p a t", a=8), func=AF.Copy), R=[rpb[6]], W=[rpTs])
                    for h in range(4):
                        for c in range(2):
                            fw.op(PE, lambda e: e.matmul(pbf(7)[0:4, h * 128:(h + 1) * 128], pTs[:, h * 2 + c, :], vcS[:, c, g * 128:(g + 1) * 128], start=(c == 0), stop=(c == 1)),
                                  R=[rpTs, rvcS], W=[rpb[7]])
                    fw.op(ACT, lambda e: e.activation(out=o4[0:4, g * 512:(g + 1) * 512], in_=pbf(7)[0:4, :], func=AF.Copy), R=[rpb[7]], W=[ro4])
                for col in (0, 255, 256):
                    fw.op(DVE, lambda e: e.memset(scS[:, :, col:col + 1], 1e4), W=[rscS])
                for g in range(4):
                    fw.op(DVE, lambda e: e.max(out=m8s[:, 0, :], in_=scS[:, g, :]), R=[rscS], W=[rm8s])
                    fw.op(DVE, lambda e: e.match_replace(out=scts[:, :], in_to_replace=m8s[:, 0, :], in_values=scS[:, g, :], imm_value=-1e9), R=[rscS, rm8s], W=[rscts])
                    fw.op(DVE, lambda e: e.max(out=m8s[:, 1, :], in_=scts[:, :]), R=[rscts], W=[rm8s])
                    fw.op(DVE, lambda e: e.tensor_scalar(out=sel01[:, g, :], in0=scS[:, g, 0:256], scalar1=m8s[:, 1, 7:8], scalar2=None, op0=ALU.is_ge),
                          R=[rscS], W=[rsel], RS=[rm8s])
                rscr = Res('selscr')
                fw.dma(SP, dsm, selscr[:, :, :], sel01[:, :, :], R=[rsel], W=[rscr])
                with nc.allow_non_contiguous_dma(reason="4096-element selection-mask relayout (query-major -> page-parity-major)"):
                    s2v = sel2[:, :].rearrange("t (p g q) -> t p g q", p=128, g=4)
                    for q_ in range(4):
                        for g_ in range(4):
                            fw.dma(POOL, dsm2, s2v[:, :, g_, q_], selscr[q_, g_, :].rearrange("(p t) -> t p", t=2), W=[rsel2], R=[rscr])
                for c4 in range(4):
                    fw.op(PE, lambda e: e.matmul(pbf(c4)[:, :], h2[:, :], sel2[:, c4 * 512:(c4 + 1) * 512], start=True, stop=True), R=[rsc, rsel2], W=[rpb[c4]])
                    fw.op(ACT, lambda e: e.activation(out=Msb[:, c4 * 32:(c4 + 1) * 32, :], in_=pbf(c4)[:, :].rearrange("p (a b) -> p a b", b=16), func=AF.Copy), R=[rpb[c4]], W=[rM])

                def attend_tile(KT_ap, V_ap, nk, mask_ap, rdeps, bsel, it):
                    u = it % 2
                    sbk = u
                    for g in range(4):
                        fw.op(PE, lambda e: e.matmul(pbf(sbk)[0:nk, g * 16:(g + 1) * 16].rearrange("p (h q) -> p h q", h=4), KT_ap(g), qS[:, 4 * g:4 * g + 4, 0:4], start=True, stop=True),
                              R=list(rdeps) + [rqS], W=[rpb[sbk]])
                    fw.op(ACT, lambda e: e.activation(out=PT[0:nk, u, :], in_=pbf(sbk)[0:nk, 0:64], func=AF.Exp), R=[rpb[sbk]], W=[rPT[u]])
                    if mask_ap is not None:
                        fw.op(DVE, lambda e: e.tensor_tensor(out=PT[0:nk, u, :].rearrange("p (g h q) -> p g h q", g=4, h=4), in0=PT[0:nk, u, :].rearrange("p (g h q) -> p g h q", g=4, h=4),
                                                             in1=mask_ap, op=ALU.mult), R=[rPT[u], rM, rsc], W=[rPT[u]])
                    for g in range(4):
                        ob = 2 + g // 2
                        fw.op(PE, lambda e: e.matmul(pbf(ob)[0:16, (g % 2) * 130:(g % 2) * 130 + 130], PT[0:nk, u, g * 16:(g + 1) * 16], V_ap(g), start=True, stop=True),
                              R=list(rdeps) + [rPT[u]], W=[rpb[ob]])
                    for gp in range(2):
                        fw.op(DVE, lambda e: e.tensor_tensor(out=accS[:, bsel, 2 * gp:2 * gp + 2, :], in0=accS[:, bsel, 2 * gp:2 * gp + 2, :],
                                                             in1=pbf(2 + gp)[0:16, 0:260].rearrange("p (g d) -> p g d", g=2), op=ALU.add), R=[rpb[2 + gp], racc], W=[racc])

                def cached_tile(u, rsrcs, mask_ap, bsel, it):
                    for g in range(4):
                        fw.op(PE, lambda e: e.transpose(out=pbb(6)[:, g * 128:(g + 1) * 128], in_=cpage[:, u, g * 128:(g + 1) * 128], identity=identb[:, :]),
                              R=[rcp[u], rid], W=[rpb[6]])
                    fw.op(ACT, lambda e: e.activation(out=KTp[:, u, :], in_=pbb(6)[:, 0:512], func=AF.Copy), R=[rpb[6]], W=[rKTp[u]])
                    fw.op(DVE, lambda e: e.tensor_copy(out=Va[:, 0, :, 0:128], in_=cpage[:, u, 512:1024].rearrange("p (g d) -> p g d", g=4)), R=[rcp[u]], W=[rVa[u]])
                    attend_tile(lambda g: KTp[:, u, g * 128:(g + 1) * 128], lambda g: Va[:, 0, g, :], 128, mask_ap, [rKTp[u], rVa[u]], bsel, it)

                it = 0
                for p in range(128):
                    u = p % 2
                    page_dma(cache_slc, p, u)
                    m_ap = Msb[:, p, :].rearrange("p (g q) -> p g q", g=4).unsqueeze(2).to_broadcast([128, 4, 4, 4])
                    cached_tile(u, None, m_ap, 0, it); it += 1
                cn_ap = cn[0:4, :].rearrange("p (h q) -> p h q", h=4).unsqueeze(1).to_broadcast([4, 4, 4, 4])
                attend_tile(lambda g: knT[:, 0, g, :], lambda g: vn[0:4, 0, g, :], 4, cn_ap, [rknT, rvn], 0, it); it += 1
                for t4 in range(4):
                    u = t4 % 2
                    fw.dma(POOL, dcp[u], cpage[:, u, :], winbuf[t4 * 128:(t4 + 1) * 128, :], W=[rcp[u]])
                    m_ap = wf[:, :].rearrange("p (h q) -> p h q", h=4).unsqueeze(1).to_broadcast([128, 4, 4, 4]) if t4 == 0 else None
                    cached_tile(u, None, m_ap, 1, it); it += 1
                attend_tile(lambda g: knT[:, 1, g, :], lambda g: vn[0:4, 1, g, :], 4, cn_ap, [rknT, rvn], 1, it); it += 1

                for bsel in range(2):
                    for h in range(4):
                        for gp in range(2):
                            b = 4 + gp
                            fw.op(PE, lambda e: e.matmul(pbf(b)[0:4, 0:260], identf[0:16, h * 4:(h + 1) * 4], accS[:, bsel, 2 * gp:2 * gp + 2, :].rearrange("p g d -> p (g d)"),
                                                         start=True, stop=True), R=[racc, rid], W=[rpb[b]])
                            for gg in range(2):
                                hd = 4 * (2 * gp + gg) + h; c0 = gg * 130
                                fw.op(DVE, lambda e: e.reciprocal(out=coef4[:, 0:1], in_=pbf(b)[0:4, c0 + 128:c0 + 129]), R=[rpb[b]], W=[rcoef4])
                                fw.op(DVE, lambda e: e.tensor_tensor(out=coef4[:, 1:2], in0=coef4[:, 0:1], in1=gs4[0:4, hd * 3 + 1 + bsel:hd * 3 + 2 + bsel], op=ALU.mult),
                                      R=[rcoef4, rgs4], W=[rcoef4])
                                fw.op(DVE, lambda e: e.scalar_tensor_tensor(out=o4[0:4, hd * 128:(hd + 1) * 128], in0=pbf(b)[0:4, c0:c0 + 128], scalar=coef4[:, 1:2],
                                                                            in1=o4[0:4, hd * 128:(hd + 1) * 128], op0=ALU.mult, op1=ALU.add), R=[rpb[b], ro4], W=[ro4], RS=[rcoef4])
                ob4 = cpage[:, :, :].rearrange("p a b -> p (a b)")
                fw.op(ACT, lambda e: e.activation(out=ob4[0:4, :], in_=o4[0:4, :], func=AF.Copy), R=[ro4], W=rcp)
                for c in range(16):
                    fw.op(PE, lambda e: e.transpose(out=pbb(6)[:, c * 4:c * 4 + 4], in_=ob4[0:4, c * 128:(c + 1) * 128], identity=identb[0:4, 0:4]), R=rcp + [rid], W=[rpb[6]])
                fw.op(DVE, lambda e: e.tensor_copy(out=oT[:, :, 0:4], in_=pbb(6)[:, 0:64].rearrange("p (a t) -> p a t", a=16)), R=[rpb[6]], W=[roT])
                return [dsm, dsm2, dcp[0], dcp[1]]

            if do_sample:
                load_x_tile((xs[0:4, :] if os.environ.get('SKIP_L0') else x1s[T:T + 4, :]), 4)
                modulate_from_x(1, 0, 1, 1, 4)
                extra = {}
                def phase_s():
                    extra["ds"] = nsa_sample_tile()
                with contextlib.ExitStack() as pes:
                    phase["es"] = pes
                    phase_s()
                    barrier([dkv] + extra["ds"])
                phase["es"] = None

                def rest_s():
                    dense_fm(nout_groups, oT, roT, 4, resid_evac(1, 2, 1, 4))
                    layer_norm_T(1, 0, 1, 4, mod=(1, 3, 4, 1))
                    ffn(1, 1, 4)
                    layer_norm_T(1, 2, 3, 4, mod=None)
                    store_x_tile(y_s[0:4, :], 4)
                run_phase(rest_s)
            barrier([dio, dkv])
            for ds in (dkv,):
                if ds.n:
                    SP.obj.wait_ge(ds.sem, ds.n)
        dump("modT", modT[:, :, :, :], [rmod]); dump("vT_ln", vT_ln[:, :], [rvt]); dump("vT_g", vT_g[:, :], [rvt])

        for ds in (dio, dst, dbgd["sem"]):
            if ds is not None and ds.n:
                SP.obj.wait_ge(ds.sem, ds.n)
        print(f"[build] ops={fw.nops} waits={fw.nwaits} sbuf_left={nc.sbuf_bytes_remaining}")
    return nc


def make_in_map(inputs, core, consts):
    b = core % 4; s = core
    f = lambda a: np.ascontiguousarray(a, dtype=np.float32)
    m = {}
    m["xp"] = f(inputs["x_prompt"][b]); m["xs"] = f(inputs["x_sample"][s])
    m["cvec"] = f(np.concatenate([inputs["c_prompt"][b].reshape(16, 128), inputs["c_sample"][s].reshape(16, 128)], 0))
    rows = []
    for l in range(2):
        for nm in ("ln_mix_g", "ln_mix_b", "ln_ffn_g", "ln_ffn_b"):
            rows.append(inputs[nm][l].reshape(16, 128))
    m["vtab_ln"] = f(np.concatenate(rows, 0))
    m["vtab_ada"] = f(inputs["ada_b"].reshape(2, 96, 128))
    m["vtab_g"] = f(inputs["gla_norm_g"][0].reshape(16, 128))
    m["ada_w"] = f(inputs["ada_w"])
    m["gla_w_in"] = f(inputs["gla_w_in"][0]); m["gla_w_out"] = f(inputs["gla_w_out"][0])
    wa = np.zeros((33, 1024), np.float32); wa[0:16] = inputs["gla_w_alpha"][0]; wa[32] = inputs["gla_b_alpha"][0]
    m["gla_wa"] = wa
    m["ffn_w_in"] = f(inputs["ffn_w_in"]); m["ffn_w_out"] = f(inputs["ffn_w_out"])
    m["state_in"] = f(inputs["state_gla"][s, 0])
    m["nsa_w_in"] = f(inputs["nsa_w_in"][0]); m["nsa_w_out"] = f(inputs["nsa_w_out"][0])
    m["nsa_bg"] = f(inputs["nsa_b_gate"][0].reshape(1, 48))
    m["cache_cmp"] = inputs["cache_cmp_kv"].reshape(1280 * 128, 1024)
    m["cache_slc"] = inputs["cache_slc_kv"].reshape(1280 * 128, 1024)
    m["winbuf"] = f(inputs["cache_win_kv"][s, 0].reshape(512, 1024))
    m["ptab"] = np.ascontiguousarray(inputs["page_table"][s].reshape(1, 128), dtype=np.int32)
    m["iota_in"] = np.arange(128, dtype=np.int32).reshape(128, 1)
    for k, v in consts.items():
        m["c_" + k] = v
    return m


_PROGRAM = {}


def kernel(**inputs):
    inputs = {k: np.asarray(v) for k, v in inputs.items()}
    if "nc" not in _PROGRAM:
        _PROGRAM["nc"] = build_program(n_ptiles=4, do_sample=True, stage="full", dbg=False)
    nc = _PROGRAM["nc"]
    consts = _const_tables()
    n = 8
    in_maps = [make_in_map(inputs, c, consts) for c in range(n)]
    res = run_bass_kernel_spmd(nc, in_maps, core_ids=list(range(n)))
    r = res.results
    f32 = np.float32
    y_prompt = np.stack([np.asarray(r[b]["y_p"], f32) for b in range(4)], 0)
    y_sample = np.stack([np.asarray(r[s]["y_s"], f32) for s in range(8)], 0)
    gla_p = np.stack([np.asarray(r[b]["st_p"], f32) for b in range(4)], 0)[:, None]
    gla_s = np.stack([np.asarray(r[s]["st_s"], f32) for s in range(8)], 0)[:, None]
    kv = lambda name, idx, rows: np.stack([np.asarray(r[i][name], f32).reshape(rows, 2, 4, 128) for i in idx], 0)[:, None]
    cmp_p = kv("cmp_p", range(4), 2048); cmp_s = kv("cmp_s", range(8), 4)
    slc_p = kv("slc_p", range(4), 2048); slc_s = kv("slc_s", range(8), 4)
    win_p = kv("win_p", range(4), 512); win_s = kv("win_s", range(8), 512)
    return (y_prompt, y_sample, gla_p, gla_s, cmp_p, cmp_s, slc_p, slc_s, win_p, win_s)
```

```python
import contextlib
import numpy as np
import concourse.bass as bass
import concourse.mybir as mybir
from concourse.bass_utils import run_bass_kernel_spmd

F32 = mybir.dt.float32; BF16 = mybir.dt.bfloat16; I32 = mybir.dt.int32
AF = mybir.ActivationFunctionType
ALU = mybir.AluOpType
AX = mybir.AxisListType

D = 2048; NCH = 16; T = 2048; DFF = 5632
GLA_IN_W = 6160; NSA_IN_W = 5168
ALPHA = 4.0 ** 0.25
EPS = 1e-5
EPS_LN = EPS / (ALPHA * ALPHA)
NEG = -30000.0
WSLOT_ELEMS = 4096
NSLOT = 4
import os
SYNC_RAW_DIST = int(os.environ.get('SYNC_RAW_DIST', '1'))
SYNC_WAW = int(os.environ.get('SYNC_WAW', '1'))
SYNC_PE = int(os.environ.get('SYNC_PE', '0'))


class Eng:
    def __init__(s, fw, name, obj):
        s.name = name; s.obj = obj; s.n = 0; s.known = {}
        s.sem = fw.es.enter_context(fw.nc.semaphore("pg_" + name))


class DSem:
    def __init__(s, fw, name, serial=True):
        s.sem = fw.es.enter_context(fw.nc.semaphore("ds_" + name)); s.n = 0; s.serial = serial


class Res:
    __slots__ = ("name", "w", "rd")

    def __init__(s, name=""):
        s.name = name; s.w = None; s.rd = {}


class FW:
    def __init__(s, nc, es):
        s.nc = nc; s.es = es
        s.PE = Eng(s, "pe", nc.tensor); s.ACT = Eng(s, "act", nc.scalar)
        s.DVE = Eng(s, "dve", nc.vector); s.POOL = Eng(s, "pool", nc.gpsimd)
        s.SP = Eng(s, "sp", nc.sync)
        s.nops = 0; s.nwaits = 0

    def sb(s, name, shape, dt):
        return s.es.enter_context(s.nc.sbuf_tensor(name, list(shape), dt))

    def ps(s, name, shape, dt=F32):
        return s.es.enter_context(s.nc.psum_tensor(name, list(shape), dt))

    def _wait(s, E, tok, force=False):
        kind, k, v = tok
        if kind == 'e' and k is E and not force:
            return
        kk = id(k)
        if E.known.get(kk, 0) >= v:
            return
        E.obj.wait_ge(k.sem, v); E.known[kk] = v; s.nwaits += 1

    def _deps(s, E, R, W):
        for r in R:
            if r.w is not None:
                near = (r.w[0] == 'e' and r.w[1] is E and E.name != "pe" and (E.n - r.w[2]) <= SYNC_RAW_DIST)
                s._wait(E, r.w, force=near)
        for w in W:
            if w.w is not None:
                near = (SYNC_WAW and w.w[0] == 'e' and w.w[1] is E and (E.name != "pe" or SYNC_PE) and (E.n - w.w[2]) <= 2)
                s._wait(E, w.w, force=near)
            for t in w.rd.values():
                near = (SYNC_WAW and t[0] == 'e' and t[1] is E and (E.name != "pe" or SYNC_PE) and (E.n - t[2]) <= 2)
                s._wait(E, t, force=near)

    def op(s, E, fn, R=(), W=(), RS=()):
        s._deps(E, R, W)
        for r in RS:
            if r.w is not None:
                s._wait(E, r.w, force=True)
        ins = fn(E.obj)
        E.n += 1; ins.then_inc(E.sem, 1); s.nops += 1
        tok = ('e', E, E.n)
        for r in R:
            r.rd[id(E)] = tok
        for r in RS:
            r.rd[id(E)] = tok
        for w in W:
            w.w = tok; w.rd = {}
        return ins

    def dma(s, Q, dsem, out, in_, R=(), W=()):
        s._deps(Q, R, W)
        if dsem.serial and dsem.n > 0:
            s._wait(Q, ('d', dsem, dsem.n))
        ins = Q.obj.dma_start(out=out, in_=in_)
        dsem.n += 16; ins.then_inc(dsem.sem, 16)
        tok = ('d', dsem, dsem.n)
        for r in R:
            r.rd[id(dsem)] = tok
        for w in W:
            w.w = tok; w.rd = {}
        return ins

    def finish(s, E, resources):
        for r in resources:
            if r.w is not None:
                s._wait(E, r.w)
            for t in r.rd.values():
                s._wait(E, t)


def _const_tables():
    j = np.arange(128)[:, None]; i = np.arange(128)[None, :]
    c = {}
    c["tri"] = np.where(j <= i, -1.0 / 16.0, 0.0).astype(np.float32)
    c["caus01"] = (j <= i).astype(np.float32)
    c["causb"] = np.tile(np.where(j <= i, 0.0, NEG), (1, 4)).astype(np.float32)
    c["winb"] = np.tile(np.where(j > i, 0.0, NEG), (1, 4)).astype(np.float32)
    key = np.arange(2048)[None, :]; blk = np.arange(32)[:, None]
    c["eall"] = (key // 64 == blk).astype(np.float32)
    kk = np.arange(128)[:, None]; cc = np.arange(62)[None, :]
    c["avgA"] = np.where(cc == 30 + (kk >= 64), 1.0 / 64.0, 0.0).astype(np.float32)
    ii = np.arange(128)[:, None]; m = np.arange(62)[None, :]
    fl = (ii + 1) // 64
    ok = (m - 30) <= fl - 1
    c["Tc"] = np.where(ok, 0.0, NEG).astype(np.float32)
    c["Tm"] = ok.astype(np.float32)
    d = (m - 30) - (ii >= 64)
    c["Tf"] = np.where(d > 0, -1e4, np.where(d >= -1, 1e4, 0.0)).astype(np.float32)
    k1 = np.arange(128)[None, :]
    c["h2"] = np.stack([(k1[0] < 64), (k1[0] >= 64)]).astype(np.float32)
    jq = np.arange(4)[:, None]; qq = np.tile(np.arange(4), 4)[None, :]
    c["cn"] = (jq <= qq).astype(np.float32)
    rr = np.arange(128)[:, None]
    c["wf"] = (rr > qq).astype(np.float32)
    cb = np.arange(254)[None, :]
    c["avgB"] = np.where(cb == 126 + (kk >= 64), 1.0 / 64.0, 0.0).astype(np.float32)
    return c


def build_program(n_ptiles=4, do_sample=True, stage="full", dbg=False):
    nc = bass.Bass("TRN2", target_bir_lowering=False)
    dram = {}

    def din(name, shape, dt=F32):
        dram[name] = nc.dram_tensor(name, list(shape), dt, kind="ExternalInput").ap()
        return dram[name]

    def dout(name, shape, dt=F32):
        dram[name] = nc.dram_tensor(name, list(shape), dt, kind="ExternalOutput").ap()
        return dram[name]

    xp = din("xp", [T, D]); xs = din("xs", [4, D])
    cvec = din("cvec", [32, 128])
    vtab_ln = din("vtab_ln", [128, 128])
    vtab_ada = din("vtab_ada", [2, 96, 128])
    vtab_g = din("vtab_g", [16, 128])
    ada_w = din("ada_w", [2, D, 6 * D])
    gla_w_in = din("gla_w_in", [D, GLA_IN_W]); gla_w_out = din("gla_w_out", [D, D])
    gla_wa = din("gla_wa", [33, 1024])
    ffn_w_in = din("ffn_w_in", [2, D, 2 * DFF]); ffn_w_out = din("ffn_w_out", [2, DFF, D])
    state_in = din("state_in", [4, 256, 512])
    nsa_w_in = din("nsa_w_in", [D, NSA_IN_W]); nsa_w_out = din("nsa_w_out", [D, D])
    nsa_bg = din("nsa_bg", [1, 48])
    cache_cmp = din("cache_cmp", [1280 * 128, 1024]); cache_slc = din("cache_slc", [1280 * 128, 1024])
    winbuf = din("winbuf", [512, 1024]); ptab = din("ptab", [1, 128], I32)
    selscr = nc.dram_tensor("selscr", [4, 4, 256], F32).ap()
    iota_in = din("iota_in", [128, 1], I32)
    ctab = {k: din("c_" + k, list(v.shape)) for k, v in _const_tables().items()}

    y_p = dout("y_p", [T, D]); y_s = dout("y_s", [4, D])
    st_p = dout("st_p", [4, 256, 512]); st_s = dout("st_s", [4, 256, 512])
    cmp_p = dout("cmp_p", [T, 1024]); slc_p = dout("slc_p", [T, 1024]); win_p = dout("win_p", [512, 1024])
    cmp_s = dout("cmp_s", [4, 1024]); slc_s = dout("slc_s", [4, 1024]); win_s = dout("win_s", [512, 1024])

    es = contextlib.ExitStack()
    with es:
        fw = FW(nc, es)
        PE, ACT, DVE, POOL, SP = fw.PE, fw.ACT, fw.DVE, fw.POOL, fw.SP
        sb = fw.sb
        phase = {"es": None}
        dbgd = {"sem": None, "done": set()}

        def dump(name, ap, R=()):
            if not dbg or name in dbgd["done"]:
                return
            dbgd["done"].add(name)
            if dbgd["sem"] is None:
                dbgd["sem"] = DSem(fw, "dbg")
            shp = list(ap.shape)
            o = nc.dram_tensor("dbg_" + name, shp, ap.dtype, kind="ExternalOutput").ap()
            fw.dma(SP, dbgd["sem"], o, ap, R=list(R))
            for E in (PE, ACT, DVE):
                fw._wait(E, ('d', dbgd["sem"], dbgd["sem"].n))

        def psb(name, shape, dt):
            phase["n"] = phase.get("n", 0) + 1
            return phase["es"].enter_context(nc.sbuf_tensor(f"{name}_{phase['n']}", list(shape), dt))

        def barrier(dsems=()):
            engs = (PE, ACT, DVE)
            snap = [(E, E.n) for E in engs]
            for E in engs:
                for (O, n) in snap:
                    if O is not E and n > 0:
                        fw._wait(E, ('e', O, n))
                for ds in dsems:
                    if ds.n:
                        fw._wait(E, ('d', ds, ds.n))

        pb = [fw.ps(f"pb{i}", [128, 512], F32) for i in range(8)]
        rpb = [Res(f"pb{i}") for i in range(8)]

        def pbf(i):
            return pb[i]

        def pbb(i):
            return pb[i][:].bitcast(BF16)

        identf = sb("identf", [128, 128], F32); identb = sb("identb", [128, 128], BF16); rid = Res("ident")
        onesb = sb("onesb", [128, 128], BF16); rones = Res("ones")
        onesf = sb("onesf", [128, 128], F32)
        tri = sb("tri", [128, 128], F32); caus01 = sb("caus01", [128, 128], F32); rmask = Res("masks")
        dconst = DSem(fw, "const")
        fw.op(POOL, lambda e: e.memset(identf[:], 1.0), W=[rid])
        fw.op(POOL, lambda e: e.affine_select(out=identf[:], in_=identf[:], pattern=[[-1, 128]], compare_op=ALU.is_equal,
                                              fill=0.0, base=0, channel_multiplier=1), R=[rid], W=[rid])
        fw.op(DVE, lambda e: e.tensor_copy(out=identb[:], in_=identf[:]), R=[rid], W=[rid])
        fw.op(DVE, lambda e: e.memset(onesb[:], 1.0), W=[rones])
        fw.op(DVE, lambda e: e.memset(onesf[:], 1.0), W=[rones])
        fw.dma(SP, dconst, tri[:], ctab["tri"][:, :], W=[rmask])
        fw.dma(SP, dconst, caus01[:], ctab["caus01"][:, :], W=[rmask])

        vT_ln = sb("vT_ln", [128, 128], F32); vT_g = sb("vT_g", [128, 16], F32); rvt = Res("vt")
        adab = sb("adab", [128, 2, 96], F32)
        cT = sb("cT", [128, 32], F32); scT = sb("scT", [128, 16, 2], BF16); rct = Res("cT")
        ld0 = sb("ld0", [128, 128], F32); ld1 = sb("ld1", [128, 128], F32); rld0 = Res(); rld1 = Res()
        dld = DSem(fw, "ld")

        def load_T(dst_ap, src_ap, rows, rdst, stg, rstg, bank):
            fw.dma(SP, dld, stg[0:rows, :], src_ap, W=[rstg])
            fw.op(PE, lambda e: e.transpose(out=pbf(bank)[:, 0:rows], in_=stg[0:rows, :], identity=identf[0:rows, 0:rows]),
                  R=[rstg, rid], W=[rpb[bank]])
            fw.op(DVE, lambda e: e.tensor_copy(out=dst_ap, in_=pbf(bank)[:, 0:rows]), R=[rpb[bank]], W=[rdst])

        load_T(vT_ln[:, :], vtab_ln[:, :], 128, rvt, ld0, rld0, 0)
        load_T(vT_g[:, :], vtab_g[:, :], 16, rvt, ld1, rld1, 1)
        load_T(adab[:, 0, :], vtab_ada[0, :, :], 96, rvt, ld0, rld0, 0)
        load_T(adab[:, 1, :], vtab_ada[1, :, :], 96, rvt, ld1, rld1, 1)
        load_T(cT[:, :], cvec[:, :], 32, rct, ld0, rld0, 0)
        fw.op(ACT, lambda e: e.activation(out=scT[:].rearrange("p c v -> p v c"), in_=cT[:].rearrange("p (v c) -> p v c", v=2),
                                          func=AF.Silu), R=[rct], W=[rct])

        wsl = [sb(f"wsl{i}", [128, WSLOT_ELEMS], BF16) for i in range(NSLOT)]
        rws = [Res(f"wsl{i}") for i in range(NSLOT)]
        dws = [DSem(fw, f"w{i}", serial=False) for i in range(NSLOT)]
        wstate = {"next": 0, "issued": {}}

        def wissue(g):
            if g is None or g[0] in wstate["issued"]:
                return
            key, kch, nct, parts = g
            assert kch * nct <= WSLOT_ELEMS
            si = wstate["next"]; wstate["next"] = (si + 1) % NSLOT
            view = wsl[si][:, 0:kch * nct].rearrange("p (k c) -> p k c", k=kch)
            for (ap, co) in parts:
                ncols = ap.shape[1]
                fw.dma(POOL, dws[si], view[:, :, co:co + ncols], ap.rearrange("(k p) c -> p k c", p=128), W=[rws[si]])
            wstate["issued"][key] = si

        def wget(g):
            wissue(g)
            si = wstate["issued"].pop(g[0])
            key, kch, nct, parts = g
            return si, wsl[si][:, 0:kch * nct].rearrange("p (k c) -> p k c", k=kch)

        def wgroup(key, W2d, col0, ncols, kch=NCH, row0=0):
            return (key, kch, ncols, [(W2d[row0:row0 + kch * 128, col0:col0 + ncols], 0)])

        dense_banks = [0, 1, 2, 3]
        dstate = {"i": 0}

        def next_bank():
            b = dense_banks[dstate["i"] % len(dense_banks)]; dstate["i"] += 1
            return b

        def dense_fm(groups, inT, rin, ntok, evac, nxt=None):
            nxl = list(nxt) if isinstance(nxt, list) else ([nxt] if nxt is not None else [])
            seq = list(groups) + nxl
            for gi, g in enumerate(groups):
                si, wv = wget(g)
                for a in range(1, NSLOT):
                    if gi + a < len(seq):
                        wissue(seq[gi + a])
                key, kch, nct, parts = g
                for c0 in range(0, nct, 128):
                    M = min(128, nct - c0)
                    b = next_bank()
                    for k in range(kch):
                        fw.op(PE, lambda e: e.matmul(pbf(b)[0:M, 0:ntok], wv[:, k, c0:c0 + M], inT[:, k, 0:ntok],
                                                     start=(k == 0), stop=(k == kch - 1)),
                              R=[rws[si], rin], W=[rpb[b]])
                    evac(key, c0 // 128, pbf(b)[0:M, 0:ntok], rpb[b])

        def dense_tm(groups, inT, rin, ntok, evac, nxt=None):
            nst = (ntok + 127) // 128
            nxl = list(nxt) if isinstance(nxt, list) else ([nxt] if nxt is not None else [])
            seq = list(groups) + nxl
            for gi, g in enumerate(groups):
                si, wv = wget(g)
                for a in range(1, NSLOT):
                    if gi + a < len(seq):
                        wissue(seq[gi + a])
                key, kch, nct, parts = g
                for st in range(nst):
                    nt = min(128, ntok - st * 128)
                    b = next_bank()
                    for k in range(kch):
                        fw.op(PE, lambda e: e.matmul(pbf(b)[0:nt, 0:nct], inT[:, k, st * 128:st * 128 + nt], wv[:, k, 0:nct],
                                                     start=(k == 0), stop=(k == kch - 1)),
                              R=[rws[si], rin], W=[rpb[b]])
                    evac(key, st, pbf(b)[0:nt, 0:nct], rpb[b], nt)

        modT = sb("modT", [128, 2, 96, 2], F32); rmod = Res("mod")

        def ada_evac(l):
            def f(key, ci, ps, rps):
                c = key[2] * 2 + ci
                fw.op(DVE, lambda e: e.tensor_scalar(out=modT[:, l, c, :], in0=ps, scalar1=adab[:, l, c:c + 1], scalar2=None,
                                                     op0=ALU.add), R=[rps, rvt], W=[rmod])
            return f

        def ada_groups(l):
            return [wgroup(("ada", l, gi), ada_w[l], gi * 256, 256) for gi in range(48)]

        def ada_post(l):
            for v in (1, 4):
                fw.op(DVE, lambda e: e.tensor_scalar(out=modT[:, l, v * 16:(v + 1) * 16, :], in0=modT[:, l, v * 16:(v + 1) * 16, :],
                                                     scalar1=1.0, scalar2=None, op0=ALU.add), R=[rmod], W=[rmod])
            for v in (2, 5):
                fw.op(DVE, lambda e: e.tensor_scalar(out=modT[:, l, v * 16:(v + 1) * 16, :], in0=modT[:, l, v * 16:(v + 1) * 16, :],
                                                     scalar1=1.0 / ALPHA, scalar2=None, op0=ALU.mult), R=[rmod], W=[rmod])

        xT = sb("xT", [128, NCH, 512], F32); rxT = Res("xT")
        hT = sb("hT", [128, NCH, 512], BF16); rhT = Res("hT")
        oT = hT; roT = rhT
        iobuf = sb("iobuf", [128, D], F32); rio = Res("iobuf"); dio = DSem(fw, "io")
        kvstag = sb("kvstag", [128, 2, 256], F32)
        epsln = sb("epsln", [128, 1], F32); epsb = sb("epsb", [128, 1], F32)
        fw.op(DVE, lambda e: e.memset(epsln[:], EPS_LN), W=[rones])
        fw.op(DVE, lambda e: e.memset(epsb[:], EPS), W=[rones])

        def load_x_tile(src, ntok):
            nst = (ntok + 127) // 128
            for st in range(nst):
                nt = min(128, ntok - st * 128)
                fw.dma(SP, dio, iobuf[0:nt, :], src[st * 128:st * 128 + nt, :], W=[rio])
                for c4 in range(4):
                    b = next_bank()
                    for cc in range(4):
                        c = c4 * 4 + cc
                        fw.op(PE, lambda e: e.transpose(out=pbf(b)[:, cc * 128:cc * 128 + nt], in_=iobuf[0:nt, c * 128:(c + 1) * 128],
                                                        identity=identf[0:nt, 0:nt]), R=[rio, rid], W=[rpb[b]])
                    src_v = pbf(b)[:, :].rearrange("p (a t) -> p a t", a=4)[:, :, 0:nt]
                    dst_v = xT[:, c4 * 4:(c4 + 1) * 4, st * 128:st * 128 + nt]
                    if c4 % 2 == 0:
                        fw.op(ACT, lambda e: e.activation(out=dst_v, in_=src_v, func=AF.Copy), R=[rpb[b]], W=[rxT])
                    else:
                        fw.op(DVE, lambda e: e.tensor_copy(out=dst_v, in_=src_v), R=[rpb[b]], W=[rxT])

        def store_x_tile(dst, ntok):
            nst = (ntok + 127) // 128
            for st in range(nst):
                nt = min(128, ntok - st * 128)
                for c4 in range(4):
                    b = next_bank()
                    for cc in range(4):
                        c = c4 * 4 + cc
                        fw.op(PE, lambda e: e.transpose(out=pbf(b)[0:nt, cc * 128:(cc + 1) * 128], in_=xT[:, c, st * 128:st * 128 + nt], identity=identf[:, :]),
                              R=[rxT, rid], W=[rpb[b]])
                    if c4 % 2 == 0:
                        fw.op(ACT, lambda e: e.activation(out=iobuf[0:nt, c4 * 512:(c4 + 1) * 512], in_=pbf(b)[0:nt, :], func=AF.Copy), R=[rpb[b]], W=[rio])
                    else:
                        fw.op(DVE, lambda e: e.tensor_copy(out=iobuf[0:nt, c4 * 512:(c4 + 1) * 512], in_=pbf(b)[0:nt, :]), R=[rpb[b]], W=[rio])
                fw.dma(SP, dio, dst[st * 128:st * 128 + nt, :], iobuf[0:nt, :], R=[rio])

        def modulate_from_x(l, vsh, vsc, grp, ntok):
            for c in range(NCH):
                fw.op(ACT, lambda e: e.activation(out=hT[:, c, 0:ntok], in_=xT[:, c, 0:ntok], func=AF.Identity,
                                                  scale=modT[:, l, vsc * 16 + c, grp:grp + 1], bias=modT[:, l, vsh * 16 + c, grp:grp + 1]),
                      R=[rxT, rmod], W=[rhT])

        def layer_norm_T(l_ln, vg, vb, ntok, mod=None):
            if phase.get("ln_owner") is not phase["es"]:
                phase["ln_owner"] = phase["es"]
                phase["ln"] = (psb("zt", [128, 2, 512], F32), [Res("zt0"), Res("zt1")], psb("lnst", [128, 3, 512], F32), Res("lnst"),
                               psb("ub", [128, 2, 2, 512], BF16), [Res("ub0"), Res("ub1")])
            zt, rzt, lnst, rlnst, ub, rub = phase["ln"]
            b1, b2 = 4, 5
            for c in range(NCH):
                u = c % 2
                fw.op(DVE, lambda e: e.tensor_copy(out=ub[:, u, 0, 0:ntok], in_=xT[:, c, 0:ntok]), R=[rxT], W=[rub[u]])
                fw.op(ACT, lambda e: e.activation(out=ub[:, u, 1, 0:ntok], in_=xT[:, c, 0:ntok], func=AF.Square), R=[rxT], W=[rub[u]])
                fw.op(PE, lambda e: e.matmul(pbf(b1)[:, 0:ntok], onesb[:, :], ub[:, u, 0, 0:ntok], start=(c == 0), stop=(c == NCH - 1)),
                      R=[rub[u], rones], W=[rpb[b1]])
                fw.op(PE, lambda e: e.matmul(pbf(b2)[:, 0:ntok], onesb[:, :], ub[:, u, 1, 0:ntok], start=(c == 0), stop=(c == NCH - 1)),
                      R=[rub[u], rones], W=[rpb[b2]])
            mean = lnst[:, 0, 0:ntok]; var = lnst[:, 1, 0:ntok]; rstd = lnst[:, 2, 0:ntok]
            fw.op(DVE, lambda e: e.tensor_scalar(out=mean, in0=pbf(b1)[:, 0:ntok], scalar1=1.0 / D, scalar2=None, op0=ALU.mult),
                  R=[rpb[b1]], W=[rlnst])
            fw.op(DVE, lambda e: e.tensor_tensor(out=var, in0=mean, in1=mean, op=ALU.mult), R=[rlnst], W=[rlnst])
            fw.op(DVE, lambda e: e.scalar_tensor_tensor(out=var, in0=pbf(b2)[:, 0:ntok], scalar=1.0 / D, in1=var, op0=ALU.mult, op1=ALU.subtract),
                  R=[rpb[b2], rlnst], W=[rlnst])
            fw.op(ACT, lambda e: e.activation(out=var, in_=var, func=AF.Sqrt, bias=epsln[:, 0:1]), R=[rlnst, rones], W=[rlnst])
            fw.op(DVE, lambda e: e.reciprocal(out=rstd, in_=var), R=[rlnst], W=[rlnst])
            for c in range(NCH):
                u = c % 2
                z = zt[:, u, 0:ntok]
                fw.op(DVE, lambda e: e.tensor_tensor(out=z, in0=xT[:, c, 0:ntok], in1=mean, op=ALU.subtract), R=[rxT, rlnst], W=[rzt[u]])
                fw.op(DVE, lambda e: e.tensor_tensor(out=z, in0=z, in1=rstd, op=ALU.mult), R=[rlnst, rzt[u]], W=[rzt[u]])
                gcol = vT_ln[:, (l_ln * 4 + vg) * 16 + c:(l_ln * 4 + vg) * 16 + c + 1]
                bcol = vT_ln[:, (l_ln * 4 + vb) * 16 + c:(l_ln * 4 + vb) * 16 + c + 1]
                fw.op(ACT, lambda e: e.activation(out=xT[:, c, 0:ntok], in_=z, func=AF.Identity, scale=gcol, bias=bcol),
                      R=[rzt[u], rvt], W=[rxT])
                if mod is not None:
                    l, vsh, vsc, grp = mod
                    fw.op(DVE, lambda e: e.tensor_scalar(out=hT[:, c, 0:ntok], in0=xT[:, c, 0:ntok],
                                                         scalar1=modT[:, l, vsc * 16 + c, grp:grp + 1], scalar2=modT[:, l, vsh * 16 + c, grp:grp + 1],
                                                         op0=ALU.mult, op1=ALU.add), R=[rxT, rmod], W=[rhT])

        def resid_evac(l, vgt, grp, ntok):
            def f(key, ci, ps, rps):
                c = key[2] * 2 + ci
                fw.op(DVE, lambda e: e.scalar_tensor_tensor(out=xT[:, c, 0:ntok], in0=ps, scalar=modT[:, l, vgt * 16 + c, grp:grp + 1],
                                                            in1=xT[:, c, 0:ntok], op0=ALU.mult, op1=ALU.add), R=[rps, rmod, rxT], W=[rxT])
            return f

        def ffn(l, grp, ntok, nxt=None):
            gT = psb("gT", [128, 22, 512], BF16); rgT = Res("gT")
            sa = psb("sa", [128, 2, 512], F32); rsa = [Res("sa0"), Res("sa1")]
            cnt = {"i": 0}
            for hh in range(2):
                groups = []
                for j in range(22):
                    c0 = hh * 2816 + j * 128
                    groups.append((("fi", l, hh, j), NCH, 256, [(ffn_w_in[l][:, c0:c0 + 128], 0), (ffn_w_in[l][:, DFF + c0:DFF + c0 + 128], 128)]))
                out_groups = [(("fo", l, hh, og), 22, 128, [(ffn_w_out[l][hh * 2816:(hh + 1) * 2816, og * 128:(og + 1) * 128], 0)]) for og in range(16)]

                def evac_in(key, ci, ps, rps):
                    j = key[3]; u = j % 2
                    if ci == 0:
                        fw.op(ACT, lambda e: e.activation(out=sa[:, u, 0:ntok], in_=ps, func=AF.Silu), R=[rps], W=[rsa[u]])
                    else:
                        fw.op(DVE, lambda e: e.tensor_tensor(out=gT[:, j, 0:ntok], in0=ps, in1=sa[:, u, 0:ntok], op=ALU.mult),
                              R=[rps, rsa[u]], W=[rgT])
                dense_fm(groups, hT, rhT, ntok, evac_in, nxt=out_groups[0:3])

                def evac_out(key, ci, ps, rps):
                    c = key[3]
                    fw.op(DVE, lambda e: e.scalar_tensor_tensor(out=xT[:, c, 0:ntok], in0=ps, scalar=modT[:, l, 5 * 16 + c, grp:grp + 1],
                                                                in1=xT[:, c, 0:ntok], op0=ALU.mult, op1=ALU.add), R=[rps, rmod, rxT], W=[rxT])
                dense_fm(out_groups, gT, rgT, ntok, evac_out, nxt=(nxt if hh == 1 else None))

        alr = sb("alr", [33, 512], F32); ralr = Res("alr")
        wa = sb("wa", [33, 1024], F32); rwa = Res("wa")
        fw.dma(SP, dconst, wa[:, :], gla_wa[:, :], W=[rwa])
        fw.op(DVE, lambda e: e.memset(alr[:, :], 0.0), W=[ralr])
        fw.op(DVE, lambda e: e.memset(alr[32:33, :], 1.0), W=[ralr])
        dst = DSem(fw, "state")
        gq_groups = [wgroup(("gq", 0, gi), gla_w_in, gi * 256, 256) for gi in range(8)] + [wgroup(("ga", 0, 0), gla_w_in, 6144, 16)]
        gv_groups = [wgroup(("gv", 0, gi), gla_w_in, 2048 + gi * 256, 256) for gi in range(16)]
        gout_groups = [wgroup(("go", 0, gi), gla_w_out, gi * 256, 256) for gi in range(8)]

        def gla_mixer(ntok, S, rS):
            qT = psb("qT", [128, 8, 512], BF16); kT = psb("kT", [128, 8, 512], BF16); rqk = Res("qk")
            vtm = psb("vtm", [128, 4, D], BF16); rvtm = Res("vtm")
            grtm = psb("grtm", [128, 4, D], BF16); rgr = Res("grtm")
            Sb = psb("Sb", [128, 2, 2, 512], BF16); rSb = [Res("Sb0"), Res("Sb1")]
            lsp = psb("lsp", [128, 1024], F32); rlsp = Res("lsp")
            bT = psb("bT", [128, 8, 128], F32); rbT = Res("bT")
            dd = psb("dd", [128, 8, 128], F32); rdd = Res("dd")
            ee = psb("ee", [128, 8, 128], F32); ree = Res("ee")
            qtl = psb("qtl", [128, 8, 128], BF16); qh = psb("qh", [128, 8, 128], BF16)
            kh = psb("kh", [128, 8, 128], BF16); kbT = psb("kbT", [128, 8, 128], BF16); rqq = Res("qq")
            kbtm = psb("kbtm", [128, 1024], BF16); rkb = Res("kbtm")
            dec = psb("dec", [128, 8], F32); rdec = Res("dec")
            aT = psb("aT", [128, 2, 128], BF16); raT = [Res("aT0"), Res("aT1")]
            ssq = psb("ssq", [128, 8], F32); rssq = Res("ssq")
            junk = psb("junk", [128, 512], BF16); rjunk = Res("junk")
            on = psb("on", [128, D], BF16); ron = Res("on")
            if ntok == 4:
                print("[build] gla phase sbuf_left", nc.sbuf_bytes_remaining)

            def qk_evac(key, ci, ps, rps):
                if key[0] == "ga":
                    fw.op(DVE, lambda e: e.tensor_copy(out=alr[0:16, 0:ntok], in_=ps), R=[rps], W=[ralr])
                    return
                c = key[2] * 2 + ci
                if c < 8:
                    fw.op(ACT, lambda e: e.activation(out=qT[:, c, 0:ntok], in_=ps, func=AF.Copy, scale=256.0 ** -0.5), R=[rps], W=[rqk])
                else:
                    fw.op(DVE, lambda e: e.tensor_copy(out=kT[:, c - 8, 0:ntok], in_=ps), R=[rps], W=[rqk])

            def vr_evac(key, st, ps, rps, nt):
                gi = key[2]
                if gi < 8:
                    fw.op(DVE, lambda e: e.tensor_copy(out=vtm[0:nt, st, gi * 256:(gi + 1) * 256], in_=ps), R=[rps], W=[rvtm])
                else:
                    fw.op(ACT, lambda e: e.activation(out=grtm[0:nt, st, (gi - 8) * 256:(gi - 7) * 256], in_=ps, func=AF.Silu), R=[rps], W=[rgr])

            dense_fm(gq_groups, hT, rhT, ntok, qk_evac, nxt=gv_groups[0:3])
            dense_tm(gv_groups, hT, rhT, ntok, vr_evac, nxt=gout_groups[0:3])

            for ci in range((ntok + 127) // 128):
                C = min(128, ntok - ci * 128); t0 = ci * 128
                mid = max(C // 2 - 1, 0)
                for hf in range(2):
                    b = 6 + hf
                    fw.op(PE, lambda e: e.matmul(pbf(b)[0:C, :], alr[:, t0:t0 + C], wa[:, hf * 512:(hf + 1) * 512], start=True, stop=True),
                          R=[ralr, rwa], W=[rpb[b]])
                    fw.op(ACT, lambda e: e.activation(out=lsp[0:C, hf * 512:(hf + 1) * 512], in_=pbf(b)[0:C, :], func=AF.Exp, scale=-1.0),
                          R=[rpb[b]], W=[rlsp])
                fw.op(ACT, lambda e: e.activation(out=lsp[0:C, :], in_=lsp[0:C, :], func=AF.Ln, bias=onesf[0:C, 0:1]), R=[rlsp, rones], W=[rlsp])
                for dc in range(8):
                    b = 6 + dc // 4
                    fw.op(PE, lambda e: e.matmul(pbf(b)[:, (dc % 4) * 128:(dc % 4) * 128 + C], lsp[0:C, dc * 128:(dc + 1) * 128], tri[0:C, 0:C],
                                                 start=True, stop=True), R=[rlsp, rmask], W=[rpb[b]])
                for hf in range(2):
                    b = 6 + hf
                    fw.op(DVE, lambda e: e.tensor_copy(out=bT[:, hf * 4:(hf + 1) * 4, 0:C],
                                                       in_=pbf(b)[:, :].rearrange("p (a t) -> p a t", a=4)[:, :, 0:C]), R=[rpb[b]], W=[rbT])
                q3 = qT[:, :, t0:t0 + C]; k3 = kT[:, :, t0:t0 + C]
                bmid = bT[:, :, mid:mid + 1].to_broadcast([128, 8, C]); blast = bT[:, :, C - 1:C].to_broadcast([128, 8, C])
                fw.op(ACT, lambda e: e.activation(out=ee[:, :, 0:C], in_=bT[:, :, 0:C], func=AF.Exp), R=[rbT], W=[ree])
                fw.op(DVE, lambda e: e.tensor_tensor(out=qtl[:, :, 0:C], in0=q3, in1=ee[:, :, 0:C], op=ALU.mult), R=[rqk, ree], W=[rqq])
                fw.op(ACT, lambda e: e.activation(out=dec[:, :], in_=bT[:, :, C - 1], func=AF.Exp), R=[rbT], W=[rdec])
                fw.op(DVE, lambda e: e.tensor_tensor(out=dd[:, :, 0:C], in0=bT[:, :, 0:C], in1=bmid, op=ALU.subtract), R=[rbT], W=[rdd])
                fw.op(ACT, lambda e: e.activation(out=ee[:, :, 0:C], in_=dd[:, :, 0:C], func=AF.Exp), R=[rdd], W=[ree])
                fw.op(DVE, lambda e: e.tensor_tensor(out=qh[:, :, 0:C], in0=q3, in1=ee[:, :, 0:C], op=ALU.mult), R=[rqk, ree], W=[rqq])
                fw.op(ACT, lambda e: e.activation(out=ee[:, :, 0:C], in_=dd[:, :, 0:C], func=AF.Exp, scale=-1.0), R=[rdd], W=[ree])
                fw.op(DVE, lambda e: e.tensor_tensor(out=kh[:, :, 0:C], in0=k3, in1=ee[:, :, 0:C], op=ALU.mult), R=[rqk, ree], W=[rqq])
                fw.op(DVE, lambda e: e.tensor_tensor(out=dd[:, :, 0:C], in0=bT[:, :, 0:C], in1=blast, op=ALU.subtract), R=[rbT], W=[rdd])
                fw.op(ACT, lambda e: e.activation(out=ee[:, :, 0:C], in_=dd[:, :, 0:C], func=AF.Exp, scale=-1.0), R=[rdd], W=[ree])
                fw.op(DVE, lambda e: e.tensor_tensor(out=kbT[:, :, 0:C], in0=k3, in1=ee[:, :, 0:C], op=ALU.mult), R=[rqk, ree], W=[rqq])
                for dc in range(8):
                    fw.op(PE, lambda e: e.transpose(out=pbb(6)[0:C, dc * 128:(dc + 1) * 128], in_=kbT[:, dc, 0:C], identity=identb[:, :]),
                          R=[rqq, rid], W=[rpb[6]])
                fw.op(DVE, lambda e: e.tensor_copy(out=kbtm[0:C, :], in_=pbb(6)[0:C, :]), R=[rpb[6]], W=[rkb])
                fw.op(DVE, lambda e: e.memset(ssq[:, :], 0.0), W=[rssq])
                for h in range(4):
                    u = h % 2
                    bo = 4 + u; ba = u
                    fw.op(ACT, lambda e: e.activation(out=Sb[:, u, :, :], in_=S[:, 2 * h:2 * h + 2, :], func=AF.Copy), R=[rS], W=[rSb[u]])
                    for half in range(2):
                        fw.op(PE, lambda e: e.matmul(pbf(ba)[0:C, 0:C], kh[:, 2 * h + half, 0:C], qh[:, 2 * h + half, 0:C],
                                                     start=(half == 0), stop=(half == 1)), R=[rqq], W=[rpb[ba]])
                    fw.op(DVE, lambda e: e.tensor_tensor(out=aT[0:C, u, 0:C], in0=pbf(ba)[0:C, 0:C], in1=caus01[0:C, 0:C], op=ALU.mult),
                          R=[rpb[ba], rmask], W=[raT[u]])
                    for half in range(2):
                        fw.op(PE, lambda e: e.matmul(pbf(bo)[0:C, :], qtl[:, 2 * h + half, 0:C], Sb[:, u, half, :], start=(half == 0), stop=False),
                              R=[rqq, rSb[u]], W=[rpb[bo]])
                    fw.op(PE, lambda e: e.matmul(pbf(bo)[0:C, :], aT[0:C, u, 0:C], vtm[0:C, ci, h * 512:(h + 1) * 512], start=False, stop=True),
                          R=[raT[u], rvtm], W=[rpb[bo]])
                    for half in range(2):
                        bs = 2 + half
                        fw.op(PE, lambda e: e.matmul(pbf(bs)[:, :], kbtm[0:C, (2 * h + half) * 128:(2 * h + half + 1) * 128],
                                                     vtm[0:C, ci, h * 512:(h + 1) * 512], start=True, stop=True), R=[rkb, rvtm], W=[rpb[bs]])
                        fw.op(DVE, lambda e: e.scalar_tensor_tensor(out=S[:, 2 * h + half, :], in0=S[:, 2 * h + half, :],
                                                                    scalar=dec[:, 2 * h + half:2 * h + half + 1], in1=pbf(bs)[:, :],
                                                                    op0=ALU.mult, op1=ALU.add), R=[rS, rdec, rpb[bs]], W=[rS])
                    fw.op(ACT, lambda e: e.activation(out=junk[0:C, :], in_=pbf(bo)[0:C, :], func=AF.Square, accum_out=ssq[0:C, h:h + 1]),
                          R=[rpb[bo]], W=[rjunk, rssq])
                    fw.op(ACT, lambda e: e.activation(out=ssq[0:C, 4 + h:5 + h], in_=ssq[0:C, h:h + 1], func=AF.Sqrt, scale=1.0 / 512, bias=epsb[0:C, 0:1]),
                          R=[rssq, rones], W=[rssq])
                    fw.op(DVE, lambda e: e.reciprocal(out=ssq[0:C, 4 + h:5 + h], in_=ssq[0:C, 4 + h:5 + h]), R=[rssq], W=[rssq])
                    fw.op(DVE, lambda e: e.scalar_tensor_tensor(out=on[0:C, h * 512:(h + 1) * 512], in0=pbf(bo)[0:C, :], scalar=ssq[0:C, 4 + h:5 + h],
                                                                in1=grtm[0:C, ci, h * 512:(h + 1) * 512], op0=ALU.mult, op1=ALU.mult),
                          R=[rpb[bo], rgr], W=[ron], RS=[rssq])
                dump("on", on[:, :], [ron]); dump("ssq", ssq[:, :], [rssq]); dump("bT", bT[:, :, :], [rbT]); dump("qT", qT[:, :, :], [rqk]); dump("kT", kT[:, :, :], [rqk])
                dump("vtm", vtm[:, 0, :], [rvtm]); dump("grtm", grtm[:, 0, :], [rgr]); dump("lsp", lsp[:, :], [rlsp]); dump("aT", aT[:, :, :], raT); dump("qh", qh[:, :, :], [rqq]); dump("kh", kh[:, :, :], [rqq])
                dump("alr", alr[:, :], [ralr]); dump("hT", hT[:, :, :], [rhT])
                for half in range(2):
                    b = 4 + half
                    for cc in range(8):
                        c = half * 8 + cc
                        fw.op(PE, lambda e: e.transpose(out=pbb(b)[:, cc * 128:cc * 128 + C], in_=on[0:C, c * 128:(c + 1) * 128], identity=identb[0:C, 0:C]),
                              R=[ron, rid], W=[rpb[b]])
                    fw.op(DVE, lambda e: e.tensor_tensor(out=oT[:, half * 8:(half + 1) * 8, t0:t0 + C],
                                                         in0=pbb(b)[:, :].rearrange("p (a t) -> p a t", a=8)[:, :, 0:C],
                                                         in1=vT_g[:, half * 8:(half + 1) * 8].unsqueeze(2).to_broadcast([128, 8, C]), op=ALU.mult),
                          R=[rpb[b], rvt], W=[roT])

        def run_phase(fn, dsems=()):
            with contextlib.ExitStack() as pes:
                phase["es"] = pes
                fn()
                barrier(dsems)
            phase["es"] = None

        if os.environ.get("SKIP_L0"):
            fw.op(DVE, lambda e: e.memset(modT[:, :, :, :], 0.5), W=[rmod])
        for l in range(2 if not os.environ.get("SKIP_L0") else 0):
            dense_fm(ada_groups(l), scT, rct, 2, ada_evac(l))
            ada_post(l)

        x1s = nc.dram_tensor("x1_scratch", [T + 128, D], F32).ap()
        with contextlib.ExitStack() as les:
            S = les.enter_context(nc.sbuf_tensor("S", [128, 8, 512], F32)); rS = Res("S")
            fw.op(DVE, lambda e: e.memset(S[:, :, :], 0.0), W=[rS])

            def layer0_tile(src, ntok, grp, dst):
                load_x_tile(src, ntok)
                modulate_from_x(0, 0, 1, grp, ntok)
                run_phase(lambda: gla_mixer(ntok, S, rS))

                def rest():
                    dense_fm(gout_groups, oT, roT, ntok, resid_evac(0, 2, grp, ntok))
                    if stage == "pre0":
                        store_x_tile(dst, ntok); return
                    layer_norm_T(0, 0, 1, ntok, mod=(0, 3, 4, grp))
                    if stage == "mix0":
                        store_x_tile(dst, ntok); return
                    ffn(0, grp, ntok)
                    layer_norm_T(0, 2, 3, ntok, mod=None)
                    store_x_tile(dst, ntok)
                run_phase(rest)

            for ti in range(n_ptiles if not os.environ.get("SKIP_L0") else 0):
                layer0_tile(xp[ti * 512:(ti + 1) * 512, :], 512, 0, (y_p if stage in ("l0", "mix0", "pre0") else x1s)[ti * 512:(ti + 1) * 512, :])
            for h in range(4):
                fw.dma(SP, dst, st_p[h].rearrange("(a p) e -> p a e", p=128), S[:, 2 * h:2 * h + 2, :], R=[rS])
            if do_sample and not os.environ.get("SKIP_L0"):
                for h in range(4):
                    fw.dma(SP, dst, S[:, 2 * h:2 * h + 2, :], state_in[h].rearrange("(a p) e -> p a e", p=128), W=[rS])
                layer0_tile(xs[:, :], 4, 1, (y_s[0:4, :] if stage in ("l0", "mix0", "pre0") else x1s[T:T + 4, :]))
                for h in range(4):
                    fw.dma(SP, dst, st_s[h].rearrange("(a p) e -> p a e", p=128), S[:, 2 * h:2 * h + 2, :], R=[rS])
            barrier([dst, dio])
        if stage in ("full", "mix1", "pre1"):
          with contextlib.ExitStack() as les:
            def lsb(name, shape, dt):
                return les.enter_context(nc.sbuf_tensor(name, list(shape), dt))
            KsT = lsb("KsT", [128, 4, T], BF16); KwT = lsb("KwT", [128, 4, 1024], BF16); rKs = Res("KsT"); rKw = Res("KwT")
            Vs = lsb("Vs", [128, 16, 4, 130], BF16); Vw = lsb("Vw", [128, 8, 4, 130], BF16); rVs = Res("Vs"); rVw = Res("Vw")
            kcacc = lsb("kcacc", [128, 4, 32], F32); vcacc = lsb("vcacc", [32, 512], F32); rkc = Res("kc"); rvc = Res("vc")
            kcT = lsb("kcT", [128, 4, 32], BF16); vcb = lsb("vcb", [32, 512], BF16)
            causb = lsb("causb", [128, 512], BF16); winb = lsb("winb", [128, 512], BF16); eall = lsb("eall", [32, T], BF16)
            avgA = lsb("avgA", [128, 62], BF16)
            Tc = lsb("Tc", [128, 62], F32); Tm = lsb("Tm", [128, 62], F32); Tf = lsb("Tf", [128, 62], F32)
            bg = lsb("bg", [128, 48], F32); rk1 = Res("l1const"); rk1p = Res("l1constp")
            dk1 = DSem(fw, "l1const"); dk1p = DSem(fw, "l1constp"); dkv = DSem(fw, "kvout")
            for (dst_t, nm, rows, cols) in ((causb, "causb", 128, 512), (winb, "winb", 128, 512), (eall, "eall", 32, T), (avgA, "avgA", 128, 62)):
                fw.dma(SP, dio, iobuf[0:rows, 0:cols], ctab[nm][:, :], W=[rio])
                fw.op(DVE, lambda e: e.tensor_copy(out=dst_t[:, :], in_=iobuf[0:rows, 0:cols]), R=[rio], W=[rk1p])
            fw.dma(SP, dk1, Tc[:, :], ctab["Tc"][:, :], W=[rk1])
            fw.dma(SP, dk1, Tm[:, :], ctab["Tm"][:, :], W=[rk1])
            fw.dma(SP, dk1, Tf[:, :], ctab["Tf"][:, :], W=[rk1])
            fw.dma(SP, dk1, bg[:, :], nsa_bg[0:1, :].partition_broadcast(128), W=[rk1])
            fw.op(DVE, lambda e: e.memset(Vs[:, :, :, :], 1.0), W=[rVs])
            fw.op(DVE, lambda e: e.memset(Vw[:, :, :, :], 1.0), W=[rVw])
            fw.op(DVE, lambda e: e.memset(kcacc[:, :, :], 0.0), W=[rkc])
            fw.op(DVE, lambda e: e.memset(vcacc[:, :], 0.0), W=[rvc])

            nq_groups = [wgroup(("nq", 1, gi), nsa_w_in, gi * 256, 256) for gi in range(8)]
            nkv_groups = [wgroup(("nkv", 1, gi), nsa_w_in, 2048 + gi * 256, 256) for gi in range(12)] + [wgroup(("ng", 1, 0), nsa_w_in, 5120, 48)]
            nout_groups = [wgroup(("no", 1, gi), nsa_w_out, gi * 256, 256) for gi in range(8)]
            kv_outs = [cmp_p, slc_p, win_p]

            def nsa_prompt_tile(ti):
                ntok = 512
                qTn = psb("qTn", [128, 16, 512], BF16); rq = Res("qTn")
                gsg = psb("gsg", [128, 4, 48], F32); rgs = Res("gsg")
                o_tm = psb("o_tm", [128, D], F32); ro = Res("o_tm")
                obf = psb("obf", [128, D], BF16); rob = Res("obf")
                stag = kvstag; rstag = [Res("stag0"), Res("stag1")]
                ktmp = psb("ktmp", [128, 2, 256], BF16); rkt = [Res("kt0"), Res("kt1")]
                sm = psb("sm", [128, 16, 32], F32); rsm = Res("sm")
                pcb = psb("pcb", [128, 16, 32], BF16); rpcb = Res("pcb")
                pcT = psb("pcT", [32, 16, 128], BF16); rpcT = Res("pcT")
                st16 = psb("st16", [128, 4, 16], F32); rst16 = Res("st16")
                imp = psb("imp", [128, 4, 32], F32); rimp = Res("imp")
                m8 = psb("m8", [128, 2, 8], F32); rm8 = Res("m8")
                sct = psb("sct", [128, 32], F32); rsct = Res("sct")
                selb = psb("selb", [128, 32], F32); rselb = Res("selb")
                selbT = psb("selbT", [32, 4, 128], BF16); rselT = Res("selbT")
                pT = psb("pT", [128, 2, 512], BF16); rpT = [Res("pT0"), Res("pT1")]
                coef = psb("coef", [128, 8], F32); rcoef = Res("coef")
                cnt = {"stag": 0, "kt": 0, "pt": 0}
                if ti == 0:
                    print("[build] nsa phase sbuf_left", nc.sbuf_bytes_remaining)

                def q_evac(key, ci, ps, rps):
                    hd = key[2] * 2 + ci
                    fw.op(ACT, lambda e: e.activation(out=qTn[:, hd, 0:ntok], in_=ps, func=AF.Copy, scale=128.0 ** -0.5), R=[rps], W=[rq])

                KVP = os.environ.get("KV_PARTS", "dma,cmp,kt,v,ng").split(",")

                def kv_evac(key, st, ps, rps, nt):
                    kt_g = ti * 4 + st
                    if key[0] == "ng":
                        fw.op(DVE, lambda e: e.tensor_tensor(out=gsg[:, st, :], in0=ps, in1=bg[:, :], op=ALU.add), R=[rps, rk1], W=[rgs])
                        fw.op(ACT, lambda e: e.activation(out=gsg[:, st, :], in_=gsg[:, st, :], func=AF.Sigmoid), R=[rgs], W=[rgs])
                        return
                    gi = key[2]; br = gi // 4; sub = gi % 4; isv = sub // 2; gp = sub % 2
                    u = cnt["stag"] % 2; cnt["stag"] += 1
                    fw.op(ACT, lambda e: e.activation(out=stag[:, u, :], in_=ps, func=AF.Copy), R=[rps], W=[rstag[u]])
                    src = stag[:, u, :]; rsrc = rstag[u]
                    if br < 2 or ti == 3:
                        row0 = (ti * 512 + st * 128) if br < 2 else st * 128
                        fw.dma(SP, dkv, kv_outs[br][row0:row0 + 128, sub * 256:(sub + 1) * 256], src, R=[rsrc])
                    if br == 0:
                        uk = cnt["kt"] % 2; cnt["kt"] += 1
                        fw.op(DVE, lambda e: e.tensor_copy(out=ktmp[:, uk, :], in_=src), R=[rsrc], W=[rkt[uk]])
                        a0 = 30 - 2 * kt_g
                        if isv == 0:
                            for gg in range(2):
                                g = gp * 2 + gg
                                fw.op(PE, lambda e: e.matmul(pbf(4)[:, 0:32], ktmp[:, uk, gg * 128:(gg + 1) * 128], avgA[:, a0:a0 + 32], start=True, stop=True),
                                      R=[rkt[uk], rk1p], W=[rpb[4]])
                                fw.op(DVE, lambda e: e.tensor_tensor(out=kcacc[:, g, :], in0=kcacc[:, g, :], in1=pbf(4)[:, 0:32], op=ALU.add),
                                      R=[rpb[4], rkc], W=[rkc])
                        else:
                            fw.op(PE, lambda e: e.matmul(pbf(5)[0:32, 0:256], avgA[:, a0:a0 + 32], ktmp[:, uk, :], start=True, stop=True),
                                  R=[rkt[uk], rk1p], W=[rpb[5]])
                            fw.op(DVE, lambda e: e.tensor_tensor(out=vcacc[:, gp * 256:(gp + 1) * 256], in0=vcacc[:, gp * 256:(gp + 1) * 256],
                                                                 in1=pbf(5)[0:32, 0:256], op=ALU.add), R=[rpb[5], rvc], W=[rvc])
                    else:
                        KT, rK, VV, rV = (KsT, rKs, Vs, rVs) if br == 1 else (KwT, rKw, Vw, rVw)
                        ks_ = kt_g if br == 1 else kt_g % 8
                        if isv == 0:
                            uk = cnt["kt"] % 2; cnt["kt"] += 1
                            fw.op(DVE, lambda e: e.tensor_copy(out=ktmp[:, uk, :], in_=src), R=[rsrc], W=[rkt[uk]])
                            for gg in range(2):
                                fw.op(PE, lambda e: e.transpose(out=pbb(6)[:, gg * 128:(gg + 1) * 128], in_=ktmp[:, uk, gg * 128:(gg + 1) * 128], identity=identb[:, :]),
                                      R=[rkt[uk], rid], W=[rpb[6]])
                            fw.op(ACT, lambda e: e.activation(out=KT[:, gp * 2:gp * 2 + 2, ks_ * 128:(ks_ + 1) * 128],
                                                              in_=pbb(6)[:, 0:256].rearrange("p (a t) -> p a t", a=2), func=AF.Copy), R=[rpb[6]], W=[rK])
                        else:
                            fw.op(DVE, lambda e: e.tensor_copy(out=VV[:, ks_, gp * 2:gp * 2 + 2, 0:128],
                                                               in_=src.rearrange("p (a t) -> p a t", a=2)), R=[rsrc], W=[rV])

                CUT = int(os.environ.get("L1_CUT", "9"))
                if CUT <= 1:
                    return
                dense_fm(nq_groups, hT, rhT, ntok, q_evac, nxt=nkv_groups[0] if CUT > 2 else nout_groups[0])
                if CUT <= 2:
                    return
                dense_tm(nkv_groups, hT, rhT, ntok, kv_evac, nxt=nout_groups[0:3])
                if CUT <= 3:
                    return
                fw.op(ACT, lambda e: e.activation(out=kcT[:, :, :], in_=kcacc[:, :, :], func=AF.Copy), R=[rkc], W=[rkc])
                fw.op(ACT, lambda e: e.activation(out=vcb[:, :], in_=vcacc[:, :], func=AF.Copy), R=[rvc], W=[rvc])

                for st in range(4 if not os.environ.get("L1_SKIP_ATTN") else 0):
                    qt = ti * 4 + st
                    qs = slice(st * 128, (st + 1) * 128)
                    m0 = 30 - 2 * qt
                    for hd in range(16):
                        fw.op(PE, lambda e: e.matmul(pbf(4)[:, hd * 32:(hd + 1) * 32], qTn[:, hd, qs], kcT[:, hd // 4, :], start=True, stop=True),
                              R=[rq, rkc], W=[rpb[4]])
                    ps3 = pbf(4)[:, :].rearrange("p (h j) -> p h j", h=16)
                    fw.op(DVE, lambda e: e.tensor_tensor(out=sm[:, :, :], in0=ps3, in1=Tc[:, m0:m0 + 32].unsqueeze(1).to_broadcast([128, 16, 32]), op=ALU.add),
                          R=[rpb[4], rk1], W=[rsm])
                    fw.op(DVE, lambda e: e.tensor_reduce(out=st16[:, 0, :], in_=sm[:, :, :], axis=AX.X, op=ALU.max), R=[rsm], W=[rst16])
                    fw.op(DVE, lambda e: e.tensor_tensor(out=sm[:, :, :], in0=sm[:, :, :], in1=st16[:, 0, :].unsqueeze(2).to_broadcast([128, 16, 32]), op=ALU.subtract),
                          R=[rsm, rst16], W=[rsm])
                    fw.op(ACT, lambda e: e.activation(out=sm[:, :, :], in_=sm[:, :, :], func=AF.Exp), R=[rsm], W=[rsm])
                    fw.op(DVE, lambda e: e.tensor_tensor(out=sm[:, :, :], in0=sm[:, :, :], in1=Tm[:, m0:m0 + 32].unsqueeze(1).to_broadcast([128, 16, 32]), op=ALU.mult),
                          R=[rsm, rk1], W=[rsm])
                    fw.op(DVE, lambda e: e.tensor_reduce(out=st16[:, 1, :], in_=sm[:, :, :], axis=AX.X, op=ALU.add), R=[rsm], W=[rst16])
                    fw.op(DVE, lambda e: e.tensor_scalar(out=st16[:, 1, :], in0=st16[:, 1, :], scalar1=1e-30, scalar2=None, op0=ALU.max), R=[rst16], W=[rst16])
                    fw.op(DVE, lambda e: e.reciprocal(out=st16[:, 2, :], in_=st16[:, 1, :]), R=[rst16], W=[rst16])
                    fw.op(DVE, lambda e: e.tensor_tensor(out=sm[:, :, :], in0=sm[:, :, :], in1=st16[:, 2, :].unsqueeze(2).to_broadcast([128, 16, 32]), op=ALU.mult),
                          R=[rsm, rst16], W=[rsm])
                    if qt >= 8:
                        fw.op(DVE, lambda e: e.tensor_reduce(out=imp[:, :, :], in_=sm[:, :, :].rearrange("p (g h) j -> p g j h", g=4), axis=AX.X, op=ALU.add),
                              R=[rsm], W=[rimp])
                        fw.op(DVE, lambda e: e.tensor_tensor(out=imp[:, :, :], in0=imp[:, :, :], in1=Tf[:, m0:m0 + 32].unsqueeze(1).to_broadcast([128, 4, 32]), op=ALU.add),
                              R=[rimp, rk1], W=[rimp])
                        fw.op(DVE, lambda e: e.memset(imp[:, :, 0:1], 1e4), W=[rimp])
                        for g in range(4):
                            fw.op(DVE, lambda e: e.max(out=m8[:, 0, :], in_=imp[:, g, :]), R=[rimp], W=[rm8])
                            fw.op(DVE, lambda e: e.match_replace(out=sct[:, :], in_to_replace=m8[:, 0, :], in_values=imp[:, g, :], imm_value=-1e9), R=[rimp, rm8], W=[rsct])
                            fw.op(DVE, lambda e: e.max(out=m8[:, 1, :], in_=sct[:, :]), R=[rsct], W=[rm8])
                            fw.op(DVE, lambda e: e.tensor_scalar(out=selb[:, :], in0=imp[:, g, :], scalar1=m8[:, 1, 7:8], scalar2=-NEG, op0=ALU.is_ge, op1=ALU.mult),
                                  R=[rimp], W=[rselb], RS=[rm8])
                            fw.op(DVE, lambda e: e.tensor_scalar(out=selb[:, :], in0=selb[:, :], scalar1=NEG, scalar2=None, op0=ALU.add), R=[rselb], W=[rselb])
                            fw.op(PE, lambda e: e.transpose(out=pbf(6)[0:32, 0:128], in_=selb[:, :], identity=identf[:, :]), R=[rselb, rid], W=[rpb[6]])
                            fw.op(ACT, lambda e: e.activation(out=selbT[:, g, :], in_=pbf(6)[0:32, 0:128], func=AF.Copy), R=[rpb[6]], W=[rselT])
                    gc = gsg[:, st, :].rearrange("p (h b) -> p h b", b=3)[:, :, 0:1].to_broadcast([128, 16, 32])
                    fw.op(DVE, lambda e: e.tensor_tensor(out=pcb[:, :, :], in0=sm[:, :, :], in1=gc, op=ALU.mult), R=[rsm, rgs], W=[rpcb])
                    for half in range(2):
                        b = 6 + half
                        for hh in range(8):
                            hd = half * 8 + hh
                            fw.op(PE, lambda e: e.transpose(out=pbb(b)[0:32, hh * 128:(hh + 1) * 128], in_=pcb[:, hd, :], identity=identb[:, :]),
                                  R=[rpcb, rid], W=[rpb[b]])
                        fw.op(ACT, lambda e: e.activation(out=pcT[:, half * 8:(half + 1) * 8, :], in_=pbb(b)[0:32, :].rearrange("p (a t) -> p a t", a=8), func=AF.Copy),
                              R=[rpb[b]], W=[rpcT])
                    for g in range(4):
                        for h in range(4):
                            fw.op(PE, lambda e: e.matmul(pbf(5)[:, h * 128:(h + 1) * 128], pcT[:, 4 * g + h, :], vcb[:, g * 128:(g + 1) * 128], start=True, stop=True),
                                  R=[rpcT, rvc], W=[rpb[5]])
                        fw.op(ACT, lambda e: e.activation(out=o_tm[:, g * 512:(g + 1) * 512], in_=pbf(5)[:, :], func=AF.Copy), R=[rpb[5]], W=[ro])
                    for (KT, rK, VV, rV, kts, bi) in ((KsT, rKs, Vs, rVs, list(range(0, qt + 1)), 1), (KwT, rKw, Vw, rVw, list(range(max(0, qt - 4), qt + 1)), 2)):
                        for g in range(4):
                            rhsQ = qTn[:, 4 * g:4 * g + 4, qs]
                            for ki, kt in enumerate(kts):
                                ksl = kt if bi == 1 else kt % 8
                                u = cnt["pt"] % 2; cnt["pt"] += 1
                                sb_ = u
                                mask = None
                                if kt == qt:
                                    mask = ("id", causb[:, :])
                                elif bi == 2 and kt == qt - 4:
                                    mask = ("id", winb[:, :])
                                elif bi == 1 and qt >= 8:
                                    mask = ("sel", None)
                                out3 = pbf(sb_)[:, :].rearrange("p (a t) -> p a t", a=4)
                                fw.op(PE, lambda e: e.matmul(out3, KT[:, g, ksl * 128:(ksl + 1) * 128], rhsQ, start=True, stop=(mask is None)),
                                      R=[rK, rq], W=[rpb[sb_]])
                                if mask is not None and mask[0] == "id":
                                    fw.op(PE, lambda e: e.matmul(pbf(sb_)[:, :], identb[:, :], mask[1], start=False, stop=True), R=[rid, rk1p], W=[rpb[sb_]])
                                elif mask is not None:
                                    fw.op(PE, lambda e: e.matmul(out3, eall[:, kt * 128:(kt + 1) * 128], selbT[:, g, :].unsqueeze(1).to_broadcast([32, 4, 128]),
                                                                 start=False, stop=True), R=[rk1p, rselT], W=[rpb[sb_]])
                                fw.op(ACT, lambda e: e.activation(out=pT[:, u, :], in_=pbf(sb_)[:, :], func=AF.Exp), R=[rpb[sb_]], W=[rpT[u]])
                                for h in range(4):
                                    ob = 2 + h
                                    fw.op(PE, lambda e: e.matmul(pbf(ob)[:, 0:130], pT[:, u, h * 128:(h + 1) * 128], VV[:, ksl, g, 0:130],
                                                                 start=(ki == 0), stop=(ki == len(kts) - 1)), R=[rpT[u], rV], W=[rpb[ob]])
                            for h in range(4):
                                ob = 2 + h; c0 = 0; hd = 4 * g + h
                                fw.op(DVE, lambda e: e.reciprocal(out=coef[:, h:h + 1], in_=pbf(ob)[:, c0 + 128:c0 + 129]), R=[rpb[ob]], W=[rcoef])
                                fw.op(DVE, lambda e: e.tensor_tensor(out=coef[:, 4 + h:5 + h], in0=coef[:, h:h + 1], in1=gsg[:, st, hd * 3 + bi:hd * 3 + bi + 1], op=ALU.mult),
                                      R=[rcoef, rgs], W=[rcoef])
                                fw.op(DVE, lambda e: e.scalar_tensor_tensor(out=o_tm[:, hd * 128:(hd + 1) * 128], in0=pbf(ob)[:, c0:c0 + 128], scalar=coef[:, 4 + h:5 + h],
                                                                            in1=o_tm[:, hd * 128:(hd + 1) * 128], op0=ALU.mult, op1=ALU.add),
                                      R=[rpb[ob], ro], W=[ro], RS=[rcoef])
                    dump("o_tm", o_tm[:, :], [ro]); dump("gsg", gsg[:, :, :], [rgs]); dump("sm", sm[:, :, :], [rsm]); dump("qTn", qTn[:, :, :], [rq])
                    dump("KsT", KsT[:, :, 0:512], [rKs]); dump("Vs", Vs[:, 0:4, :, :], [rVs]); dump("kcT", kcT[:, :, :], [rkc]); dump("vcb", vcb[:, :], [rvc]); dump("coef", coef[:, :], [rcoef])
                    fw.op(ACT, lambda e: e.activation(out=obf[:, :], in_=o_tm[:, :], func=AF.Copy), R=[ro], W=[rob])
                    for half in range(2):
                        b = 6 + half
                        for cc in range(8):
                            c = half * 8 + cc
                            fw.op(PE, lambda e: e.transpose(out=pbb(b)[:, cc * 128:(cc + 1) * 128], in_=obf[:, c * 128:(c + 1) * 128], identity=identb[:, :]),
                                  R=[rob, rid], W=[rpb[b]])
                        fw.op(DVE, lambda e: e.tensor_copy(out=oT[:, half * 8:(half + 1) * 8, qs], in_=pbb(b)[:, :].rearrange("p (a t) -> p a t", a=8)), R=[rpb[b]], W=[roT])

            def layer1_tile_prompt(ti):
                load_x_tile((xp if os.environ.get('SKIP_L0') else x1s)[ti * 512:(ti + 1) * 512, :], 512)
                modulate_from_x(1, 0, 1, 0, 512)
                run_phase(lambda: nsa_prompt_tile(ti), dsems=[dkv])

                def rest():
                    dense_fm(nout_groups, oT, roT, 512, resid_evac(1, 2, 0, 512))
                    if stage == "pre1":
                        store_x_tile(y_p[ti * 512:(ti + 1) * 512, :], 512); return
                    layer_norm_T(1, 0, 1, 512, mod=(1, 3, 4, 0))
                    if stage == "mix1":
                        store_x_tile(y_p[ti * 512:(ti + 1) * 512, :], 512); return
                    ffn(1, 0, 512)
                    layer_norm_T(1, 2, 3, 512, mod=None)
                    store_x_tile(y_p[ti * 512:(ti + 1) * 512, :], 512)
                run_phase(rest)

            for ti in range(n_ptiles):
                layer1_tile_prompt(ti)
            def nsa_sample_tile():
                NT_ = 4
                qS = psb("qS", [128, 16, 4], BF16); rqS = Res("qS")
                gs4 = psb("gs4", [128, 48], F32); rgs4 = Res("gs4")
                o4 = iobuf; ro4 = rio
                nkb = psb("nkb", [4, 3, 1024], BF16); rnkb = Res("nkb")
                stag = kvstag; rstag = [Res("sstag0"), Res("sstag1")]
                idxt = psb("idxt", [128, 128], I32); iot = psb("iot", [128, 1], I32); rpt = Res("ptab")
                cpage = psb("cpage", [128, 2, 1024], BF16); rcp = [Res("cp0"), Res("cp1")]
                dcp = [DSem(fw, "cp0", serial=False), DSem(fw, "cp1", serial=False)]
                kcS = psb("kcS", [128, 4, 256], BF16); vcS = psb("vcS", [128, 2, 512], BF16); rkcS = Res("kcS"); rvcS = Res("vcS")
                smS = psb("smS", [4, 4, 256], F32); rsmS = Res("smS")
                pcS = psb("pcS", [4, 4, 256], BF16); rpcS = Res("pcS")
                pTs = psb("pTs", [128, 8, 4], BF16); rpTs = Res("pTs")
                s4 = psb("s4", [4, 3, 4], F32); rs4 = Res("s4")
                scS = psb("scS", [4, 4, 264], F32); rscS = Res("scS")
                m8s = psb("m8s", [4, 2, 8], F32); rm8s = Res("m8s")
                scts = psb("scts", [4, 264], F32); rscts = Res("scts")
                sel01 = smS; rsel = rsmS
                sel2 = psb("sel2", [2, 2048], BF16); rsel2 = Res("sel2")
                Msb = psb("Msb", [128, 128, 16], BF16); rM = Res("Msb")
                KTp = psb("KTp", [128, 2, 512], BF16); rKTp = [Res("KTp0"), Res("KTp1")]
                Va = psb("Va", [128, 1, 4, 130], BF16); rVa = [Res("Va0"), Res("Va0b")]; rVa[1] = rVa[0]
                PT = psb("PT", [128, 2, 64], BF16); rPT = [Res("PT0"), Res("PT1")]
                accS = psb("accS", [16, 2, 4, 130], F32); racc = Res("accS")
                knT = psb("knT", [128, 2, 4, 4], BF16); rknT = Res("knT")
                vn = psb("vn", [4, 2, 4, 130], BF16); rvn = Res("vn")
                h2 = psb("h2", [2, 128], BF16); cn = psb("cn", [4, 16], BF16); wf = psb("wf", [128, 16], BF16); avgB = psb("avgB", [128, 254], BF16); rsc = Res("sconst")
                coef4 = psb("coef4", [4, 8], F32); rcoef4 = Res("coef4")
                dsm = DSem(fw, "smisc"); dsm2 = DSem(fw, "smisc2")
                cnt = {"stag": 0}
                for (dst_t, nm, rows, cols) in ((h2, "h2", 2, 128), (cn, "cn", 4, 16), (wf, "wf", 128, 16), (avgB, "avgB", 128, 254)):
                    fw.dma(SP, dio, iobuf[0:rows, 0:cols], ctab[nm][:, :], W=[rio])
                    fw.op(DVE, lambda e: e.tensor_copy(out=dst_t[:, :], in_=iobuf[0:rows, 0:cols]), R=[rio], W=[rsc])
                fw.dma(SP, dsm, idxt[:, :], ptab[0:1, :].partition_broadcast(128), W=[rpt])
                fw.dma(SP, dsm, iot[:, :], iota_in[:, :], W=[rpt])
                fw.op(DVE, lambda e: e.tensor_scalar(out=idxt[:, :], in0=idxt[:, :], scalar1=128, scalar2=iot[:, 0:1], op0=ALU.mult, op1=ALU.add), R=[rpt], W=[rpt])
                fw.op(DVE, lambda e: e.memset(Va[:, :, :, :], 1.0), W=[rVa[0]])
                fw.op(DVE, lambda e: e.memset(vn[:, :, :, :], 1.0), W=[rvn])
                fw.op(DVE, lambda e: e.memset(accS[:, :, :, :], 0.0), W=[racc])
                fw.dma(SP, dsm, win_s[0:508, :], winbuf[4:512, :])

                def q_evac(key, ci, ps, rps):
                    hd = key[2] * 2 + ci
                    fw.op(ACT, lambda e: e.activation(out=qS[:, hd, 0:4], in_=ps, func=AF.Copy, scale=128.0 ** -0.5), R=[rps], W=[rqS])

                def kv_evac(key, st, ps, rps, nt):
                    if key[0] == "ng":
                        fw.op(DVE, lambda e: e.tensor_tensor(out=gs4[0:4, :], in0=ps, in1=bg[0:4, :], op=ALU.add), R=[rps, rk1], W=[rgs4])
                        fw.op(ACT, lambda e: e.activation(out=gs4[0:4, :], in_=gs4[0:4, :], func=AF.Sigmoid), R=[rgs4], W=[rgs4])
                        return
                    gi = key[2]; br = gi // 4; sub = gi % 4
                    u = cnt["stag"] % 2; cnt["stag"] += 1
                    fw.op(ACT, lambda e: e.activation(out=stag[0:4, u, :], in_=ps, func=AF.Copy), R=[rps], W=[rstag[u]])
                    if br < 2:
                        fw.dma(SP, dkv, (cmp_s, slc_s)[br][0:4, sub * 256:(sub + 1) * 256], stag[0:4, u, :], R=[rstag[u]])
                    else:
                        fw.dma(SP, dkv, win_s[508:512, sub * 256:(sub + 1) * 256], stag[0:4, u, :], R=[rstag[u]])
                    fw.op(DVE, lambda e: e.tensor_copy(out=nkb[0:4, br, sub * 256:(sub + 1) * 256], in_=stag[0:4, u, :]), R=[rstag[u]], W=[rnkb])

                dense_fm(nq_groups, hT, rhT, 4, q_evac, nxt=nkv_groups[0:3])
                dense_tm(nkv_groups, hT, rhT, 4, kv_evac, nxt=nout_groups[0:3])

                for bi in (1, 2):
                    for g in range(4):
                        fw.op(PE, lambda e: e.transpose(out=pbb(6)[:, g * 4:g * 4 + 4], in_=nkb[0:4, bi, g * 128:(g + 1) * 128], identity=identb[0:4, 0:4]),
                              R=[rnkb, rid], W=[rpb[6]])
                    fw.op(ACT, lambda e: e.activation(out=knT[:, bi - 1, :, :], in_=pbb(6)[:, 0:16].rearrange("p (g t) -> p g t", g=4), func=AF.Copy), R=[rpb[6]], W=[rknT])
                    fw.op(DVE, lambda e: e.tensor_copy(out=vn[0:4, bi - 1, :, 0:128], in_=nkb[0:4, bi, 512:1024].rearrange("p (g d) -> p g d", g=4)), R=[rnkb], W=[rvn])

                def page_dma(cache, p, u):
                    fw._deps(POOL, [rpt], [rcp[u]])
                    ins = nc.gpsimd.indirect_dma_start(out=cpage[:, u, :], out_offset=None, in_=cache[:, :],
                                                       in_offset=bass.IndirectOffsetOnAxis(ap=idxt[:, p:p + 1], axis=0))
                    dcp[u].n += 16; ins.then_inc(dcp[u].sem, 16)
                    rcp[u].w = ('d', dcp[u], dcp[u].n); rcp[u].rd = {}
                    rpt.rd[id(dcp[u])] = rcp[u].w

                for p in range(128):
                    u = p % 2
                    page_dma(cache_cmp, p, u)
                    for g in range(4):
                        fw.op(PE, lambda e: e.matmul(pbf(g // 2)[:, (g % 2) * 256 + 2 * p:(g % 2) * 256 + 2 * p + 2], cpage[:, u, g * 128:(g + 1) * 128], avgA[:, 30:32],
                                                     start=True, stop=True), R=[rcp[u], rk1p], W=[rpb[g // 2]])
                    off = 126 - 2 * (p % 64)
                    fw.op(PE, lambda e: e.matmul(pbf(2 + p // 64)[:, :], avgB[:, off:off + 128], cpage[:, u, 512:1024], start=(p % 64 == 0), stop=(p % 64 == 63)),
                          R=[rcp[u], rsc], W=[rpb[2 + p // 64]])
                for b2 in range(2):
                    fw.op(ACT, lambda e: e.activation(out=kcS[:, 2 * b2:2 * b2 + 2, :], in_=pbf(b2)[:, :].rearrange("p (g j) -> p g j", g=2), func=AF.Copy), R=[rpb[b2]], W=[rkcS])
                    fw.op(DVE, lambda e: e.tensor_copy(out=vcS[:, b2, :], in_=pbf(2 + b2)[:, :]), R=[rpb[2 + b2]], W=[rvcS])
                fw.op(DVE, lambda e: e.memset(scS[:, :, :], 0.0), W=[rscS])
                for g in range(4):
                    for h in range(4):
                        fw.op(PE, lambda e: e.matmul(pbf(4 + h // 2)[0:4, (h % 2) * 256:(h % 2) * 256 + 256], qS[:, 4 * g + h, 0:4], kcS[:, g, :], start=True, stop=True),
                              R=[rqS, rkcS], W=[rpb[4 + h // 2]])
                    for b2 in range(2):
                        fw.op(DVE, lambda e: e.tensor_copy(out=smS[:, 2 * b2:2 * b2 + 2, :], in_=pbf(4 + b2)[0:4, :].rearrange("p (h j) -> p h j", h=2)), R=[rpb[4 + b2]], W=[rsmS])
                    fw.op(DVE, lambda e: e.tensor_reduce(out=s4[:, 0, :], in_=smS[:, :, :], axis=AX.X, op=ALU.max), R=[rsmS], W=[rs4])
                    fw.op(DVE, lambda e: e.tensor_tensor(out=smS[:, :, :], in0=smS[:, :, :], in1=s4[:, 0, :].unsqueeze(2).to_broadcast([4, 4, 256]), op=ALU.subtract), R=[rsmS, rs4], W=[rsmS])
                    fw.op(ACT, lambda e: e.activation(out=smS[:, :, :], in_=smS[:, :, :], func=AF.Exp), R=[rsmS], W=[rsmS])
                    fw.op(DVE, lambda e: e.tensor_reduce(out=s4[:, 1, :], in_=smS[:, :, :], axis=AX.X, op=ALU.add), R=[rsmS], W=[rs4])
                    fw.op(DVE, lambda e: e.reciprocal(out=s4[:, 2, :], in_=s4[:, 1, :]), R=[rs4], W=[rs4])
                    fw.op(DVE, lambda e: e.tensor_tensor(out=smS[:, :, :], in0=smS[:, :, :], in1=s4[:, 2, :].unsqueeze(2).to_broadcast([4, 4, 256]), op=ALU.mult), R=[rsmS, rs4], W=[rsmS])
                    fw.op(DVE, lambda e: e.tensor_reduce(out=scS[:, g, 0:256], in_=smS[:, :, :].rearrange("p h j -> p j h"), axis=AX.X, op=ALU.add), R=[rsmS], W=[rscS])
                    gc = gs4[0:4, :].rearrange("p (h b) -> p h b", b=3)[:, 4 * g:4 * g + 4, 0:1].to_broadcast([4, 4, 256])
                    fw.op(DVE, lambda e: e.tensor_tensor(out=pcS[:, :, :], in0=smS[:, :, :], in1=gc, op=ALU.mult), R=[rsmS, rgs4], W=[rpcS])
                    for h in range(4):
                        for c in range(2):
                            fw.op(PE, lambda e: e.transpose(out=pbb(6)[:, (h * 2 + c) * 4:(h * 2 + c) * 4 + 4], in_=pcS[0:4, h, c * 128:(c + 1) * 128], identity=identb[0:4, 0:4]),
                                  R=[rpcS, rid], W=[rpb[6]])
                    fw.op(ACT, lambda e: e.activation(out=pTs[:, :, :], in_=pbb(6)[:, 0:32].rearrange("p (a t) -> p a t", a=8), func=AF.Copy), R=[rpb[6]], W=[rpTs])
                    for h in range(4):
                        for c in range(2):
                            fw.op(PE, lambda e: e.matmul(pbf(7)[0:4, h * 128:(h + 1) * 128], pTs[:, h * 2 + c, :], vcS[:, c, g * 128:(g + 1) * 128], start=(c == 0), stop=(c == 1)),
                                  R=[rpTs, rvcS], W=[rpb[7]])
                    fw.op(ACT, lambda e: e.activation(out=o4[0:4, g * 512:(g + 1) * 512], in_=pbf(7)[0:4, :], func=AF.Copy), R=[rpb[7]], W=[ro4])
                for col in (0, 255, 256):
                    fw.op(DVE, lambda e: e.memset(scS[:, :, col:col + 1], 1e4), W=[rscS])
                for g in range(4):
                    fw.op(DVE, lambda e: e.max(out=m8s[:, 0, :], in_=scS[:, g, :]), R=[rscS], W=[rm8s])
                    fw.op(DVE, lambda e: e.match_replace(out=scts[:, :], in_to_replace=m8s[:, 0, :], in_values=scS[:, g, :], imm_value=-1e9), R=[rscS, rm8s], W=[rscts])
                    fw.op(DVE, lambda e: e.max(out=m8s[:, 1, :], in_=scts[:, :]), R=[rscts], W=[rm8s])
                    fw.op(DVE, lambda e: e.tensor_scalar(out=sel01[:, g, :], in0=scS[:, g, 0:256], scalar1=m8s[:, 1, 7:8], scalar2=None, op0=ALU.is_ge),
                          R=[rscS], W=[rsel], RS=[rm8s])
                rscr = Res('selscr')
                fw.dma(SP, dsm, selscr[:, :, :], sel01[:, :, :], R=[rsel], W=[rscr])
                with nc.allow_non_contiguous_dma(reason="4096-element selection-mask relayout (query-major -> page-parity-major)"):
                    s2v = sel2[:, :].rearrange("t (p g q) -> t p g q", p=128, g=4)
                    for q_ in range(4):
                        for g_ in range(4):
                            fw.dma(POOL, dsm2, s2v[:, :, g_, q_], selscr[q_, g_, :].rearrange("(p t) -> t p", t=2), W=[rsel2], R=[rscr])
                for c4 in range(4):
                    fw.op(PE, lambda e: e.matmul(pbf(c4)[:, :], h2[:, :], sel2[:, c4 * 512:(c4 + 1) * 512], start=True, stop=True), R=[rsc, rsel2], W=[rpb[c4]])
                    fw.op(ACT, lambda e: e.activation(out=Msb[:, c4 * 32:(c4 + 1) * 32, :], in_=pbf(c4)[:, :].rearrange("p (a b) -> p a b", b=16), func=AF.Copy), R=[rpb[c4]], W=[rM])

                def attend_tile(KT_ap, V_ap, nk, mask_ap, rdeps, bsel, it):
                    u = it % 2
                    sbk = u
                    for g in range(4):
                        fw.op(PE, lambda e: e.matmul(pbf(sbk)[0:nk, g * 16:(g + 1) * 16].rearrange("p (h q) -> p h q", h=4), KT_ap(g), qS[:, 4 * g:4 * g + 4, 0:4], start=True, stop=True),
                              R=list(rdeps) + [rqS], W=[rpb[sbk]])
                    fw.op(ACT, lambda e: e.activation(out=PT[0:nk, u, :], in_=pbf(sbk)[0:nk, 0:64], func=AF.Exp), R=[rpb[sbk]], W=[rPT[u]])
                    if mask_ap is not None:
                        fw.op(DVE, lambda e: e.tensor_tensor(out=PT[0:nk, u, :].rearrange("p (g h q) -> p g h q", g=4, h=4), in0=PT[0:nk, u, :].rearrange("p (g h q) -> p g h q", g=4, h=4),
                                                             in1=mask_ap, op=ALU.mult), R=[rPT[u], rM, rsc], W=[rPT[u]])
                    for g in range(4):
                        ob = 2 + g // 2
                        fw.op(PE, lambda e: e.matmul(pbf(ob)[0:16, (g % 2) * 130:(g % 2) * 130 + 130], PT[0:nk, u, g * 16:(g + 1) * 16], V_ap(g), start=True, stop=True),
                              R=list(rdeps) + [rPT[u]], W=[rpb[ob]])
                    for gp in range(2):
                        fw.op(DVE, lambda e: e.tensor_tensor(out=accS[:, bsel, 2 * gp:2 * gp + 2, :], in0=accS[:, bsel, 2 * gp:2 * gp + 2, :],
                                                             in1=pbf(2 + gp)[0:16, 0:260].rearrange("p (g d) -> p g d", g=2), op=ALU.add), R=[rpb[2 + gp], racc], W=[racc])

                def cached_tile(u, rsrcs, mask_ap, bsel, it):
                    for g in range(4):
                        fw.op(PE, lambda e: e.transpose(out=pbb(6)[:, g * 128:(g + 1) * 128], in_=cpage[:, u, g * 128:(g + 1) * 128], identity=identb[:, :]),
                              R=[rcp[u], rid], W=[rpb[6]])
                    fw.op(ACT, lambda e: e.activation(out=KTp[:, u, :], in_=pbb(6)[:, 0:512], func=AF.Copy), R=[rpb[6]], W=[rKTp[u]])
                    fw.op(DVE, lambda e: e.tensor_copy(out=Va[:, 0, :, 0:128], in_=cpage[:, u, 512:1024].rearrange("p (g d) -> p g d", g=4)), R=[rcp[u]], W=[rVa[u]])
                    attend_tile(lambda g: KTp[:, u, g * 128:(g + 1) * 128], lambda g: Va[:, 0, g, :], 128, mask_ap, [rKTp[u], rVa[u]], bsel, it)

                it = 0
                for p in range(128):
                    u = p % 2
                    page_dma(cache_slc, p, u)
                    m_ap = Msb[:, p, :].rearrange("p (g q) -> p g q", g=4).unsqueeze(2).to_broadcast([128, 4, 4, 4])
                    cached_tile(u, None, m_ap, 0, it); it += 1
                cn_ap = cn[0:4, :].rearrange("p (h q) -> p h q", h=4).unsqueeze(1).to_broadcast([4, 4, 4, 4])
                attend_tile(lambda g: knT[:, 0, g, :], lambda g: vn[0:4, 0, g, :], 4, cn_ap, [rknT, rvn], 0, it); it += 1
                for t4 in range(4):
                    u = t4 % 2
                    fw.dma(POOL, dcp[u], cpage[:, u, :], winbuf[t4 * 128:(t4 + 1) * 128, :], W=[rcp[u]])
                    m_ap = wf[:, :].rearrange("p (h q) -> p h q", h=4).unsqueeze(1).to_broadcast([128, 4, 4, 4]) if t4 == 0 else None
                    cached_tile(u, None, m_ap, 1, it); it += 1
                attend_tile(lambda g: knT[:, 1, g, :], lambda g: vn[0:4, 1, g, :], 4, cn_ap, [rknT, rvn], 1, it); it += 1

                for bsel in range(2):
                    for h in range(4):
                        for gp in range(2):
                            b = 4 + gp
                            fw.op(PE, lambda e: e.matmul(pbf(b)[0:4, 0:260], identf[0:16, h * 4:(h + 1) * 4], accS[:, bsel, 2 * gp:2 * gp + 2, :].rearrange("p g d -> p (g d)"),
                                                         start=True, stop=True), R=[racc, rid], W=[rpb[b]])
                            for gg in range(2):
                                hd = 4 * (2 * gp + gg) + h; c0 = gg * 130
                                fw.op(DVE, lambda e: e.reciprocal(out=coef4[:, 0:1], in_=pbf(b)[0:4, c0 + 128:c0 + 129]), R=[rpb[b]], W=[rcoef4])
                                fw.op(DVE, lambda e: e.tensor_tensor(out=coef4[:, 1:2], in0=coef4[:, 0:1], in1=gs4[0:4, hd * 3 + 1 + bsel:hd * 3 + 2 + bsel], op=ALU.mult),
                                      R=[rcoef4, rgs4], W=[rcoef4])
                                fw.op(DVE, lambda e: e.scalar_tensor_tensor(out=o4[0:4, hd * 128:(hd + 1) * 128], in0=pbf(b)[0:4, c0:c0 + 128], scalar=coef4[:, 1:2],
                                                                            in1=o4[0:4, hd * 128:(hd + 1) * 128], op0=ALU.mult, op1=ALU.add), R=[rpb[b], ro4], W=[ro4], RS=[rcoef4])
                ob4 = cpage[:, :, :].rearrange("p a b -> p (a b)")
                fw.op(ACT, lambda e: e.activation(out=ob4[0:4, :], in_=o4[0:4, :], func=AF.Copy), R=[ro4], W=rcp)
                for c in range(16):
                    fw.op(PE, lambda e: e.transpose(out=pbb(6)[:, c * 4:c * 4 + 4], in_=ob4[0:4, c * 128:(c + 1) * 128], identity=identb[0:4, 0:4]), R=rcp + [rid], W=[rpb[6]])
                fw.op(DVE, lambda e: e.tensor_copy(out=oT[:, :, 0:4], in_=pbb(6)[:, 0:64].rearrange("p (a t) -> p a t", a=16)), R=[rpb[6]], W=[roT])
                return [dsm, dsm2, dcp[0], dcp[1]]

            if do_sample:
                load_x_tile((xs[0:4, :] if os.environ.get('SKIP_L0') else x1s[T:T + 4, :]), 4)
                modulate_from_x(1, 0, 1, 1, 4)
                extra = {}
                def phase_s():
                    extra["ds"] = nsa_sample_tile()
                with contextlib.ExitStack() as pes:
                    phase["es"] = pes
                    phase_s()
                    barrier([dkv] + extra["ds"])
                phase["es"] = None

                def rest_s():
                    dense_fm(nout_groups, oT, roT, 4, resid_evac(1, 2, 1, 4))
                    layer_norm_T(1, 0, 1, 4, mod=(1, 3, 4, 1))
                    ffn(1, 1, 4)
                    layer_norm_T(1, 2, 3, 4, mod=None)
                    store_x_tile(y_s[0:4, :], 4)
                run_phase(rest_s)
            barrier([dio, dkv])
            for ds in (dkv,):
                if ds.n:
                    SP.obj.wait_ge(ds.sem, ds.n)
        dump("modT", modT[:, :, :, :], [rmod]); dump("vT_ln", vT_ln[:, :], [rvt]); dump("vT_g", vT_g[:, :], [rvt])

        for ds in (dio, dst, dbgd["sem"]):
            if ds is not None and ds.n:
                SP.obj.wait_ge(ds.sem, ds.n)
        print(f"[build] ops={fw.nops} waits={fw.nwaits} sbuf_left={nc.sbuf_bytes_remaining}")
    return nc


def make_in_map(inputs, core, consts):
    b = core % 4; s = core
    f = lambda a: np.ascontiguousarray(a, dtype=np.float32)
    m = {}
    m["xp"] = f(inputs["x_prompt"][b]); m["xs"] = f(inputs["x_sample"][s])
    m["cvec"] = f(np.concatenate([inputs["c_prompt"][b].reshape(16, 128), inputs["c_sample"][s].reshape(16, 128)], 0))
    rows = []
    for l in range(2):
        for nm in ("ln_mix_g", "ln_mix_b", "ln_ffn_g", "ln_ffn_b"):
            rows.append(inputs[nm][l].reshape(16, 128))
    m["vtab_ln"] = f(np.concatenate(rows, 0))
    m["vtab_ada"] = f(inputs["ada_b"].reshape(2, 96, 128))
    m["vtab_g"] = f(inputs["gla_norm_g"][0].reshape(16, 128))
    m["ada_w"] = f(inputs["ada_w"])
    m["gla_w_in"] = f(inputs["gla_w_in"][0]); m["gla_w_out"] = f(inputs["gla_w_out"][0])
    wa = np.zeros((33, 1024), np.float32); wa[0:16] = inputs["gla_w_alpha"][0]; wa[32] = inputs["gla_b_alpha"][0]
    m["gla_wa"] = wa
    m["ffn_w_in"] = f(inputs["ffn_w_in"]); m["ffn_w_out"] = f(inputs["ffn_w_out"])
    m["state_in"] = f(inputs["state_gla"][s, 0])
    m["nsa_w_in"] = f(inputs["nsa_w_in"][0]); m["nsa_w_out"] = f(inputs["nsa_w_out"][0])
    m["nsa_bg"] = f(inputs["nsa_b_gate"][0].reshape(1, 48))
    m["cache_cmp"] = inputs["cache_cmp_kv"].reshape(1280 * 128, 1024)
    m["cache_slc"] = inputs["cache_slc_kv"].reshape(1280 * 128, 1024)
    m["winbuf"] = f(inputs["cache_win_kv"][s, 0].reshape(512, 1024))
    m["ptab"] = np.ascontiguousarray(inputs["page_table"][s].reshape(1, 128), dtype=np.int32)
    m["iota_in"] = np.arange(128, dtype=np.int32).reshape(128, 1)
    for k, v in consts.items():
        m["c_" + k] = v
    return m


_PROGRAM = {}


def kernel(**inputs):
    inputs = {k: np.asarray(v) for k, v in inputs.items()}
    if "nc" not in _PROGRAM:
        _PROGRAM["nc"] = build_program(n_ptiles=4, do_sample=True, stage="full", dbg=False)
    nc = _PROGRAM["nc"]
    consts = _const_tables()
    n = 8
    in_maps = [make_in_map(inputs, c, consts) for c in range(n)]
    res = run_bass_kernel_spmd(nc, in_maps, core_ids=list(range(n)))
    r = res.results
    f32 = np.float32
    y_prompt = np.stack([np.asarray(r[b]["y_p"], f32) for b in range(4)], 0)
    y_sample = np.stack([np.asarray(r[s]["y_s"], f32) for s in range(8)], 0)
    gla_p = np.stack([np.asarray(r[b]["st_p"], f32) for b in range(4)], 0)[:, None]
    gla_s = np.stack([np.asarray(r[s]["st_s"], f32) for s in range(8)], 0)[:, None]
    kv = lambda name, idx, rows: np.stack([np.asarray(r[i][name], f32).reshape(rows, 2, 4, 128) for i in idx], 0)[:, None]
    cmp_p = kv("cmp_p", range(4), 2048); cmp_s = kv("cmp_s", range(8), 4)
    slc_p = kv("slc_p", range(4), 2048); slc_s = kv("slc_s", range(8), 4)
    win_p = kv("win_p", range(4), 512); win_s = kv("win_s", range(8), 512)
    return (y_prompt, y_sample, gla_p, gla_s, cmp_p, cmp_s, slc_p, slc_s, win_p, win_s)
```
